# Optimizing a Trainium2 kernel written in Bass

```python
import jax, jax.numpy as jnp
from jax import lax
import numpy as np

D_MODEL = 2048
BATCH = 16
SEQ = 256
DEPTH = 1
DEC_BATCH = 8
DEC_SEQ = 2048
PAST_LEN = 512

GRID_W = 64
N_HEADS = 4
D_MLSTM = D_MODEL // 2
HEAD_DIM = D_MLSTM // N_HEADS
N_FGROUPS = 4
D_FOURIER = D_MODEL // 2
FGROUP_DIM = D_FOURIER // N_FGROUPS
D_FF = 5632
CONV_W = 3
CHUNK = 64
N_DIR = 2
N_GATE = 2 * N_DIR * N_HEADS
D_IN = 4 * D_MLSTM + N_GATE + D_FOURIER + 2 * D_MODEL
ALPHA = (2.0 * DEPTH) ** 0.25
BETA = (8.0 * DEPTH) ** -0.25
LN_EPS = 1e-5

kernel_name = "hybrid_mlstm_fnet_convffn_diffusion_step"


def _layernorm(x, g=None, b=None):
    xf = x.astype(jnp.float32)
    mu = jnp.mean(xf, -1, keepdims=True)
    var = jnp.mean(jnp.square(xf - mu), -1, keepdims=True)
    y = (xf - mu) * lax.rsqrt(var + LN_EPS)
    if g is not None:
        y = y * g.astype(jnp.float32) + b.astype(jnp.float32)
    return y.astype(x.dtype)


def _grid_posemb(n_tokens):
    rows = n_tokens // GRID_W
    t = jnp.arange(rows * GRID_W)
    r = (t // GRID_W).astype(jnp.float32)[:, None]
    col = (t % GRID_W).astype(jnp.float32)[:, None]
    quarter = D_MODEL // 4
    freq = 1.0 / (10000.0 ** (jnp.arange(quarter, dtype=jnp.float32) / quarter))
    er, ec = r * freq, col * freq
    return jnp.concatenate([jnp.sin(er), jnp.cos(er), jnp.sin(ec), jnp.cos(ec)], -1)


def _mlstm_chunkwise(q, k, v, ig, lf, C0, n0, m0):
    f32 = jnp.float32
    B, H, T, dh = q.shape
    nc = T // CHUNK

    def to_chunks(a):
        a = a.astype(f32).reshape(a.shape[:2] + (nc, CHUNK) + a.shape[3:])
        return jnp.moveaxis(a, 2, 0)

    causal = jnp.tril(jnp.ones((CHUNK, CHUNK), dtype=bool))

    def step(carry, xs):
        C, n, m = carry
        qc, kc, vc, ic, fc = xs
        b = jnp.cumsum(fc, axis=-1)
        dmat = jnp.where(causal, b[..., :, None] - b[..., None, :] + ic[..., None, :], -jnp.inf)
        inter = b + m[..., None]
        m_t = jnp.maximum(inter, jnp.max(dmat, -1))
        w = jnp.exp(dmat - m_t[..., None])
        wi = jnp.exp(inter - m_t)
        s = jnp.einsum("bhtd,bhsd->bhts", qc, kc) * w
        num = jnp.einsum("bhts,bhsv->bhtv", s, vc) + wi[..., None] * jnp.einsum("bhtd,bhdv->bhtv", qc, C)
        den = jnp.sum(s, -1) + wi * jnp.einsum("bhtd,bhd->bht", qc, n)
        h = num / jnp.maximum(jnp.abs(den), jnp.exp(-m_t))[..., None]
        b_last = b[..., -1]
        g = b_last[..., None] - b + ic
        m_new = jnp.maximum(b_last + m, jnp.max(g, -1))
        wk = jnp.exp(g - m_new[..., None])
        decay = jnp.exp(b_last + m - m_new)
        C_new = decay[..., None, None] * C + jnp.einsum("bhs,bhsd,bhsv->bhdv", wk, kc, vc)
        n_new = decay[..., None] * n + jnp.einsum("bhs,bhsd->bhd", wk, kc)
        return (C_new, n_new, m_new), h

    xs = (to_chunks(q), to_chunks(k), to_chunks(v), to_chunks(ig), to_chunks(lf))
    (C, n, m), h = lax.scan(step, (C0.astype(f32), n0.astype(f32), m0.astype(f32)), xs)
    h = jnp.moveaxis(h, 0, 2).reshape(B, H, T, dh)
    return h, C, n, m


def _mlstm_bidir(q, k, v, gates, C0, n0, m0):
    B, T = gates.shape[:2]
    g = jnp.transpose(gates.reshape(B, T, N_DIR, 2, N_HEADS), (2, 3, 0, 4, 1))
    ig = g[:, 0]
    lf = jax.nn.log_sigmoid(g[:, 1])
    hf, Cf, nf, mf = _mlstm_chunkwise(q, k, v, ig[0], lf[0], C0[:, 0], n0[:, 0], m0[:, 0])
    hb, Cb, nb, mb = _mlstm_chunkwise(jnp.flip(q, 2), jnp.flip(k, 2), jnp.flip(v, 2),
                                      jnp.flip(ig[1], 2), jnp.flip(lf[1], 2),
                                      C0[:, 1], n0[:, 1], m0[:, 1])
    h = hf + jnp.flip(hb, 2)
    return h, jnp.stack([Cf, Cb], 1), jnp.stack([nf, nb], 1), jnp.stack([mf, mb], 1)


def _layer(x, cond, C0, n0, m0, w_ada, b_ada, w_in, b_gate, w_hnorm, w_br_m, w_br_f, w_out,
           ln1_g, ln1_b, w_up, w_conv, b_conv, w_down, ln2_g, ln2_b):
    B, T, _ = x.shape
    mod = jax.nn.silu(cond) @ w_ada + b_ada
    sh1, sc1, g1, sh2, sc2, g2 = [a[:, None, :] for a in jnp.split(mod, 6, axis=-1)]

    h = _layernorm(x) * (1.0 + sc1) + sh1
    p = h @ w_in
    splits = (D_MLSTM, 2 * D_MLSTM, 3 * D_MLSTM, 4 * D_MLSTM, 4 * D_MLSTM + N_GATE,
              4 * D_MLSTM + N_GATE + D_FOURIER, 4 * D_MLSTM + N_GATE + D_FOURIER + D_MODEL)
    q, k, v, o, gates, fr, ga, gb = jnp.split(p, splits, axis=-1)

    def heads(a):
        return a.reshape(B, T, N_HEADS, HEAD_DIM).transpose(0, 2, 1, 3)

    hm, C, n, m = _mlstm_bidir(heads(q) * HEAD_DIM ** -0.5, heads(k), heads(v),
                               (gates.astype(jnp.float32) + b_gate.astype(jnp.float32)), C0, n0, m0)
    hm = _layernorm(hm).transpose(0, 2, 1, 3) * w_hnorm.reshape(N_HEADS, HEAD_DIM)
    hm = jax.nn.sigmoid(o) * hm.reshape(B, T, D_MLSTM).astype(x.dtype)

    fr = fr.reshape(B, T, N_FGROUPS, FGROUP_DIM).astype(jnp.float32)
    fr = jnp.fft.fft2(fr, axes=(1, 3), norm="ortho").real.astype(x.dtype).reshape(B, T, D_FOURIER)

    mixed = jax.nn.sigmoid(ga) * (hm @ w_br_m) + jax.nn.sigmoid(gb) * (fr @ w_br_f)
    x = _layernorm(ALPHA * x + g1 * (mixed @ w_out), ln1_g, ln1_b)

    h = _layernorm(x) * (1.0 + sc2) + sh2
    u = h @ w_up
    up = jnp.pad(u, ((0, 0), (1, 1), (0, 0)))
    u = up[:, :-2] * w_conv[0] + up[:, 1:-1] * w_conv[1] + up[:, 2:] * w_conv[2] + b_conv
    val, gate = jnp.split(u, 2, axis=-1)
    y = (jax.nn.silu(gate) * val) @ w_down
    x = _layernorm(ALPHA * x + g2 * y, ln2_g, ln2_b)
    return x, C, n, m


def setup_inputs(seed: int = 0) -> dict:
    key = jax.random.key(seed)
    ks = jax.random.split(key, 24)
    f32 = jnp.float32
    nrm = lambda k, shape, s: jax.random.normal(k, shape, f32) * s
    fbias = jnp.linspace(3.0, 6.0, N_HEADS).astype(f32)
    gate_offset = jnp.stack([jnp.zeros((N_HEADS,), f32), fbias])[None, None]
    b_gate = (nrm(ks[10], (DEPTH, N_DIR, 2, N_HEADS), 0.1) + gate_offset).reshape(DEPTH, N_GATE)
    return {
        "x_prompt": nrm(ks[0], (BATCH, SEQ, D_MODEL), 1.0),
        "x_sample": nrm(ks[1], (DEC_BATCH, DEC_SEQ, D_MODEL), 1.0),
        "c": nrm(ks[2], (DEC_BATCH, D_MODEL), 1.0),
        "state_C": nrm(ks[3], (DEC_BATCH, DEPTH, N_DIR, N_HEADS, HEAD_DIM, HEAD_DIM), HEAD_DIM ** -0.5),
        "state_n": nrm(ks[4], (DEC_BATCH, DEPTH, N_DIR, N_HEADS, HEAD_DIM), HEAD_DIM ** -0.5),
        "state_m": nrm(ks[5], (DEC_BATCH, DEPTH, N_DIR, N_HEADS), 1.0),
        "c_ctx": nrm(ks[6], (D_MODEL,), 1.0),
        "w_ada": nrm(ks[7], (DEPTH, D_MODEL, 6 * D_MODEL), 0.5 * D_MODEL ** -0.5),
        "b_ada": nrm(ks[8], (DEPTH, 6 * D_MODEL), 0.01),
        "w_in": nrm(ks[9], (DEPTH, D_MODEL, D_IN), D_MODEL ** -0.5),
        "b_gate": b_gate,
        "w_hnorm": 1.0 + nrm(ks[11], (DEPTH, D_MLSTM), 0.02),
        "w_br_m": nrm(ks[12], (DEPTH, D_MLSTM, D_MODEL), D_MLSTM ** -0.5),
        "w_br_f": nrm(ks[13], (DEPTH, D_FOURIER, D_MODEL), D_FOURIER ** -0.5),
        "w_out": nrm(ks[14], (DEPTH, D_MODEL, D_MODEL), BETA * D_MODEL ** -0.5),
        "ln1_g": 1.0 + nrm(ks[15], (DEPTH, D_MODEL), 0.02),
        "ln1_b": nrm(ks[16], (DEPTH, D_MODEL), 0.02),
        "w_up": nrm(ks[17], (DEPTH, D_MODEL, 2 * D_FF), D_MODEL ** -0.5),
        "w_conv": nrm(ks[18], (DEPTH, CONV_W, 2 * D_FF), CONV_W ** -0.5),
        "b_conv": nrm(ks[19], (DEPTH, 2 * D_FF), 0.01),
        "w_down": nrm(ks[20], (DEPTH, D_FF, D_MODEL), BETA * D_FF ** -0.5),
        "ln2_g": 1.0 + nrm(ks[21], (DEPTH, D_MODEL), 0.02),
        "ln2_b": nrm(ks[22], (DEPTH, D_MODEL), 0.02),
    }


def reference(x_prompt, x_sample, c, state_C, state_n, state_m, c_ctx, w_ada, b_ada, w_in, b_gate,
              w_hnorm, w_br_m, w_br_f, w_out, ln1_g, ln1_b, w_up, w_conv, b_conv, w_down, ln2_g, ln2_b):
    f32 = jnp.float32
    B = x_prompt.shape[0]
    xp = x_prompt
    xs = x_sample + _grid_posemb(x_sample.shape[1]).astype(x_sample.dtype)[None]
    C0 = jnp.zeros((B, N_DIR, N_HEADS, HEAD_DIM, HEAD_DIM), f32)
    n0 = jnp.zeros((B, N_DIR, N_HEADS, HEAD_DIM), f32)
    m0 = jnp.zeros((B, N_DIR, N_HEADS), f32)
    Cs, ns, ms = [], [], []
    for l in range(DEPTH):
        params = (w_ada[l], b_ada[l], w_in[l], b_gate[l], w_hnorm[l], w_br_m[l], w_br_f[l], w_out[l],
                  ln1_g[l], ln1_b[l], w_up[l], w_conv[l], b_conv[l], w_down[l], ln2_g[l], ln2_b[l])
        xp, Cl, nl, ml = _layer(xp, c_ctx[None, :], C0, n0, m0, *params)
        Cs.append(Cl)
        ns.append(nl)
        ms.append(ml)
        xs, _, _, _ = _layer(xs, c, state_C[:, l], state_n[:, l], state_m[:, l], *params)
    new_C = jnp.stack(Cs, 1)
    new_n = jnp.stack(ns, 1)
    new_m = jnp.stack(ms, 1)
    return (xp, xs, new_C, new_n, new_m)
```

```python
import numpy as np
import ml_dtypes
import concourse.bass as bass
import concourse.mybir as mybir
from concourse.bass_utils import run_bass_kernel_spmd

F32 = mybir.dt.float32
BF16 = mybir.dt.bfloat16
AF = mybir.ActivationFunctionType
ALU = mybir.AluOpType
AX = mybir.AxisListType

PE, ACT, DVE, POOL, SP = "pe", "act", "dve", "pool", "sp"
N_DMA_SEMS = 8

D = 2048
KC = 16
NT = 20
TOK = 2560
DH = 256
NH = 4
DFF = 5632
NJ = 44
D_IN = 9232
ALPHA = 2.0 ** 0.25
EPS = 1e-5
SEQS = [(0, 2048, 0), (2048, 256, 1), (2304, 256, 1)]


class Tk:
    __slots__ = ("w", "r", "ws")

    def __init__(self):
        self.w = None
        self.r = []
        self.ws = []


class Op:
    __slots__ = ("eng", "fn", "deps", "is_dma", "needs_inc", "idx", "dslot", "dgen")

    def __init__(self, eng, fn, is_dma):
        self.eng = eng
        self.fn = fn
        self.deps = []
        self.is_dma = is_dma
        self.needs_inc = False
        self.idx = None
        self.dslot = None
        self.dgen = None


class Sched:
    def __init__(self, nc):
        self.nc = nc
        self.ops = {PE: [], ACT: [], DVE: [], POOL: [], SP: []}
        self.ndma = {ACT: 0, POOL: 0, SP: 0}
        self.dma_hist = {ACT: [], POOL: [], SP: []}
        self.last = {PE: None, ACT: None, DVE: None, POOL: None}
        self.bar = {}

    def op(self, eng, fn, reads=(), writes=(), dma=False, mwrites=()):
        o = Op(eng, fn, dma)
        deps = []
        for t in reads:
            if t.w is not None:
                deps.append(t.w)
            deps.extend(t.ws)
        for t in writes:
            if t.w is not None:
                deps.append(t.w)
            deps.extend(t.r)
            deps.extend(t.ws)
        for t in mwrites:
            if t.w is not None:
                deps.append(t.w)
            deps.extend(t.r)
        if eng in self.bar:
            deps.extend(self.bar.pop(eng))
        if dma:
            k = self.ndma[eng]
            o.dslot = k % N_DMA_SEMS
            o.dgen = k // N_DMA_SEMS
            self.ndma[eng] = k + 1
            hist = self.dma_hist[eng]
            if k >= N_DMA_SEMS:
                deps.append(hist[k - N_DMA_SEMS])
            hist.append(o)
        else:
            self.last[eng] = o
        seen = set()
        for d in deps:
            if d is o or id(d) in seen:
                continue
            seen.add(id(d))
            if (not d.is_dma) and (not dma) and d.eng == PE and eng == PE:
                continue
            o.deps.append(d)
            if not d.is_dma:
                d.needs_inc = True
        for t in reads:
            t.r.append(o)
        for t in writes:
            t.w = o
            t.r = []
            t.ws = []
        for t in mwrites:
            t.ws.append(o)
        self.ops[eng].append(o)
        return o

    def barrier(self):
        pend = [o for o in self.last.values() if o is not None]
        for e in self.dma_hist:
            pend.extend(self.dma_hist[e][-N_DMA_SEMS:])
        for e in self.ops:
            self.bar[e] = list(pend)

    def emit(self):
        nc = self.nc
        from contextlib import ExitStack
        with ExitStack() as es:
            esem = {e: es.enter_context(nc.semaphore("s_" + e)) for e in (PE, ACT, DVE, POOL)}
            dsem = {e: [es.enter_context(nc.semaphore("d_%s%d" % (e, i))) for i in range(N_DMA_SEMS)]
                    for e in (ACT, POOL, SP)}
            for e in (PE, ACT, DVE, POOL):
                c = 0
                for o in self.ops[e]:
                    if (not o.is_dma) and o.needs_inc:
                        c += 1
                        o.idx = c
            block = es.enter_context(nc.Block())

            def run(e, engh):
                waited = {}
                for o in self.ops[e]:
                    for d in o.deps:
                        if d.is_dma:
                            sem = dsem[d.eng][d.dslot]
                            val = 16 * (d.dgen + 1)
                        else:
                            sem = esem[d.eng]
                            val = d.idx
                        key = id(sem)
                        if waited.get(key, 0) >= val:
                            continue
                        waited[key] = val
                        engh.wait_ge(sem, val)
                    ins = o.fn(engh)
                    if o.is_dma:
                        ins.then_inc(dsem[e][o.dslot], 16)
                    elif o.needs_inc:
                        ins.then_inc(esem[e], 1)
                if e in self.dma_hist:
                    for o in self.dma_hist[e][-N_DMA_SEMS:]:
                        engh.wait_ge(dsem[e][o.dslot], 16 * (o.dgen + 1))

            @block.tensor
            def _(eng):
                run(PE, eng)

            @block.scalar
            def _(eng):
                run(ACT, eng)

            @block.vector
            def _(eng):
                run(DVE, eng)

            @block.gpsimd
            def _(eng):
                run(POOL, eng)

            @block.sync
            def _(eng):
                run(SP, eng)


class Arena:
    def __init__(self, nc, base, limit):
        self.nc = nc
        self.base = base
        self.limit = limit
        self.cur = base
        self.n = 0

    def mark(self):
        return self.cur

    def reset(self, m):
        self.cur = m

    def alloc(self, shape, dtype, name=None):
        esz = 4 if dtype == F32 else 2
        free = 1
        for s in shape[1:]:
            free *= s
        nbytes = (free * esz + 63) // 64 * 64
        off = self.cur
        assert off + nbytes <= self.limit, ("SBUF arena overflow", name, off, nbytes, self.limit)
        self.cur = off + nbytes
        self.n += 1
        return self.nc.alloc_sbuf_tensor_at("%s_%d" % (name or "t", self.n), list(shape), dtype, offset=off)


def build_program(upto=99, debug=()):
    nc = bass.Bass("TRN2", target_bir_lowering=False)
    S = Sched(nc)

    def din(name, shape, dt=F32):
        return nc.dram_tensor(name, list(shape), dt, kind="ExternalInput").ap()

    def dout(name, shape, dt=F32):
        return nc.dram_tensor(name, list(shape), dt, kind="ExternalOutput").ap()

    def dscr(name, shape, dt=F32):
        return nc.dram_tensor(name, list(shape), dt).ap()

    xs = din("xs", [2048, D])
    xp = din("xp", [512, D])
    cond = din("cond", [2, D])
    sC = din("sC", [2, NH, DH, DH])
    sn = din("sn", [2, NH, DH])
    sm = din("sm", [8])
    w_ada = din("w_ada", [D, 6 * D])
    b_ada = din("b_ada", [6 * D])
    w_in = din("w_in", [D, D_IN])
    b_gate = din("b_gate", [16])
    w_hnorm = din("w_hnorm", [1024])
    w_br_m = din("w_br_m", [1024, D])
    w_br_f = din("w_br_f", [1024, D])
    w_out = din("w_out", [D, D])
    ln1_g = din("ln1_g", [D])
    ln1_b = din("ln1_b", [D])
    w_up = din("w_up", [D, 2 * DFF])
    w_conv = din("w_conv", [3, 2 * DFF])
    b_conv = din("b_conv", [2 * DFF])
    w_down = din("w_down", [DFF, D])
    ln2_g = din("ln2_g", [D])
    ln2_b = din("ln2_b", [D])
    posemb = din("posemb", [2048, D])
    maskF_d = din("maskF", [128, 128])
    maskB_d = din("maskB", [128, 128])
    dftc_d = din("dftc", [256, 512], BF16)
    ct2048 = din("ct2048", [2048, 2048], BF16)
    st2048 = din("st2048", [2048, 2048], BF16)
    ct256 = din("ct256", [256, 256], BF16)
    st256 = din("st256", [256, 256], BF16)

    y_s = dout("y_s", [2048, D])
    y_p = dout("y_p", [512, D])
    o_C = dout("o_C", [2, 2, NH, DH, DH])
    o_n = dout("o_n", [2, 2, NH, DH])
    o_m = dout("o_m", [2, 8])
    dbg = {}

    def x_rows(i):
        return xs[i * 128:(i + 1) * 128, :] if i < 16 else xp[(i - 16) * 128:(i - 15) * 128, :]

    def y_rows(i):
        return y_s[i * 128:(i + 1) * 128, :] if i < 16 else y_p[(i - 16) * 128:(i - 15) * 128, :]

    A = Arena(nc, 16512, 229376)
    ident = A.alloc([128, 128], BF16, "ident")
    identf = A.alloc([128, 128], F32, "identf")
    modT = A.alloc([128, 96, 2], F32, "modT")
    t_modT = Tk()
    t_ident = Tk()
    condB = A.alloc([128, 2, 16], BF16, "condB")
    badaT = A.alloc([128, 96], F32, "badaT")
    gstage = A.alloc([1, 512], F32, "gstage")
    gbrow = A.alloc([1, 256], F32, "gbrow")
    pers_mark = A.mark()

    PS = [nc.alloc_psum_tensor("ps%d" % i, [128, 512], F32) for i in range(8)]
    tPS = [Tk() for _ in range(8)]
    PSB = [PS[6][:].bitcast(BF16), PS[7][:].bitcast(BF16)]
    tPSB = [tPS[6], tPS[7]]

    S.op(POOL, lambda e: e.memset(identf[:], 1.0), writes=[t_ident])
    S.op(POOL, lambda e: e.affine_select(out=identf[:], in_=identf[:], pattern=[[-1, 128]],
                                         compare_op=ALU.is_equal, fill=0.0, base=0, channel_multiplier=1),
         reads=[t_ident], writes=[t_ident])
    S.op(DVE, lambda e: e.tensor_copy(out=ident[:], in_=identf[:]), reads=[t_ident], writes=[t_ident])

    NWB = 3
    wbufs = [A.alloc([128, KC, 256], BF16, "wbuf") for _ in range(NWB)]
    t_wb = [Tk() for _ in range(NWB)]
    wctr = [0]

    def load_w(src2d, ncols, kc=KC, rows_pk=False):
        b = wctr[0] % len(wbufs)
        wctr[0] += 1
        if rows_pk:
            v = src2d.rearrange("(p k) n -> p k n", k=kc)
        else:
            v = src2d.rearrange("(k p) n -> p k n", p=128)
        buf = wbufs[b]
        S.op(POOL, lambda e: e.dma_start(out=buf[:, 0:kc, 0:ncols], in_=v), writes=[t_wb[b]], dma=True)
        return buf, t_wb[b]

    work_mark = A.mark()

    pbank = [0]

    def next_bank(lo=0, hi=8):
        b = lo + pbank[0] % (hi - lo)
        pbank[0] += 1
        return b

    hT = A.alloc([128, KC, TOK], BF16, "hT")
    t_hT = [Tk() for _ in range(NT)]
    ln_mark = A.mark()
    condS = A.alloc([128, 2, 16], F32, "condS")
    badaR = A.alloc([96, 128], F32, "badaR")
    t_cond, t_condB, t_badaR, t_badaT, t_gst = [Tk() for _ in range(5)]
    g_scr = dscr("g_scr", [2, 2, D])
    t_gscr = Tk()

    for c in range(2):
        S.op(SP, lambda e, c=c: e.dma_start(out=condS[:, c, :], in_=cond[c].rearrange("(p k) -> p k", k=16)),
             writes=[t_cond], dma=True)
    S.op(ACT, lambda e: e.activation(out=condB[:], in_=condS[:], func=AF.Silu), reads=[t_cond], writes=[t_condB])
    S.op(SP, lambda e: e.dma_start(out=badaR[:], in_=b_ada.rearrange("(j p) -> j p", p=128)),
         writes=[t_badaR], dma=True)
    S.op(PE, lambda e: e.transpose(PS[1][:, 0:96], badaR[:], identf[0:96, 0:96]),
         reads=[t_badaR, t_ident], writes=[tPS[1]])
    S.op(ACT, lambda e: e.copy(out=badaT[:], in_=PS[1][:, 0:96]), writes=[tPS[1], t_badaT])
    for sec in (1, 4):
        S.op(DVE, lambda e, sec=sec: e.tensor_scalar_add(out=badaT[:, sec * 16:(sec + 1) * 16],
                                                         in0=badaT[:, sec * 16:(sec + 1) * 16], scalar1=1.0),
             writes=[t_badaT])

    def ada_block(blk):
        sec = blk // 8
        wb, twb = load_w(w_ada[:, blk * 256:(blk + 1) * 256], 256, rows_pk=True)
        b = next_bank()
        if sec in (2, 5):
            gi = 0 if sec == 2 else 1
            c0 = (blk % 8) * 256
            S.op(SP, lambda e: e.dma_start(out=gbrow[0:1, :], in_=b_ada[sec * D + c0:sec * D + c0 + 256].partition_broadcast(1)),
                 writes=[t_gst], dma=True)
            for c in range(2):
                for k in range(16):
                    S.op(PE, lambda e, c=c, k=k: e.matmul(
                        PS[b][0:1, c * 256:(c + 1) * 256], lhsT=condB[:, c, k:k + 1], rhs=wb[:, k, 0:256],
                        start=(k == 0), stop=(k == 15)), reads=[t_condB, twb], writes=[tPS[b]])
            for c in range(2):
                S.op(DVE, lambda e, c=c: e.tensor_tensor(
                    out=gstage[0:1, c * 256:(c + 1) * 256], in0=PS[b][0:1, c * 256:(c + 1) * 256], in1=gbrow[0:1, :],
                    op=ALU.add), writes=[tPS[b], t_gst])
            for c in range(2):
                S.op(SP, lambda e, c=c: e.dma_start(
                    out=g_scr[gi, c:c + 1, c0:c0 + 256], in_=gstage[0:1, c * 256:(c + 1) * 256]),
                    reads=[t_gst], mwrites=[t_gscr], dma=True)
        else:
            jj0 = blk * 2
            for j in range(2):
                for k in range(16):
                    S.op(PE, lambda e, j=j, k=k: e.matmul(
                        PS[b][:, j * 2:j * 2 + 2], lhsT=wb[:, k, j * 128:(j + 1) * 128], rhs=condB[:, :, k],
                        start=(k == 0), stop=(k == 15)), reads=[t_condB, twb], writes=[tPS[b]])
            for c in range(2):
                S.op(DVE, lambda e, c=c: e.tensor_tensor(
                    out=modT[:, jj0:jj0 + 2, c], in0=PS[b][:, 0:4].rearrange("p (j c) -> p j c", c=2)[:, :, c],
                    in1=badaT[:, jj0:jj0 + 2], op=ALU.add), reads=[t_badaT], writes=[tPS[b], t_modT])

    for blk in range(16):
        ada_block(blk)
    ada_pending = list(range(16, 48))

    class LNB:
        def __init__(self, with_x=True, with_xn=True):
            if with_x:
                self.xt = [A.alloc([128, D], F32, "xt") for _ in range(2)]
                self.pt = [A.alloc([128, D], F32, "pt") for _ in range(2)]
                self.t_xt = [Tk(), Tk()]
                self.t_pt = [Tk(), Tk()]
            if with_xn:
                self.xn = [A.alloc([128, D], BF16, "xn") for _ in range(2)]
                self.t_xn = [Tk(), Tk()]
            self.stats = [A.alloc([128, 4, 6], F32, "stats") for _ in range(2)]
            self.mv = [A.alloc([128, 4], F32, "mv") for _ in range(2)]
            self.t_mv = [Tk(), Tk()]

    def ln_stats(L, src, b, t_src):
        st, mvb, tmv = L.stats[b], L.mv[b], L.t_mv[b]
        for c4 in range(4):
            S.op(DVE, lambda e, c4=c4: e.bn_stats(out=st[:, c4, :], in_=src[:, c4 * 512:(c4 + 1) * 512]),
                 reads=[t_src], writes=[tmv])
        S.op(DVE, lambda e: e.bn_aggr(out=mvb[:, 0:2], in_=st[:]), writes=[tmv])
        S.op(DVE, lambda e: e.tensor_scalar_add(out=mvb[:, 2:3], in0=mvb[:, 1:2], scalar1=EPS), writes=[tmv])
        S.op(ACT, lambda e: e.activation(out=mvb[:, 2:3], in_=mvb[:, 2:3], func=AF.Ln), writes=[tmv])
        S.op(ACT, lambda e: e.activation(out=mvb[:, 2:3], in_=mvb[:, 2:3], func=AF.Exp, scale=-0.5), writes=[tmv])

    def norm_mod_T(L, src, b, t_src, i, dstT, t_dst, sec_sh, sec_sc, c):
        norm_part1(L, src, b, t_src)
        norm_part2(L, b, i, dstT, t_dst, sec_sh, sec_sc, c)

    def norm_part1(L, src, b, t_src):
        ln_stats(L, src, b, t_src)
        xnb, txn, mvb, tmv = L.xn[b], L.t_xn[b], L.mv[b], L.t_mv[b]
        S.op(DVE, lambda e: e.tensor_scalar(out=xnb[:], in0=src[:], scalar1=mvb[:, 0:1], scalar2=mvb[:, 2:3],
                                            op0=ALU.subtract, op1=ALU.mult), reads=[t_src, tmv], writes=[txn])

    def norm_part2(L, b, i, dstT, t_dst, sec_sh, sec_sc, c, act_evac=False, only=None):
        xnb, txn = L.xn[b], L.t_xn[b]
        for hb in range(2):
            for q in range(8):
                k = hb * 8 + q
                if only == "evac":
                    break
                S.op(PE, lambda e, k=k, q=q, hb=hb: e.transpose(
                    PSB[hb][:, q * 128:(q + 1) * 128], xnb[:, k * 128:(k + 1) * 128], ident[:]),
                    reads=[txn, t_ident], writes=[tPSB[hb]])
            for q in range(8):
                if only == "pe":
                    break
                k = hb * 8 + q
                src_ps = PSB[hb][:, q * 128:(q + 1) * 128]
                dst = dstT[:, k, i * 128:(i + 1) * 128]
                sc_ap = modT[:, sec_sc * 16 + k, c:c + 1]
                sh_ap = modT[:, sec_sh * 16 + k, c:c + 1]
                if act_evac == 2 or (act_evac and q % 2 == 0):
                    S.op(ACT, lambda e, src_ps=src_ps, dst=dst, sc_ap=sc_ap, sh_ap=sh_ap: e.activation(
                        out=dst, in_=src_ps, func=AF.Identity, bias=sh_ap, scale=sc_ap),
                        reads=[t_modT], writes=[tPSB[hb], t_dst])
                else:
                    S.op(DVE, lambda e, src_ps=src_ps, dst=dst, sc_ap=sc_ap, sh_ap=sh_ap: e.tensor_scalar(
                        out=dst, in0=src_ps, scalar1=sc_ap, scalar2=sh_ap, op0=ALU.mult, op1=ALU.add),
                        reads=[t_modT], writes=[tPSB[hb], t_dst])

    def load_x(L, i, b):
        xtb, ptb, txt, tpt = L.xt[b], L.pt[b], L.t_xt[b], L.t_pt[b]
        S.op(SP, lambda e: e.dma_start(out=xtb[:], in_=x_rows(i)), writes=[txt], dma=True)
        if i < 16:
            S.op(SP, lambda e: e.dma_start(out=ptb[:], in_=posemb[i * 128:(i + 1) * 128, :]), writes=[tpt], dma=True)
            S.op(POOL, lambda e: e.tensor_tensor(out=xtb[:], in0=xtb[:], in1=ptb[:], op=ALU.add),
                 reads=[tpt], writes=[txt])

    L1 = LNB()

    load_x(L1, 0, 0)
    load_x(L1, 1, 1)
    norm_part1(L1, L1.xt[0], 0, L1.t_xt[0])
    for i in range(NT):
        if i + 1 < NT:
            norm_part1(L1, L1.xt[(i + 1) % 2], (i + 1) % 2, L1.t_xt[(i + 1) % 2])
        if i + 2 < NT:
            load_x(L1, i + 2, i % 2)
        norm_part2(L1, i % 2, i, hT, t_hT[i], 0, 1, 0 if i < 16 else 1, act_evac=True)
    if "modT" in debug:
        dbg["modT"] = dout("dbg_modT", [128, 192])
        S.op(SP, lambda e: e.dma_start(out=dbg["modT"], in_=modT[:].rearrange("p j c -> p (j c)")),
             reads=[t_modT], dma=True)
    if "hT" in debug:
        dbg["hT"] = dout("dbg_hT", [128, KC, TOK], BF16)
        for k in range(KC):
            S.op(SP, lambda e, k=k: e.dma_start(out=dbg["hT"][:, k, :], in_=hT[:, k, :]), reads=t_hT, dma=True)
    if upto <= 1:
        S.emit()
        return nc

    S.barrier()
    A.reset(ln_mark)
    maskF = A.alloc([128, 128], F32, "maskF")
    maskB = A.alloc([128, 128], F32, "maskB")
    U = A.alloc([128, 8, 20], F32, "U")
    WI = A.alloc([128, 8, 20], F32, "WI")
    CL = A.alloc([128, 8, 20], F32, "CL")
    gate_mark = A.mark()
    ones = A.alloc([128, 128], F32, "ones")
    bgate_bc = A.alloc([128, 16], F32, "bgate")
    GT = A.alloc([128, 16, 20], F32, "GT")
    LF = A.alloc([128, 8, 20], F32, "LF")
    Bc = A.alloc([128, 8, 20], F32, "Bc")
    BL = A.alloc([128, 8, 20], F32, "BL")
    A_ = A.alloc([128, 8, 20], F32, "A_")
    AMX = A.alloc([128, 8, 20], F32, "AMX")
    MX = A.alloc([128, 8, 20], F32, "MX")
    WIL = A.alloc([128, 8, 20], F32, "WIL")
    AM = A.alloc([80, 2], F32, "AM")
    D1 = A.alloc([80, 2, 80], F32, "D1")
    MS = A.alloc([128, 3, 8], F32, "MS")
    t_c, t_g, t_ms = Tk(), Tk(), Tk()
    t_rec = [Tk(), Tk()]
    S.op(SP, lambda e: e.dma_start(out=maskF[:], in_=maskF_d), writes=[t_c], dma=True)
    S.op(SP, lambda e: e.dma_start(out=maskB[:], in_=maskB_d), writes=[t_c], dma=True)
    S.op(SP, lambda e: e.dma_start(out=bgate_bc[:], in_=b_gate.partition_broadcast(128)), writes=[t_c], dma=True)
    S.op(POOL, lambda e: e.memset(ones[:], 1.0), writes=[t_c])
    S.op(SP, lambda e: e.dma_start(out=MS[:, 0, :], in_=sm.partition_broadcast(128)), writes=[t_ms], dma=True)
    S.op(POOL, lambda e: e.memset(MS[:, 1:3, :], 0.0), writes=[t_ms])

    wg, twg = load_w(w_in[:, 4096:4112], 16)
    for i in range(NT):
        for k in range(KC):
            S.op(PE, lambda e, i=i, k=k: e.matmul(PS[0][:, i * 16:(i + 1) * 16], lhsT=hT[:, k, i * 128:(i + 1) * 128],
                                                  rhs=wg[:, k, 0:16], start=(k == 0), stop=(k == KC - 1)),
                 reads=[twg], writes=[tPS[0]])
    S.op(DVE, lambda e: e.tensor_tensor(out=GT[:], in0=PS[0][:, 0:320].rearrange("p (i g) -> p g i", g=16),
                                        in1=bgate_bc[:, :].unsqueeze(2).to_broadcast([128, 16, 20]), op=ALU.add),
         reads=[t_c], writes=[tPS[0], t_g])
    GT4 = GT[:].rearrange("p (d k h) i -> p d k h i", d=2, k=2)
    for d in range(2):
        S.op(ACT, lambda e, d=d: e.activation(out=LF[:, d * 4:(d + 1) * 4, :], in_=GT4[:, d, 1], func=AF.Exp,
                                              scale=-1.0), reads=[t_g], writes=[t_g])
    S.op(ACT, lambda e: e.activation(out=LF[:], in_=LF[:], func=AF.Ln, bias=1.0), reads=[t_g], writes=[t_g])
    S.op(DVE, lambda e: e.tensor_scalar_mul(out=LF[:], in0=LF[:], scalar1=-1.0), reads=[t_g], writes=[t_g])
    LFf = LF[:].rearrange("p a b -> p (a b)")
    S.op(PE, lambda e: e.matmul(PS[1][:, 0:80], lhsT=maskF[:], rhs=LFf[:, 0:80], start=True, stop=True),
         reads=[t_c, t_g], writes=[tPS[1]])
    S.op(PE, lambda e: e.matmul(PS[1][:, 80:160], lhsT=maskB[:], rhs=LFf[:, 80:160], start=True, stop=True),
         reads=[t_c, t_g], writes=[tPS[1]])
    S.op(PE, lambda e: e.matmul(PS[1][:, 160:320], lhsT=ones[:], rhs=LFf, start=True, stop=True),
         reads=[t_c, t_g], writes=[tPS[1]])
    S.op(DVE, lambda e: e.tensor_copy(out=Bc[:].rearrange("p a b -> p (a b)"), in_=PS[1][:, 0:160]),
         writes=[tPS[1], t_g])
    S.op(DVE, lambda e: e.tensor_copy(out=BL[:].rearrange("p a b -> p (a b)"), in_=PS[1][:, 160:320]),
         writes=[tPS[1], t_g])
    for d in range(2):
        S.op(DVE, lambda e, d=d: e.tensor_tensor(out=A_[:, d * 4:(d + 1) * 4, :], in0=GT4[:, d, 0],
                                                 in1=Bc[:, d * 4:(d + 1) * 4, :], op=ALU.subtract),
             reads=[t_g], writes=[t_g])
    Af = A_[:].rearrange("p a b -> p (a b)")
    for half in range(2):
        S.op(PE, lambda e, half=half: e.transpose(PS[2][0:80, half * 128:(half + 1) * 128],
                                                  Af[:, half * 80:(half + 1) * 80], identf[:]),
             reads=[t_g, t_ident], writes=[tPS[2]])
    for half in range(2):
        S.op(DVE, lambda e, half=half: e.reduce_max(out=AM[:, half:half + 1],
                                                    in_=PS[2][0:80, half * 128:(half + 1) * 128], axis=AX.X),
             writes=[tPS[2], t_g])
        S.op(DVE, lambda e, half=half: e.tensor_scalar_mul(out=D1[:, half, :], in0=identf[0:80, 0:80],
                                                           scalar1=AM[:, half:half + 1]),
             reads=[t_ident, t_g], writes=[t_g])
        S.op(PE, lambda e, half=half: e.matmul(PS[3][:, half * 80:(half + 1) * 80], lhsT=ones[0:80, :],
                                               rhs=D1[:, half, :], start=True, stop=True),
             reads=[t_c, t_g], writes=[tPS[3]])
    S.op(DVE, lambda e: e.tensor_copy(out=AMX[:].rearrange("p a b -> p (a b)"), in_=PS[3][:, 0:160]),
         writes=[tPS[3], t_g])
    for sq, (st0, ln0, _c) in enumerate(SEQS):
        i0, n = st0 // 128, ln0 // 128
        for j in range(n):
            for d in range(2):
                i = i0 + j if d == 0 else i0 + n - 1 - j
                eng = DVE
                sl = slice(d * 4, (d + 1) * 4)
                mp = MS[:, sq, sl]
                S.op(eng, lambda e, mp=mp, sl=sl, i=i: e.tensor_tensor(out=MX[:, sl, i], in0=mp, in1=AMX[:, sl, i],
                                                                       op=ALU.max),
                     reads=[t_g, t_ms], writes=[t_rec[d]])
                S.op(eng, lambda e, mp=mp, sl=sl, i=i: e.tensor_tensor(out=WIL[:, sl, i], in0=mp, in1=MX[:, sl, i],
                                                                       op=ALU.subtract),
                     reads=[t_g, t_ms], writes=[t_rec[d]])
                S.op(eng, lambda e, mp=mp, sl=sl, i=i: e.tensor_tensor(out=mp, in0=BL[:, sl, i], in1=MX[:, sl, i],
                                                                       op=ALU.add),
                     reads=[t_g, t_ms], writes=[t_rec[d]])
    for p in range(2):
        S.op(SP, lambda e, p=p: e.dma_start(out=o_m[p:p + 1, :], in_=MS[0:1, 1 + p, :]), reads=t_rec, dma=True)
    S.op(DVE, lambda e: e.tensor_tensor(out=U[:], in0=A_[:], in1=MX[:], op=ALU.subtract),
         reads=[t_g] + t_rec, writes=[t_g])
    S.op(ACT, lambda e: e.activation(out=U[:], in_=U[:], func=AF.Exp), reads=[t_g], writes=[t_g])
    S.op(DVE, lambda e: e.tensor_tensor(out=CL[:], in0=Bc[:], in1=MX[:], op=ALU.add), reads=[t_g] + t_rec,
         writes=[t_g])
    S.op(ACT, lambda e: e.activation(out=CL[:], in_=CL[:], func=AF.Exp, scale=-1.0), reads=[t_g], writes=[t_g])
    S.op(ACT, lambda e: e.activation(out=WI[:], in_=WIL[:], func=AF.Exp), reads=t_rec, writes=[t_g])
    if "gates" in debug:
        for nm, tl in (("GT", GT), ("U", U), ("WI", WI), ("CL", CL), ("MX", MX), ("Bc", Bc)):
            dbg[nm] = dout("dbg_" + nm, [128, tl.shape[1] * 20])
            S.op(SP, lambda e, nm=nm, tl=tl: e.dma_start(out=dbg[nm], in_=tl[:].rearrange("p a b -> p (a b)")),
                 reads=[t_g] + t_rec, dma=True)
    if upto <= 2:
        S.emit()
        return nc

    S.barrier()
    A.reset(gate_mark)
    hmT_d = dscr("hmT_d", [8, 128, TOK], BF16)
    t_hmT_d = Tk()
    wbufs.append(A.alloc([128, KC, 256], BF16, "wbuf4"))
    t_wb.append(Tk())
    qT = A.alloc([128, 2, TOK], BF16, "qT")
    kT = A.alloc([128, 2, TOK], BF16, "kT")
    kh = A.alloc([128, NT, 256], BF16, "kh")
    vh = A.alloc([128, NT, 257], BF16, "vh")
    HS = A.alloc([128, NT, 256], F32, "HS")
    hmTh = [A.alloc([128, 2, 128], BF16, "hmTh") for _ in range(2)]
    whns = [A.alloc([128, 256], F32, "whn") for _ in range(2)]
    t_whns = [Tk(), Tk()]
    pending_f = []
    Cst = [[A.alloc([128, 2, 257], F32, "Cst") for d in range(2)] for sq in range(2)]
    Cbf = [[A.alloc([128, 2, 257], BF16, "Cbf") for d in range(2)] for sq in range(2)]
    PT = [[A.alloc([128, 128], BF16, "PT") for d in range(2)] for sq in range(2)]
    Vp = [[A.alloc([128, 257], BF16, "Vp") for d in range(2)] for sq in range(2)]
    dnr = [[A.alloc([128, 2], F32, "dnr") for d in range(2)] for sq in range(2)]
    so = [A.alloc([128, 256], F32, "so") for _ in range(2)]
    hn = [A.alloc([128, 256], F32, "hn") for _ in range(2)]
    hg = [A.alloc([128, 256], BF16, "hg") for _ in range(2)]
    hst = A.alloc([128, NT, 6], F32, "hst")
    hmv = A.alloc([128, NT, 2], F32, "hmv")
    hrs = A.alloc([128, NT, 2], F32, "hrs")
    t_qT, t_kT, t_kh, t_vh, t_whn, t_hmTh, t_hstat = [Tk() for _ in range(7)]
    t_HS = [Tk() for _ in range(NT)]
    t_Cst = [[Tk() for d in range(2)] for sq in range(2)]
    t_Cbf = [[Tk() for d in range(2)] for sq in range(2)]
    t_PT = [[Tk() for d in range(2)] for sq in range(2)]
    t_Vp = [[Tk() for d in range(2)] for sq in range(2)]
    t_dnr = [[Tk() for d in range(2)] for sq in range(2)]
    t_so = [Tk(), Tk()]
    t_hn = [Tk(), Tk()]
    t_hg = [Tk(), Tk()]
    S.op(POOL, lambda e: e.memset(vh[:, :, 256:257], 1.0), writes=[t_vh])
    t_hmTh = [Tk(), Tk()]
    evac_ctr = [0]

    def evac(dst, src_ps, t_ps, t_dst, scale=None):
        evac_ctr[0] += 1
        if evac_ctr[0] % 2 == 0:
            if scale is None:
                S.op(ACT, lambda e: e.copy(out=dst, in_=src_ps), writes=[t_ps, t_dst])
            else:
                S.op(ACT, lambda e: e.mul(out=dst, in_=src_ps, mul=scale), writes=[t_ps, t_dst])
        else:
            if scale is None:
                S.op(DVE, lambda e: e.tensor_copy(out=dst, in_=src_ps), writes=[t_ps, t_dst])
            else:
                S.op(DVE, lambda e: e.tensor_scalar_mul(out=dst, in0=src_ps, scalar1=scale), writes=[t_ps, t_dst])

    def proj_featmajor(wb, twb, ncol0, dstT, t_dst, j, scale=None, banks=(0, 8), hook=None):
        for tb in range(5):
            b = next_bank(*banks)
            for k in range(KC):
                S.op(PE, lambda e, k=k, b=b, tb=tb: e.matmul(
                    PS[b][:, :], lhsT=wb[:, k, ncol0:ncol0 + 128], rhs=hT[:, k, tb * 512:(tb + 1) * 512],
                    start=(k == 0), stop=(k == KC - 1)), reads=[twb], writes=[tPS[b]])
            evac(dstT[:, j, tb * 512:(tb + 1) * 512], PS[b][:, :], tPS[b], t_dst, scale)
            if hook is not None:
                hook()

    def proj_tokmajor(wb, twb, dst3, t_dst, banks=(0, 8), hook=None):
        for i2 in range(NT // 2):
            b = next_bank(*banks)
            for ii in range(2):
                i = i2 * 2 + ii
                for k in range(KC):
                    S.op(PE, lambda e, k=k, b=b, i=i, ii=ii: e.matmul(
                        PS[b][:, ii * 256:(ii + 1) * 256], lhsT=hT[:, k, i * 128:(i + 1) * 128], rhs=wb[:, k, 0:256],
                        start=(k == 0), stop=(k == KC - 1)), reads=[twb], writes=[tPS[b]])
            evac(dst3[:, i2 * 2:i2 * 2 + 2, 0:256], PS[b][:, :].rearrange("p (a b) -> p a b", a=2), tPS[b], t_dst)
            if hook is not None:
                hook()

    for h in range(NH):
        wq, twq = load_w(w_in[:, h * 256:(h + 1) * 256], 256)
        wk, twk = load_w(w_in[:, 1024 + h * 256:1024 + (h + 1) * 256], 256)
        wv, twv = load_w(w_in[:, 2048 + h * 256:2048 + (h + 1) * 256], 256)
        hk_ctr = [0]

        def f_hook():
            hk_ctr[0] += 1
            if pending_f and hk_ctr[0] % 2 == 0:
                pending_f.pop(0)()

        for j in range(2):
            proj_featmajor(wq, twq, j * 128, qT, t_qT, j, scale=DH ** -0.5, hook=f_hook)
        for j in range(2):
            proj_featmajor(wk, twk, j * 128, kT, t_kT, j, hook=f_hook)
        for g4 in range(NT // 4):
            b = next_bank()
            bv = PS[b][:].bitcast(BF16)
            for ii in range(4):
                i = g4 * 4 + ii
                for jj in range(2):
                    S.op(PE, lambda e, bv=bv, ii=ii, jj=jj, i=i: e.transpose(
                        bv[:, ii * 256 + jj * 128:ii * 256 + (jj + 1) * 128], kT[:, jj, i * 128:(i + 1) * 128], ident[:]),
                        reads=[t_kT, t_ident], writes=[tPS[b]])
            evac(kh[:, g4 * 4:(g4 + 1) * 4, :], bv[:, :].rearrange("p (a b) -> p a b", a=4), tPS[b], t_kh)
            f_hook()
        proj_tokmajor(wv, twv, vh, t_vh, hook=f_hook)
        while pending_f:
            pending_f.pop(0)()
        wo, two = load_w(w_in[:, 3072 + h * 256:3072 + (h + 1) * 256], 256)
        whn = whns[h % 2]
        t_whn = t_whns[h % 2]
        S.op(SP, lambda e, h=h, whn=whn: e.dma_start(out=whn[:], in_=w_hnorm[h * 256:(h + 1) * 256].partition_broadcast(128)),
             writes=[t_whn], dma=True)

        for d in range(2):
            S.op(SP, lambda e, d=d, h=h: e.dma_start(out=Cst[0][d][:, :, 0:256],
                                                in_=sC[d, h].rearrange("(j p) v -> p j v", p=128)),
                 writes=[t_Cst[0][d]], dma=True)
            S.op(SP, lambda e, d=d, h=h: e.dma_start(out=Cst[0][d][:, :, 256],
                                                in_=sn[d, h].rearrange("(j p) -> p j", p=128),
                                                allow_slow_non_contiguous=True),
                 writes=[t_Cst[0][d]], dma=True)
            S.op(POOL, lambda e, d=d: e.memset(Cst[1][d][:], 0.0), writes=[t_Cst[1][d]])

        visited = set()
        for j in range(16):
          for sq0, (st0, ln0, _c) in enumerate(SEQS):
            i0, n = st0 // 128, ln0 // 128
            off = 2 if sq0 == 2 else 0
            if not (off <= j < off + n):
                continue
            js = j - off
            sq = min(sq0, 1)
            if sq0 == 2 and js == 0:
                for d in range(2):
                    S.op(POOL, lambda e, d=d: e.memset(Cst[1][d][:], 0.0), writes=[t_Cst[1][d]])
            chains = []
            if True:
                for d in range(2):
                    i = i0 + js if d == 0 else i0 + n - 1 - js
                    chains.append(dict(
                        sq=sq, d=d, i=i, tok=slice(i * 128, (i + 1) * 128), col=d * 4 + h,
                        bS=d * 3, bN=d * 3 + 1, bC=d * 3 + 2,
                        cs=Cst[sq][d], cb=Cbf[sq][d], pt=PT[sq][d], vp=Vp[sq][d], dn=dnr[sq][d],
                        tcs=t_Cst[sq][d], tcb=t_Cbf[sq][d], tpt=t_PT[sq][d], tvp=t_Vp[sq][d], tdn=t_dnr[sq][d],
                        wi=WI[:, d * 4 + h, i:i + 1]))
            for C_ in chains:
                S.op(ACT, lambda e, C_=C_: e.mul(out=C_["cb"][:], in_=C_["cs"][:], mul=C_["wi"]),
                     reads=[C_["tcs"], t_g], writes=[C_["tcb"]])
                S.op(ACT, lambda e, C_=C_: e.mul(out=C_["vp"][:], in_=vh[:, C_["i"], :],
                                                 mul=U[:, C_["col"], C_["i"]:C_["i"] + 1]),
                     reads=[t_vh, t_g], writes=[C_["tvp"]])
            for C_ in chains:
                for jj in range(2):
                    S.op(PE, lambda e, jj=jj, C_=C_: e.matmul(
                        PS[C_["bS"]][:, 0:128], lhsT=kT[:, jj, C_["tok"]], rhs=qT[:, jj, C_["tok"]], start=(jj == 0),
                        stop=(jj == 1)), reads=[t_kT, t_qT], writes=[tPS[C_["bS"]]])
            for C_ in chains:
                S.op(DVE, lambda e, C_=C_: e.tensor_tensor(
                    out=C_["pt"][:], in0=PS[C_["bS"]][:, 0:128], in1=(maskF if C_["d"] == 0 else maskB)[:],
                    op=ALU.mult), reads=[t_c], writes=[tPS[C_["bS"]], C_["tpt"]])
            for C_ in chains:
                for jj in range(2):
                    S.op(PE, lambda e, jj=jj, C_=C_: e.matmul(
                        PS[C_["bC"]][:, jj * 256:(jj + 1) * 256], lhsT=kh[:, C_["i"], jj * 128:(jj + 1) * 128],
                        rhs=C_["vp"][:, 0:256], start=True, stop=True), reads=[t_kh, C_["tvp"]],
                        writes=[tPS[C_["bC"]]])
                for jj in range(2):
                    S.op(PE, lambda e, jj=jj, C_=C_: e.matmul(
                        PS[C_["bS"]][:, 300 + jj:301 + jj], lhsT=kh[:, C_["i"], jj * 128:(jj + 1) * 128],
                        rhs=C_["vp"][:, 256:257], start=True, stop=True), reads=[t_kh, C_["tvp"]],
                        writes=[tPS[C_["bS"]]])
            for C_ in chains:
                S.op(DVE, lambda e, C_=C_: e.scalar_tensor_tensor(
                    out=C_["cs"][:, :, 0:256], in0=C_["cs"][:, :, 0:256], scalar=C_["wi"],
                    in1=PS[C_["bC"]][:, :].rearrange("p (a b) -> p a b", a=2), op0=ALU.mult, op1=ALU.add),
                    reads=[t_g, C_["tcb"]], writes=[tPS[C_["bC"]], C_["tcs"]])
                S.op(DVE, lambda e, C_=C_: e.scalar_tensor_tensor(
                    out=C_["cs"][:, :, 256], in0=C_["cs"][:, :, 256], scalar=C_["wi"], in1=PS[C_["bS"]][:, 300:302],
                    op0=ALU.mult, op1=ALU.add), reads=[t_g, C_["tcb"]], writes=[tPS[C_["bS"]], C_["tcs"]])
            if sq0 >= 1 and js == n - 1:
                pass
            for C_ in chains:
                S.op(PE, lambda e, C_=C_: e.matmul(
                    PS[C_["bN"]][:, 0:257], lhsT=C_["pt"][:], rhs=C_["vp"][:], start=True, stop=False),
                    reads=[C_["tpt"], C_["tvp"]], writes=[tPS[C_["bN"]]])
                for jj in range(2):
                    S.op(PE, lambda e, jj=jj, C_=C_: e.matmul(
                        PS[C_["bN"]][:, 0:257], lhsT=qT[:, jj, C_["tok"]], rhs=C_["cb"][:, jj, :], start=False,
                        stop=(jj == 1)), reads=[t_qT, C_["tcb"]], writes=[tPS[C_["bN"]]])
            for C_ in chains:
                S.op(DVE, lambda e, C_=C_: e.tensor_scalar_mul(
                    out=C_["dn"][:, 0:1], in0=PS[C_["bN"]][:, 256:257], scalar1=-1.0),
                    writes=[tPS[C_["bN"]], C_["tdn"]])
            for C_ in chains:
                S.op(DVE, lambda e, C_=C_: e.scalar_tensor_tensor(
                    out=C_["dn"][:, 0:1], in0=C_["dn"][:, 0:1], scalar=CL[:, C_["col"], C_["i"]:C_["i"] + 1],
                    in1=PS[C_["bN"]][:, 256:257], op0=ALU.max, op1=ALU.max), reads=[t_g],
                    writes=[tPS[C_["bN"]], C_["tdn"]])
            for C_ in chains:
                S.op(DVE, lambda e, C_=C_: e.reciprocal(out=C_["dn"][:, 1:2], in_=C_["dn"][:, 0:1]),
                     writes=[C_["tdn"]])
            for C_ in chains:
                i = C_["i"]
                if i not in visited:
                    visited.add(i)
                    S.op(ACT, lambda e, C_=C_: e.mul(out=HS[:, C_["i"], :], in_=PS[C_["bN"]][:, 0:256],
                                                     mul=C_["dn"][:, 1:2]),
                         reads=[C_["tdn"]], writes=[tPS[C_["bN"]], t_HS[i]])
                else:
                    S.op(DVE, lambda e, C_=C_: e.scalar_tensor_tensor(
                        out=HS[:, C_["i"], :], in0=PS[C_["bN"]][:, 0:256], scalar=C_["dn"][:, 1:2],
                        in1=HS[:, C_["i"], :], op0=ALU.mult, op1=ALU.add), reads=[C_["tdn"]],
                        writes=[tPS[C_["bN"]], t_HS[i]])
            if sq0 >= 1 and js == n - 1:
                for d in range(2):
                    S.op(SP, lambda e, sq0=sq0, d=d, h=h: e.dma_start(
                        out=o_C[sq0 - 1, d, h].rearrange("(j p) v -> p j v", p=128), in_=Cst[1][d][:, :, 0:256]),
                        reads=[t_Cst[1][d]], dma=True)
                    S.op(SP, lambda e, sq0=sq0, d=d, h=h: e.dma_start(
                        out=o_n[sq0 - 1, d, h].rearrange("(j p) -> p j", p=128), in_=Cst[1][d][:, :, 256],
                        allow_slow_non_contiguous=True), reads=[t_Cst[1][d]], dma=True)
        if "hraw" in debug and h == 0:
            dbg["hraw"] = dout("dbg_hraw", [128, NT, 256])
            S.op(SP, lambda e: e.dma_start(out=dbg["hraw"], in_=HS[:]), reads=t_HS, dma=True)

        for i in range(NT):
            S.op(DVE, lambda e, i=i: e.bn_stats(out=hst[:, i, :], in_=HS[:, i, :]), reads=[t_HS[i]], writes=[t_hstat])
            S.op(DVE, lambda e, i=i: e.bn_aggr(out=hmv[:, i, :], in_=hst[:, i, :]), writes=[t_hstat])
        S.op(DVE, lambda e: e.tensor_scalar_add(out=hrs[:, :, 0], in0=hmv[:, :, 1], scalar1=EPS), writes=[t_hstat])
        S.op(ACT, lambda e: e.activation(out=hrs[:, :, 0], in_=hrs[:, :, 0], func=AF.Ln), writes=[t_hstat])
        S.op(ACT, lambda e: e.activation(out=hrs[:, :, 0], in_=hrs[:, :, 0], func=AF.Exp, scale=-0.5),
             writes=[t_hstat])
        S.op(DVE, lambda e: e.scalar_tensor_tensor(out=hrs[:, :, 1], in0=hmv[:, :, 0], scalar=-1.0, in1=hrs[:, :, 0],
                                                   op0=ALU.mult, op1=ALU.mult), writes=[t_hstat])
        def f_tile(i, h=h, wo=wo, two=two, whn=whn, t_whn=t_whn):
                b2 = i % 2
                bo = 6 + b2
                for k in range(KC):
                    S.op(PE, lambda e, k=k, i=i, bo=bo, wo=wo: e.matmul(
                        PS[bo][:, 0:256], lhsT=hT[:, k, i * 128:(i + 1) * 128], rhs=wo[:, k, 0:256],
                        start=(k == 0), stop=(k == KC - 1)), reads=[two], writes=[tPS[bo]])
                S.op(ACT, lambda e, b2=b2, bo=bo: e.activation(out=so[b2][:], in_=PS[bo][:, 0:256], func=AF.Sigmoid),
                     writes=[tPS[bo], t_so[b2]])
                S.op(ACT, lambda e, b2=b2, i=i: e.activation(out=hn[b2][:], in_=HS[:, i, :], func=AF.Identity,
                                                             bias=hrs[:, i, 1:2], scale=hrs[:, i, 0:1]),
                     reads=[t_HS[i], t_hstat], writes=[t_hn[b2]])
                S.op(DVE, lambda e, b2=b2: e.tensor_tensor(out=hn[b2][:], in0=hn[b2][:], in1=whn[:],
                                                           op=ALU.mult), reads=[t_whn], writes=[t_hn[b2]])
                S.op(DVE, lambda e, b2=b2: e.tensor_tensor(out=hg[b2][:], in0=hn[b2][:], in1=so[b2][:], op=ALU.mult),
                     reads=[t_hn[b2], t_so[b2]], writes=[t_hg[b2]])
                bt = 4 + b2
                btv = PS[bt][:].bitcast(BF16)
                for jj in range(2):
                    S.op(PE, lambda e, jj=jj, b2=b2, btv=btv: e.transpose(btv[:, jj * 128:(jj + 1) * 128],
                                                                        hg[b2][:, jj * 128:(jj + 1) * 128], ident[:]),
                         reads=[t_hg[b2], t_ident], writes=[tPS[bt]])
                evac(hmTh[b2][:], btv[:, 0:256].rearrange("p (a b) -> p a b", a=2), tPS[bt], t_hmTh[b2])
                S.op(SP, lambda e, b2=b2, i=i, h=h: e.dma_start(
                    out=hmT_d[h * 2:h * 2 + 2, :, i * 128:(i + 1) * 128].rearrange("j p t -> p j t"), in_=hmTh[b2][:]),
                    reads=[t_hmTh[b2]], mwrites=[t_hmT_d], dma=True)

        for i in range(NT):
            pending_f.append(lambda i=i, f_tile=f_tile: f_tile(i))
        if h == NH - 1:
            while pending_f:
                pending_f.pop(0)()
        if "hm" in debug and h == 0:
            dbg["hm"] = dout("dbg_hm", [2, 128, TOK], BF16)
            S.op(SP, lambda e: e.dma_start(out=dbg["hm"], in_=hmT_d[0:2]), reads=[t_hmT_d], dma=True)
        if upto == 3 and h == 0 and "onehead" in debug:
            break
    if upto <= 3:
        S.emit()
        return nc

    S.barrier()
    A.reset(ln_mark)
    wbufs.pop()
    t_wb.pop()
    frT_d = dscr("frT_d", [8, 128, TOK], BF16)
    t_frT_d = Tk()
    XT = A.alloc([128, 2, 2, TOK], BF16, "XT")
    AB = A.alloc([128, 2, NT, 512], BF16, "AB")
    dftc = A.alloc([128, 2, 512], BF16, "dftc")
    ct_s = A.alloc([128, 2, 256], BF16, "ct_s")
    st_s = A.alloc([128, 2, 256], BF16, "st_s")
    ctb = [A.alloc([128, 16, 256], BF16, "ctb") for _ in range(2)]
    stb = [A.alloc([128, 16, 256], BF16, "stb") for _ in range(2)]
    frs = [A.alloc([128, 256], BF16, "frs") for _ in range(2)]
    t_XT, t_AB, t_dc = Tk(), Tk(), Tk()
    t_ctb = [Tk(), Tk()]
    t_frs = [Tk(), Tk()]
    S.op(SP, lambda e: e.dma_start(out=dftc[:], in_=dftc_d.rearrange("(j p) n -> p j n", p=128)), writes=[t_dc], dma=True)
    S.op(SP, lambda e: e.dma_start(out=ct_s[:], in_=ct256.rearrange("(j p) n -> p j n", p=128)), writes=[t_dc], dma=True)
    S.op(SP, lambda e: e.dma_start(out=st_s[:], in_=st256.rearrange("(j p) n -> p j n", p=128)), writes=[t_dc], dma=True)
    frc = [0]
    for gp in range(2):
        for gg in range(2):
            g = gp * 2 + gg
            wf, twf = load_w(w_in[:, 4112 + g * 256:4112 + (g + 1) * 256], 256)
            for j in range(2):
                proj_featmajor(wf, twf, j * 128, XT[:, gg], t_XT, j)
        for gg in range(2):
            for i in range(NT):
                b = next_bank()
                for j in range(2):
                    S.op(PE, lambda e, gg=gg, i=i, j=j, b=b: e.matmul(
                        PS[b][:, :], lhsT=XT[:, gg, j, i * 128:(i + 1) * 128], rhs=dftc[:, j, :], start=(j == 0),
                        stop=(j == 1)), reads=[t_XT, t_dc], writes=[tPS[b]])
                evac(AB[:, gg, i, :], PS[b][:, :], tPS[b], t_AB)
        for tb8 in range(8):
            cb2 = tb8 % 2
            S.op(SP, lambda e, tb8=tb8, cb2=cb2: e.dma_start(
                out=ctb[cb2][:], in_=ct2048[:, tb8 * 256:(tb8 + 1) * 256].rearrange("(i p) n -> p i n", p=128)),
                writes=[t_ctb[cb2]], dma=True)
            S.op(SP, lambda e, tb8=tb8, cb2=cb2: e.dma_start(
                out=stb[cb2][:], in_=st2048[:, tb8 * 256:(tb8 + 1) * 256].rearrange("(i p) n -> p i n", p=128)),
                writes=[t_ctb[cb2]], dma=True)
            for gg in range(2):
                for j in range(2):
                    b = next_bank()
                    for i in range(16):
                        S.op(PE, lambda e, gg=gg, j=j, i=i, b=b, cb2=cb2: e.matmul(
                            PS[b][:, 0:256], lhsT=AB[:, gg, i, j * 128:(j + 1) * 128], rhs=ctb[cb2][:, i, :],
                            start=(i == 0), stop=False), reads=[t_AB, t_ctb[cb2]], writes=[tPS[b]])
                        S.op(PE, lambda e, gg=gg, j=j, i=i, b=b, cb2=cb2: e.matmul(
                            PS[b][:, 0:256], lhsT=AB[:, gg, i, 256 + j * 128:256 + (j + 1) * 128], rhs=stb[cb2][:, i, :],
                            start=False, stop=(i == 15)), reads=[t_AB, t_ctb[cb2]], writes=[tPS[b]])
                    fb = frc[0] % 2
                    frc[0] += 1
                    evac(frs[fb][:], PS[b][:, 0:256], tPS[b], t_frs[fb])
                    S.op(POOL, lambda e, gp=gp, gg=gg, j=j, tb8=tb8, fb=fb: e.dma_start(
                        out=frT_d[(gp * 2 + gg) * 2 + j, :, tb8 * 256:(tb8 + 1) * 256], in_=frs[fb][:]),
                        reads=[t_frs[fb]], mwrites=[t_frT_d], dma=True)
        for p in range(2):
            for gg in range(2):
                for j in range(2):
                    b = next_bank()
                    for ii in range(2):
                        i = 16 + p * 2 + ii
                        S.op(PE, lambda e, gg=gg, j=j, i=i, ii=ii, b=b: e.matmul(
                            PS[b][:, 0:256], lhsT=AB[:, gg, i, j * 128:(j + 1) * 128], rhs=ct_s[:, ii, :],
                            start=(ii == 0), stop=False), reads=[t_AB, t_dc], writes=[tPS[b]])
                        S.op(PE, lambda e, gg=gg, j=j, i=i, ii=ii, b=b: e.matmul(
                            PS[b][:, 0:256], lhsT=AB[:, gg, i, 256 + j * 128:256 + (j + 1) * 128], rhs=st_s[:, ii, :],
                            start=False, stop=(ii == 1)), reads=[t_AB, t_dc], writes=[tPS[b]])
                    fb = frc[0] % 2
                    frc[0] += 1
                    evac(frs[fb][:], PS[b][:, 0:256], tPS[b], t_frs[fb])
                    S.op(POOL, lambda e, gp=gp, gg=gg, j=j, p=p, fb=fb: e.dma_start(
                        out=frT_d[(gp * 2 + gg) * 2 + j, :, 2048 + p * 256:2048 + (p + 1) * 256], in_=frs[fb][:]),
                        reads=[t_frs[fb]], mwrites=[t_frT_d], dma=True)
    if "fr" in debug:
        dbg["fr"] = dout("dbg_fr", [8, 128, TOK], BF16)
        S.op(SP, lambda e: e.dma_start(out=dbg["fr"], in_=frT_d), reads=[t_frT_d], dma=True)
    if upto <= 4:
        S.emit()
        return nc

    S.barrier()
    A.reset(ln_mark)
    sg_d = dscr("sg_d", [2, 16, 128, TOK], BF16)
    t_sg_d = Tk()
    hmT = A.alloc([128, 8, TOK], BF16, "hmT")
    frT = A.alloc([128, 8, TOK], BF16, "frT")
    t_hf = Tk()
    for k8 in range(8):
        S.op(SP, lambda e, k8=k8: e.dma_start(out=hmT[:, k8, :], in_=hmT_d[k8]), reads=[t_hmT_d], writes=[t_hf], dma=True)
        S.op(SP, lambda e, k8=k8: e.dma_start(out=frT[:, k8, :], in_=frT_d[k8]), reads=[t_frT_d], writes=[t_hf], dma=True)
    mark_hf = A.mark()
    sgs = [A.alloc([128, 512], BF16, "sgs") for _ in range(4)]
    t_sgs = [Tk() for _ in range(4)]
    sgc = [0]
    for cbk in range(16):
        for which in range(2):
            c0 = (5136 if which == 0 else 7184) + cbk * 128
            wgx, twgx = load_w(w_in[:, c0:c0 + 128], 128)
            for tb in range(5):
                b = next_bank()
                for k in range(KC):
                    S.op(PE, lambda e, k=k, b=b, tb=tb, wgx=wgx: e.matmul(
                        PS[b][:, :], lhsT=wgx[:, k, 0:128], rhs=hT[:, k, tb * 512:(tb + 1) * 512],
                        start=(k == 0), stop=(k == KC - 1)), reads=[twgx], writes=[tPS[b]])
                sb_ = sgc[0] % 4
                sgc[0] += 1
                S.op(ACT, lambda e, b=b, sb_=sb_: e.activation(out=sgs[sb_][:], in_=PS[b][:, :], func=AF.Sigmoid),
                     writes=[tPS[b], t_sgs[sb_]])
                S.op(SP, lambda e, which=which, cbk=cbk, tb=tb, sb_=sb_: e.dma_start(
                    out=sg_d[which, cbk, :, tb * 512:(tb + 1) * 512], in_=sgs[sb_][:]),
                    reads=[t_sgs[sb_]], mwrites=[t_sg_d], dma=True)
            if ada_pending:
                ada_block(ada_pending.pop(0))
    assert not ada_pending

    S.barrier()
    mixT_d = dscr("mixT_d", [16, 128, TOK], BF16)
    t_mixT_d = Tk()
    A.reset(pers_mark)
    wout = A.alloc([128, KC, D], BF16, "wout")
    g1bc = A.alloc([128, 2, D], F32, "g1bc")
    l1g = A.alloc([128, D], F32, "l1g")
    l1b = A.alloc([128, D], F32, "l1b")
    mark_6 = A.mark()
    wbr = [[A.alloc([128, 8, 128], BF16, "wbr") for _ in range(2)] for _ in range(2)]
    assert A.mark() <= ln_mark, (A.mark(), ln_mark)
    A.reset(mark_hf)
    NSG = 3
    sga = [A.alloc([128, 512], BF16, "sga") for _ in range(NSG)]
    sgb = [A.alloc([128, 512], BF16, "sgb") for _ in range(NSG)]
    tm1 = [A.alloc([128, 512], F32, "tm1") for _ in range(2)]
    tm2 = [A.alloc([128, 512], F32, "tm2") for _ in range(2)]
    mxs = [A.alloc([128, 512], BF16, "mxs") for _ in range(2)]
    t_sga = [Tk() for _ in range(NSG)]
    t_tm1 = [Tk(), Tk()]
    t_tm2 = [Tk(), Tk()]
    t_mxs = [Tk(), Tk()]
    steps5 = [(cbk, tb) for cbk in range(16) for tb in range(5)]

    def load_sg(n):
        cbk, tb = steps5[n]
        r3 = n % NSG
        S.op(SP, lambda e: e.dma_start(out=sga[r3][:], in_=sg_d[0, cbk, :, tb * 512:(tb + 1) * 512]),
             reads=[t_sg_d], writes=[t_sga[r3]], dma=True)
        S.op(SP, lambda e: e.dma_start(out=sgb[r3][:], in_=sg_d[1, cbk, :, tb * 512:(tb + 1) * 512]),
             reads=[t_sg_d], writes=[t_sga[r3]], dma=True)

    t_wbr = [[Tk(), Tk()], [Tk(), Tk()]]

    def load_br(cbk):
        par = cbk % 2
        for which, wsrc in enumerate((w_br_m, w_br_f)):
            buf = wbr[par][which]
            v = wsrc[:, cbk * 128:(cbk + 1) * 128].rearrange("(k p) n -> p k n", p=128)
            S.op(POOL, lambda e, buf=buf, v=v: e.dma_start(out=buf[:], in_=v), writes=[t_wbr[par][which]], dma=True)

    load_sg(0)
    load_br(0)
    t_wout, t_bc = Tk(), Tk()
    for c8 in range(8):
        S.op(POOL, lambda e, c8=c8: e.dma_start(
            out=wout[:, :, c8 * 256:(c8 + 1) * 256],
            in_=w_out[:, c8 * 256:(c8 + 1) * 256].rearrange("(k p) n -> p k n", p=128)), writes=[t_wout], dma=True)
    for c in range(2):
        S.op(SP, lambda e, c=c: e.dma_start(out=g1bc[:, c, :], in_=g_scr[0, c].partition_broadcast(128)),
             reads=[t_gscr], writes=[t_bc], dma=True)
    S.op(SP, lambda e: e.dma_start(out=l1g[:], in_=ln1_g.partition_broadcast(128)), writes=[t_bc], dma=True)
    S.op(SP, lambda e: e.dma_start(out=l1b[:], in_=ln1_b.partition_broadcast(128)), writes=[t_bc], dma=True)
    wm = wf2 = twm = twf2 = None
    for n, (cbk, tb) in enumerate(steps5):
        if n + 1 < len(steps5):
            load_sg(n + 1)
        if tb == 0:
            wm, twm = wbr[cbk % 2][0], t_wbr[cbk % 2][0]
            wf2, twf2 = wbr[cbk % 2][1], t_wbr[cbk % 2][1]
            if cbk + 1 < 16:
                load_br(cbk + 1)
        r2 = n % 2
        r3 = n % NSG
        bm, bf = next_bank(), next_bank()
        for k in range(8):
            S.op(PE, lambda e, k=k, bm=bm, tb=tb, wm=wm: e.matmul(
                PS[bm][:, :], lhsT=wm[:, k, 0:128], rhs=hmT[:, k, tb * 512:(tb + 1) * 512],
                start=(k == 0), stop=(k == 7)), reads=[twm, t_hf], writes=[tPS[bm]])
        for k in range(8):
            S.op(PE, lambda e, k=k, bf=bf, tb=tb, wf2=wf2: e.matmul(
                PS[bf][:, :], lhsT=wf2[:, k, 0:128], rhs=frT[:, k, tb * 512:(tb + 1) * 512],
                start=(k == 0), stop=(k == 7)), reads=[twf2, t_hf], writes=[tPS[bf]])
        S.op(DVE, lambda e, bm=bm, r2=r2, r3=r3: e.tensor_tensor(out=tm1[r2][:], in0=PS[bm][:, :], in1=sga[r3][:],
                                                                 op=ALU.mult), reads=[t_sga[r3]], writes=[tPS[bm], t_tm1[r2]])
        S.op(DVE, lambda e, bf=bf, r2=r2, r3=r3: e.tensor_tensor(out=tm2[r2][:], in0=PS[bf][:, :], in1=sgb[r3][:],
                                                                 op=ALU.mult), reads=[t_sga[r3]], writes=[tPS[bf], t_tm2[r2]])
        S.op(DVE, lambda e, r2=r2: e.tensor_tensor(out=mxs[r2][:], in0=tm1[r2][:], in1=tm2[r2][:], op=ALU.add),
             reads=[t_tm1[r2], t_tm2[r2]], writes=[t_mxs[r2]])
        S.op(ACT, lambda e, cbk=cbk, tb=tb, r2=r2: e.dma_start(
            out=mixT_d[cbk, :, tb * 512:(tb + 1) * 512], in_=mxs[r2][:]), reads=[t_mxs[r2]], mwrites=[t_mixT_d], dma=True)
    if "mixed" in debug:
        dbg["hmall"] = dout("dbg_hmall", [8, 128, TOK], BF16)
        S.op(SP, lambda e: e.dma_start(out=dbg["hmall"], in_=hmT_d), reads=[t_hmT_d], dma=True)
        dbg["sg"] = dout("dbg_sg", [2, 16, 128, TOK], BF16)
        for w_ in range(2):
            S.op(SP, lambda e, w_=w_: e.dma_start(out=dbg["sg"][w_], in_=sg_d[w_]), reads=[t_sg_d], dma=True)
        dbg["mixed"] = dout("dbg_mixed", [16, 128, TOK], BF16)
        S.op(SP, lambda e: e.dma_start(out=dbg["mixed"], in_=mixT_d), reads=[t_mixT_d], dma=True)
    if upto <= 5:
        S.emit()
        return nc

    S.barrier()
    A.reset(mark_6)
    x1_d = dscr("x1_d", [TOK, D])
    h2T_d = dscr("h2T_d", [KC, 128, TOK], BF16)
    t_x1_d, t_h2T_d = Tk(), Tk()
    L6 = LNB()
    rrs = [A.alloc([128, D], F32, "rr") for _ in range(2)]
    x1ts = [A.alloc([128, D], F32, "x1t") for _ in range(3)]
    mxt = [A.alloc([128, KC, 128], BF16, "mxt") for _ in range(3)]
    h2s = [A.alloc([128, KC, 128], BF16, "h2s") for _ in range(2)]
    t_rrs = [Tk(), Tk()]
    t_x1ts = [Tk(), Tk(), Tk()]
    t_mxt = [Tk(), Tk(), Tk()]
    t_h2s = [Tk(), Tk()]

    def load_mx(i, b):
        S.op(SP, lambda e: e.dma_start(out=mxt[b][:], in_=mixT_d[:, :, i * 128:(i + 1) * 128].rearrange("k p t -> p k t")),
             reads=[t_mixT_d], writes=[t_mxt[b]], dma=True)

    def y1_mm(i):
        b = i % 3
        for cb4 in range(4):
            for k in range(KC):
                S.op(PE, lambda e, k=k, cb4=cb4: e.matmul(
                    PS[cb4][:, :], lhsT=mxt[b][:, k, :], rhs=wout[:, k, cb4 * 512:(cb4 + 1) * 512],
                    start=(k == 0), stop=(k == KC - 1)), reads=[t_mxt[b], t_wout], writes=[tPS[cb4]])

    L6b = LNB(with_x=False, with_xn=False)
    nmt = [A.alloc([128, 4], F32, "nmt") for _ in range(2)]
    t_nmt = [Tk(), Tk()]

    def act_norm(L, nm, t_nm, src, t_src, dst, t_dst, b):
        mvb, tmv = L.mv[b], L.t_mv[b]
        S.op(DVE, lambda e: e.tensor_scalar_mul(out=nm[:, 0:1], in0=mvb[:, 0:1], scalar1=-1.0), reads=[tmv], writes=[t_nm])
        S.op(ACT, lambda e: e.mul(out=nm[:, 1:2], in_=nm[:, 0:1], mul=mvb[:, 2:3]), reads=[tmv], writes=[t_nm])
        S.op(ACT, lambda e: e.activation(out=dst[:], in_=src[:], func=AF.Identity, bias=nm[:, 1:2], scale=mvb[:, 2:3]),
             reads=[t_src, tmv, t_nm], writes=[t_dst])

    def a1_dve(i):
        b = i % 2
        c = 0 if i < 16 else 1
        rr, t_rr = rrs[b], t_rrs[b]
        for cb4 in range(4):
            sl = slice(cb4 * 512, (cb4 + 1) * 512)
            S.op(DVE, lambda e, cb4=cb4, sl=sl: e.tensor_tensor(out=rr[:, sl], in0=PS[cb4][:, :], in1=g1bc[:, c, sl],
                                                               op=ALU.mult), reads=[t_bc], writes=[tPS[cb4], t_rr])
        S.op(DVE, lambda e: e.scalar_tensor_tensor(out=rr[:], in0=L6.xt[b][:], scalar=ALPHA, in1=rr[:],
                                                   op0=ALU.mult, op1=ALU.add), reads=[L6.t_xt[b]], writes=[t_rr])
        ln_stats_dve(L6, rr, b, t_rr, nmt[0], t_nmt[0])

    def a1_act(i):
        b = i % 2
        ln_act_norm(L6, nmt[0], t_nmt[0], rrs[b], t_rrs[b], x1ts[i % 3], t_x1ts[i % 3], b)

    def a2_pool(i):
        x1t, t_x1t = x1ts[i % 3], t_x1ts[i % 3]
        S.op(DVE, lambda e: e.tensor_tensor(out=x1t[:], in0=x1t[:], in1=l1g[:], op=ALU.mult), reads=[t_bc],
             writes=[t_x1t])

    def a2_dve(i):
        x1t, t_x1t = x1ts[i % 3], t_x1ts[i % 3]
        S.op(DVE, lambda e: e.tensor_tensor(out=x1t[:], in0=x1t[:], in1=l1b[:], op=ALU.add), reads=[t_bc],
             writes=[t_x1t])
        S.op(POOL, lambda e: e.dma_start(out=x1_d[i * 128:(i + 1) * 128, :], in_=x1t[:]), reads=[t_x1t],
             mwrites=[t_x1_d], dma=True)

    def b_dve(i):
        ln_stats_dve(L6b, x1ts[i % 3], i % 2, t_x1ts[i % 3], nmt[1], t_nmt[1])

    def b_act(i):
        b = i % 2
        ln_act_norm(L6b, nmt[1], t_nmt[1], x1ts[i % 3], t_x1ts[i % 3], L6.xn[b], L6.t_xn[b], b)

    def ln_stats_dve(L, src, b, t_src, nm, t_nm):
        st, mvb, tmv = L.stats[b], L.mv[b], L.t_mv[b]
        for c4 in range(4):
            S.op(DVE, lambda e, c4=c4: e.bn_stats(out=st[:, c4, :], in_=src[:, c4 * 512:(c4 + 1) * 512]),
                 reads=[t_src], writes=[tmv])
        S.op(DVE, lambda e: e.bn_aggr(out=mvb[:, 0:2], in_=st[:]), writes=[tmv])
        S.op(DVE, lambda e: e.tensor_scalar_add(out=mvb[:, 2:3], in0=mvb[:, 1:2], scalar1=EPS), writes=[tmv])
        S.op(DVE, lambda e: e.tensor_scalar_mul(out=nm[:, 0:1], in0=mvb[:, 0:1], scalar1=-1.0), reads=[tmv], writes=[t_nm])

    def ln_act_norm(L, nm, t_nm, src, t_src, dst, t_dst, b):
        mvb, tmv = L.mv[b], L.t_mv[b]
        S.op(ACT, lambda e: e.activation(out=mvb[:, 2:3], in_=mvb[:, 2:3], func=AF.Ln), writes=[tmv])
        S.op(ACT, lambda e: e.activation(out=mvb[:, 2:3], in_=mvb[:, 2:3], func=AF.Exp, scale=-0.5), writes=[tmv])
        S.op(ACT, lambda e: e.mul(out=nm[:, 1:2], in_=nm[:, 0:1], mul=mvb[:, 2:3]), reads=[tmv], writes=[t_nm])
        S.op(ACT, lambda e: e.activation(out=dst[:], in_=src[:], func=AF.Identity, bias=nm[:, 1:2], scale=mvb[:, 2:3]),
             reads=[t_src, tmv, t_nm], writes=[t_dst])

    def c_pe(i):
        norm_part2(L6, i % 2, 0, h2s[i % 2], t_h2s[i % 2], 3, 4, 0 if i < 16 else 1, act_evac=2, only="pe")

    def c_act(i):
        norm_part2(L6, i % 2, 0, h2s[i % 2], t_h2s[i % 2], 3, 4, 0 if i < 16 else 1, act_evac=2, only="evac")

    def c_store(i):
        b = i % 2
        S.op(POOL, lambda e: e.dma_start(out=h2T_d[:, :, i * 128:(i + 1) * 128].rearrange("k p t -> p k t"),
                                         in_=h2s[b][:]), reads=[t_h2s[b]], mwrites=[t_h2T_d], dma=True)

    ok = lambda i: 0 <= i < NT
    load_x(L6, 0, 0)
    load_x(L6, 1, 1)
    for i0 in range(3):
        load_mx(i0, i0)
    y1_mm(0)
    for t in range(-2, NT + 1):
        if ok(t - 1):
            c_pe(t - 1)
        if ok(t + 2):
            a1_dve(t + 2)
        if ok(t - 1):
            c_act(t - 1)
        if ok(t + 1):
            a2_pool(t + 1)
        if ok(t + 2):
            a1_act(t + 2)
        if ok(t):
            b_dve(t)
        if ok(t + 1):
            a2_dve(t + 1)
        if ok(t):
            b_act(t)
        if ok(t + 3):
            if t + 3 >= 2:
                load_x(L6, t + 3, (t + 3) % 2)
            if t + 3 >= 3:
                load_mx(t + 3, (t + 3) % 3)
            y1_mm(t + 3)
        if ok(t - 1):
            c_store(t - 1)
    if "x1" in debug:
        dbg["x1"] = dout("dbg_x1", [TOK, D])
        S.op(SP, lambda e: e.dma_start(out=dbg["x1"], in_=x1_d), reads=[t_x1_d], dma=True)
        dbg["h2"] = dout("dbg_h2", [KC, 128, TOK], BF16)
        S.op(SP, lambda e: e.dma_start(out=dbg["h2"], in_=h2T_d), reads=[t_h2T_d], dma=True)
    if upto <= 6:
        S.emit()
        return nc

    S.barrier()
    A.reset(work_mark)
    z_d = dscr("z_d", [NJ, 128, TOK], BF16)
    t_z_d = Tk()
    h2T = A.alloc([128, KC, TOK], BF16, "h2T")
    cvR = A.alloc([88, 4, 128], F32, "cvR")
    cvT = A.alloc([128, 4, 88], F32, "cvT")
    ub = [A.alloc([128, 2, TOK], F32, "ub") for _ in range(2)]
    cbuf = A.alloc([128, 2, TOK], F32, "cbuf")
    sgt = A.alloc([128, TOK], F32, "sgt")
    zs = [A.alloc([128, TOK], BF16, "zs") for _ in range(2)]
    t_h2T, t_cv, t_cb, t_sgt = Tk(), Tk(), Tk(), Tk()
    t_ub = [Tk(), Tk()]
    t_zs = [Tk(), Tk()]
    t_h2Tb = [Tk() for _ in range(5)]
    for tb in range(5):
        S.op(SP, lambda e, tb=tb: e.dma_start(
            out=h2T[:, :, tb * 512:(tb + 1) * 512], in_=h2T_d[:, :, tb * 512:(tb + 1) * 512].rearrange("k p t -> p k t")),
            reads=[t_h2T_d], writes=[t_h2Tb[tb]], dma=True)
    for tap in range(3):
        S.op(SP, lambda e, tap=tap: e.dma_start(out=cvR[:, tap, :], in_=w_conv[tap].rearrange("(j p) -> j p", p=128)),
             writes=[t_cv], dma=True)
    S.op(SP, lambda e: e.dma_start(out=cvR[:, 3, :], in_=b_conv.rearrange("(j p) -> j p", p=128)), writes=[t_cv], dma=True)
    for tap in range(4):
        S.op(PE, lambda e, tap=tap: e.transpose(PS[7][:, tap * 88:(tap + 1) * 88], cvR[:, tap, :], identf[0:88, 0:88]),
             reads=[t_cv, t_ident], writes=[tPS[7]])
    S.op(DVE, lambda e: e.tensor_copy(out=cvT[:].rearrange("p a b -> p (a b)"), in_=PS[7][:, 0:352]),
         writes=[tPS[7], t_cv])
    for j in range(NJ):
        ubj = ub[j % 2]
        tub = t_ub[j % 2]
        for part in range(2):
            blk = part * NJ + j
            wu, twu = load_w(w_up[:, blk * 128:(blk + 1) * 128], 128)
            for tb in range(5):
                b = next_bank()
                for k in range(KC):
                    S.op(PE, lambda e, k=k, b=b, tb=tb, wu=wu: e.matmul(
                        PS[b][:, :], lhsT=wu[:, k, 0:128], rhs=h2T[:, k, tb * 512:(tb + 1) * 512],
                        start=(k == 0), stop=(k == KC - 1)), reads=[twu, t_h2Tb[tb]], writes=[tPS[b]])
                tsl = slice(tb * 512, (tb + 1) * 512)
                S.op(ACT, lambda e, b=b, part=part, tsl=tsl, blk=blk: e.activation(
                    out=cbuf[:, part, tsl], in_=PS[b][:, :], func=AF.Identity, bias=cvT[:, 3, blk:blk + 1],
                    scale=cvT[:, 1, blk:blk + 1]), reads=[t_cv], writes=[tPS[b], t_cb])
                S.op(DVE, lambda e, b=b, part=part, tsl=tsl, ubj=ubj: e.tensor_copy(out=ubj[:, part, tsl], in_=PS[b][:, :]),
                     writes=[tPS[b], tub])
            for (st0, ln0, _c) in SEQS:
                S.op(DVE, lambda e, part=part, st0=st0, ln0=ln0, blk=blk, ubj=ubj: e.scalar_tensor_tensor(
                    out=cbuf[:, part, st0 + 1:st0 + ln0], in0=ubj[:, part, st0:st0 + ln0 - 1],
                    scalar=cvT[:, 0, blk:blk + 1], in1=cbuf[:, part, st0 + 1:st0 + ln0], op0=ALU.mult, op1=ALU.add),
                    reads=[tub, t_cv], writes=[t_cb])
                S.op(DVE, lambda e, part=part, st0=st0, ln0=ln0, blk=blk, ubj=ubj: e.scalar_tensor_tensor(
                    out=cbuf[:, part, st0:st0 + ln0 - 1], in0=ubj[:, part, st0 + 1:st0 + ln0],
                    scalar=cvT[:, 2, blk:blk + 1], in1=cbuf[:, part, st0:st0 + ln0 - 1], op0=ALU.mult, op1=ALU.add),
                    reads=[tub, t_cv], writes=[t_cb])
        S.op(ACT, lambda e: e.activation(out=sgt[:], in_=cbuf[:, 1, :], func=AF.Silu), reads=[t_cb], writes=[t_sgt])
        zb = j % 2
        S.op(DVE, lambda e, zb=zb: e.tensor_tensor(out=zs[zb][:], in0=sgt[:], in1=cbuf[:, 0, :], op=ALU.mult),
             reads=[t_sgt, t_cb], writes=[t_zs[zb]])
        S.op(SP, lambda e, j=j, zb=zb: e.dma_start(out=z_d[j], in_=zs[zb][:]), reads=[t_zs[zb]], mwrites=[t_z_d], dma=True)
    if "z" in debug:
        dbg["z"] = dout("dbg_z", [NJ, 128, TOK], BF16)
        for j in range(NJ):
            S.op(SP, lambda e, j=j: e.dma_start(out=dbg["z"][j], in_=z_d[j]), reads=[t_z_d], dma=True)
    if upto <= 7:
        S.emit()
        return nc

    S.barrier()
    A.reset(pers_mark)
    r2_d = dscr("r2_d", [TOK, D])
    t_r2_d = Tk()
    wd = A.alloc([128, NJ, 1024], BF16, "wd")
    NZB = 3
    zb_ = [A.alloc([128, 11, 512], BF16, "zb") for _ in range(NZB)]
    g2bc = A.alloc([128, 2, D], F32, "g2bc")
    x1q = [A.alloc([128, 512], F32, "x1q") for _ in range(2)]
    y2s = [A.alloc([128, 512], F32, "y2s") for _ in range(2)]
    l2g = A.alloc([128, D], F32, "l2g")
    l2b = A.alloc([128, D], F32, "l2b")
    rt = [A.alloc([128, D], F32, "rt") for _ in range(2)]
    L9 = LNB(with_x=False, with_xn=False)
    t_wd = [Tk() for _ in range(4)]
    t_zb = [Tk() for _ in range(NZB)]
    t_g2, t_l2 = Tk(), Tk()
    t_x1q = [Tk(), Tk()]
    t_y2s = [Tk(), Tk()]
    t_rt = [Tk(), Tk()]
    ec = 0
    fc = 0
    pieces = [(half, tb, jp) for half in range(2) for tb in range(5) for jp in range(4)]

    def load_zb(n):
        _h, tb, jp = pieces[n]
        zz = n % NZB
        S.op(SP, lambda e: e.dma_start(
            out=zb_[zz][:], in_=z_d[jp * 11:(jp + 1) * 11, :, tb * 512:(tb + 1) * 512].rearrange("j p t -> p j t")),
            reads=[t_z_d], writes=[t_zb[zz]], dma=True)

    def load_wd(half, jp2):
        S.op(POOL, lambda e: e.dma_start(
            out=wd[:, jp2 * 11:(jp2 + 1) * 11, :],
            in_=w_down[jp2 * 11 * 128:(jp2 + 1) * 11 * 128, half * 1024:(half + 1) * 1024].rearrange(
                "(j p) n -> p j n", p=128)), writes=[t_wd[jp2]], dma=True)

    load_zb(0)
    load_zb(1)
    for c in range(2):
        S.op(SP, lambda e, c=c: e.dma_start(out=g2bc[:, c, :], in_=g_scr[1, c].partition_broadcast(128)),
             reads=[t_gscr], writes=[t_g2], dma=True)
    S.op(SP, lambda e: e.dma_start(out=l2g[:], in_=ln2_g.partition_broadcast(128)), writes=[t_l2], dma=True)
    S.op(SP, lambda e: e.dma_start(out=l2b[:], in_=ln2_b.partition_broadcast(128)), writes=[t_l2], dma=True)
    for n, (half, tb, jp) in enumerate(pieces):
        if n == 0:
            for jp2 in range(4):
                load_wd(0, jp2)
        if n + 2 < len(pieces):
            load_zb(n + 2)
        zz = n % NZB
        for ii in range(4):
            for qq in range(2):
                bk = ii * 2 + qq
                for jl in range(11):
                    jg = jp * 11 + jl
                    S.op(PE, lambda e, zz=zz, ii=ii, qq=qq, jl=jl, jg=jg, bk=bk: e.matmul(
                        PS[bk][:, :], lhsT=zb_[zz][:, jl, ii * 128:(ii + 1) * 128],
                        rhs=wd[:, jg, qq * 512:(qq + 1) * 512], start=(jg == 0), stop=(jg == NJ - 1)),
                        reads=[t_zb[zz], t_wd[jp]], writes=[tPS[bk]])
        if half == 0 and tb == 4:
            load_wd(1, jp)
        if jp == 3:
            for ii in range(4):
                i = tb * 4 + ii
                c = 0 if i < 16 else 1
                for qq in range(2):
                    bk = ii * 2 + qq
                    e2 = ec % 2
                    ec += 1
                    csl = slice(half * 1024 + qq * 512, half * 1024 + (qq + 1) * 512)
                    S.op(SP, lambda e, i=i, csl=csl, e2=e2: e.dma_start(out=x1q[e2][:], in_=x1_d[i * 128:(i + 1) * 128, csl]),
                         reads=[t_x1_d], writes=[t_x1q[e2]], dma=True)
                    S.op(DVE, lambda e, bk=bk, c=c, csl=csl, e2=e2: e.tensor_tensor(
                        out=y2s[e2][:], in0=PS[bk][:, :], in1=g2bc[:, c, csl], op=ALU.mult), reads=[t_g2],
                        writes=[tPS[bk], t_y2s[e2]])
                    S.op(DVE, lambda e, e2=e2: e.scalar_tensor_tensor(out=y2s[e2][:], in0=x1q[e2][:], scalar=ALPHA,
                                                                      in1=y2s[e2][:], op0=ALU.mult, op1=ALU.add),
                         reads=[t_x1q[e2]], writes=[t_y2s[e2]])
                    S.op(SP, lambda e, i=i, csl=csl, e2=e2: e.dma_start(out=r2_d[i * 128:(i + 1) * 128, csl], in_=y2s[e2][:]),
                         reads=[t_y2s[e2]], mwrites=[t_r2_d], dma=True)
            if half == 1:
                for ii in range(4):
                    i = tb * 4 + ii
                    b = fc % 2
                    fc += 1
                    S.op(ACT, lambda e, i=i, b=b: e.dma_start(out=rt[b][:], in_=r2_d[i * 128:(i + 1) * 128, :]),
                         reads=[t_r2_d], writes=[t_rt[b]], dma=True)
                    ln_stats(L9, rt[b], b, t_rt[b])
                    S.op(DVE, lambda e, b=b: e.tensor_scalar(out=rt[b][:], in0=rt[b][:], scalar1=L9.mv[b][:, 0:1],
                                                             scalar2=L9.mv[b][:, 2:3], op0=ALU.subtract, op1=ALU.mult),
                         reads=[L9.t_mv[b]], writes=[t_rt[b]])
                    S.op(POOL, lambda e, b=b: e.tensor_tensor(out=rt[b][:], in0=rt[b][:], in1=l2g[:], op=ALU.mult),
                         reads=[t_l2], writes=[t_rt[b]])
                    S.op(POOL, lambda e, b=b: e.tensor_tensor(out=rt[b][:], in0=rt[b][:], in1=l2b[:], op=ALU.add),
                         reads=[t_l2], writes=[t_rt[b]])
                    S.op(POOL, lambda e, i=i, b=b: e.dma_start(out=y_rows(i), in_=rt[b][:]), reads=[t_rt[b]], dma=True)
    S.emit()
    return nc


def _consts():
    f64 = np.float64
    t = np.arange(2048)
    r = (t // 64).astype(np.float32)[:, None]
    col = (t % 64).astype(np.float32)[:, None]
    quarter = D // 4
    freq = (1.0 / (10000.0 ** (np.arange(quarter, dtype=np.float32) / np.float32(quarter)))).astype(np.float32)
    er, ec = r * freq, col * freq
    posemb = np.concatenate([np.sin(er), np.cos(er), np.sin(ec), np.cos(ec)], -1).astype(np.float32)
    s = np.arange(128)
    maskF = (s[:, None] <= s[None, :]).astype(np.float32)
    maskB = (s[:, None] >= s[None, :]).astype(np.float32)
    c = np.arange(256, dtype=f64)
    ang = 2 * np.pi * np.outer(c, c) / 256.0
    dftc = np.concatenate([np.cos(ang), np.sin(ang)], 1).astype(ml_dtypes.bfloat16)

    def seq_tables(T):
        tt = np.arange(T, dtype=f64)
        a = 2 * np.pi * (np.outer(tt, tt) % T) / T
        nrm = 1.0 / np.sqrt(T * 256.0)
        return (np.cos(a) * nrm).astype(ml_dtypes.bfloat16), (-np.sin(a) * nrm).astype(ml_dtypes.bfloat16)

    ct2048, st2048 = seq_tables(2048)
    ct256, st256 = seq_tables(256)
    return dict(posemb=posemb, maskF=maskF, maskB=maskB, dftc=dftc, ct2048=ct2048, st2048=st2048,
                ct256=ct256, st256=st256)


def make_in_maps(inputs, n=8):
    cs = _consts()
    f = lambda a: np.ascontiguousarray(np.asarray(a, dtype=np.float32))
    shared = {k: f(inputs[k][0]) for k in ["w_ada", "b_ada", "w_in", "b_gate", "w_hnorm", "w_br_m", "w_br_f", "w_out",
                                           "ln1_g", "ln1_b", "w_up", "w_conv", "b_conv", "w_down", "ln2_g", "ln2_b"]}
    shared.update(cs)
    maps = []
    for i in range(n):
        m = dict(shared)
        m["xs"] = f(inputs["x_sample"][i])
        m["xp"] = f(inputs["x_prompt"][2 * i:2 * i + 2]).reshape(512, D)
        m["cond"] = np.stack([f(inputs["c"][i]), f(inputs["c_ctx"])], 0)
        m["sC"] = f(inputs["state_C"][i, 0])
        m["sn"] = f(inputs["state_n"][i, 0])
        m["sm"] = f(inputs["state_m"][i, 0]).reshape(8)
        maps.append(m)
    return maps


def kernel(**inputs):
    nc = build_program()
    maps = make_in_maps(inputs)
    res = run_bass_kernel_spmd(nc, maps, core_ids=list(range(8)))
    R = res.results
    y_p = np.concatenate([r["y_p"].reshape(2, 256, D) for r in R], 0)
    y_s = np.stack([r["y_s"] for r in R], 0)
    o_C = np.concatenate([r["o_C"] for r in R], 0)[:, None]
    o_n = np.concatenate([r["o_n"] for r in R], 0)[:, None]
    o_m = np.concatenate([r["o_m"].reshape(2, 2, NH) for r in R], 0)[:, None]
    return (y_p.astype(np.float32), y_s.astype(np.float32), o_C.astype(np.float32), o_n.astype(np.float32),
            o_m.astype(np.float32))
```

```python
import numpy as np
import ml_dtypes
import concourse.bass as bass
import concourse.mybir as mybir
from concourse.bass_utils import run_bass_kernel_spmd

F32 = mybir.dt.float32
BF16 = mybir.dt.bfloat16
AF = mybir.ActivationFunctionType
ALU = mybir.AluOpType
AX = mybir.AxisListType

PE, ACT, DVE, POOL, SP = "pe", "act", "dve", "pool", "sp"
N_DMA_SEMS = 8

D = 2048
KC = 16
NT = 20
TOK = 2560
DH = 256
NH = 4
DFF = 5632
NJ = 44
D_IN = 9232
ALPHA = 2.0 ** 0.25
EPS = 1e-5
SEQS = [(0, 2048, 0), (2048, 256, 1), (2304, 256, 1)]


class Tk:
    __slots__ = ("w", "r", "ws")

    def __init__(self):
        self.w = None
        self.r = []
        self.ws = []


class Op:
    __slots__ = ("eng", "fn", "deps", "is_dma", "needs_inc", "idx", "dslot", "dgen")

    def __init__(self, eng, fn, is_dma):
        self.eng = eng
        self.fn = fn
        self.deps = []
        self.is_dma = is_dma
        self.needs_inc = False
        self.idx = None
        self.dslot = None
        self.dgen = None


class Sched:
    def __init__(self, nc):
        self.nc = nc
        self.ops = {PE: [], ACT: [], DVE: [], POOL: [], SP: []}
        self.ndma = {ACT: 0, POOL: 0, SP: 0}
        self.dma_hist = {ACT: [], POOL: [], SP: []}
        self.last = {PE: None, ACT: None, DVE: None, POOL: None}
        self.bar = {}

    def op(self, eng, fn, reads=(), writes=(), dma=False, mwrites=()):
        o = Op(eng, fn, dma)
        deps = []
        for t in reads:
            if t.w is not None:
                deps.append(t.w)
            deps.extend(t.ws)
        for t in writes:
            if t.w is not None:
                deps.append(t.w)
            deps.extend(t.r)
            deps.extend(t.ws)
        for t in mwrites:
            if t.w is not None:
                deps.append(t.w)
            deps.extend(t.r)
        if eng in self.bar:
            deps.extend(self.bar.pop(eng))
        if dma:
            k = self.ndma[eng]
            o.dslot = k % N_DMA_SEMS
            o.dgen = k // N_DMA_SEMS
            self.ndma[eng] = k + 1
            hist = self.dma_hist[eng]
            if k >= N_DMA_SEMS:
                deps.append(hist[k - N_DMA_SEMS])
            hist.append(o)
        else:
            self.last[eng] = o
        seen = set()
        for d in deps:
            if d is o or id(d) in seen:
                continue
            seen.add(id(d))
            if (not d.is_dma) and (not dma) and d.eng == PE and eng == PE:
                continue
            o.deps.append(d)
            if not d.is_dma:
                d.needs_inc = True
        for t in reads:
            t.r.append(o)
        for t in writes:
            t.w = o
            t.r = []
            t.ws = []
        for t in mwrites:
            t.ws.append(o)
        self.ops[eng].append(o)
        return o

    def barrier(self):
        pend = [o for o in self.last.values() if o is not None]
        for e in self.dma_hist:
            pend.extend(self.dma_hist[e][-N_DMA_SEMS:])
        for e in self.ops:
            self.bar[e] = list(pend)

    def emit(self):
        nc = self.nc
        from contextlib import ExitStack
        with ExitStack() as es:
            esem = {e: es.enter_context(nc.semaphore("s_" + e)) for e in (PE, ACT, DVE, POOL)}
            dsem = {e: [es.enter_context(nc.semaphore("d_%s%d" % (e, i))) for i in range(N_DMA_SEMS)]
                    for e in (ACT, POOL, SP)}
            for e in (PE, ACT, DVE, POOL):
                c = 0
                for o in self.ops[e]:
                    if (not o.is_dma) and o.needs_inc:
                        c += 1
                        o.idx = c
            block = es.enter_context(nc.Block())

            def run(e, engh):
                waited = {}
                for o in self.ops[e]:
                    for d in o.deps:
                        if d.is_dma:
                            sem = dsem[d.eng][d.dslot]
                            val = 16 * (d.dgen + 1)
                        else:
                            sem = esem[d.eng]
                            val = d.idx
                        key = id(sem)
                        if waited.get(key, 0) >= val:
                            continue
                        waited[key] = val
                        engh.wait_ge(sem, val)
                    ins = o.fn(engh)
                    if o.is_dma:
                        ins.then_inc(dsem[e][o.dslot], 16)
                    elif o.needs_inc:
                        ins.then_inc(esem[e], 1)
                if e in self.dma_hist:
                    for o in self.dma_hist[e][-N_DMA_SEMS:]:
                        engh.wait_ge(dsem[e][o.dslot], 16 * (o.dgen + 1))

            @block.tensor
            def _(eng):
                run(PE, eng)

            @block.scalar
            def _(eng):
                run(ACT, eng)

            @block.vector
            def _(eng):
                run(DVE, eng)

            @block.gpsimd
            def _(eng):
                run(POOL, eng)

            @block.sync
            def _(eng):
                run(SP, eng)


class Arena:
    def __init__(self, nc, base, limit):
        self.nc = nc
        self.base = base
        self.limit = limit
        self.cur = base
        self.n = 0

    def mark(self):
        return self.cur

    def reset(self, m):
        self.cur = m

    def alloc(self, shape, dtype, name=None):
        esz = 4 if dtype == F32 else 2
        free = 1
        for s in shape[1:]:
            free *= s
        nbytes = (free * esz + 63) // 64 * 64
        off = self.cur
        assert off + nbytes <= self.limit, ("SBUF arena overflow", name, off, nbytes, self.limit)
        self.cur = off + nbytes
        self.n += 1
        return self.nc.alloc_sbuf_tensor_at("%s_%d" % (name or "t", self.n), list(shape), dtype, offset=off)


def build_program(upto=99, debug=()):
    nc = bass.Bass("TRN2", target_bir_lowering=False)
    S = Sched(nc)

    def din(name, shape, dt=F32):
        return nc.dram_tensor(name, list(shape), dt, kind="ExternalInput").ap()

    def dout(name, shape, dt=F32):
        return nc.dram_tensor(name, list(shape), dt, kind="ExternalOutput").ap()

    def dscr(name, shape, dt=F32):
        return nc.dram_tensor(name, list(shape), dt).ap()

    xs = din("xs", [2048, D])
    xp = din("xp", [512, D])
    cond = din("cond", [2, D])
    sC = din("sC", [2, NH, DH, DH])
    sn = din("sn", [2, NH, DH])
    sm = din("sm", [8])
    w_ada = din("w_ada", [D, 6 * D])
    b_ada = din("b_ada", [6 * D])
    w_in = din("w_in", [D, D_IN])
    b_gate = din("b_gate", [16])
    w_hnorm = din("w_hnorm", [1024])
    w_br_m = din("w_br_m", [1024, D])
    w_br_f = din("w_br_f", [1024, D])
    w_out = din("w_out", [D, D])
    ln1_g = din("ln1_g", [D])
    ln1_b = din("ln1_b", [D])
    w_up = din("w_up", [D, 2 * DFF])
    w_conv = din("w_conv", [3, 2 * DFF])
    b_conv = din("b_conv", [2 * DFF])
    w_down = din("w_down", [DFF, D])
    ln2_g = din("ln2_g", [D])
    ln2_b = din("ln2_b", [D])
    posemb = din("posemb", [2048, D])
    maskF_d = din("maskF", [128, 128])
    maskB_d = din("maskB", [128, 128])
    dftc_d = din("dftc", [256, 512], BF16)
    ct2048 = din("ct2048", [2048, 2048], BF16)
    st2048 = din("st2048", [2048, 2048], BF16)
    ct256 = din("ct256", [256, 256], BF16)
    st256 = din("st256", [256, 256], BF16)

    y_s = dout("y_s", [2048, D])
    y_p = dout("y_p", [512, D])
    o_C = dout("o_C", [2, 2, NH, DH, DH])
    o_n = dout("o_n", [2, 2, NH, DH])
    o_m = dout("o_m", [2, 8])
    dbg = {}

    def x_rows(i):
        return xs[i * 128:(i + 1) * 128, :] if i < 16 else xp[(i - 16) * 128:(i - 15) * 128, :]

    def y_rows(i):
        return y_s[i * 128:(i + 1) * 128, :] if i < 16 else y_p[(i - 16) * 128:(i - 15) * 128, :]

    A = Arena(nc, 16512, 229376)
    ident = A.alloc([128, 128], BF16, "ident")
    identf = A.alloc([128, 128], F32, "identf")
    modT = A.alloc([128, 96, 2], F32, "modT")
    t_modT = Tk()
    t_ident = Tk()
    condB = A.alloc([128, 2, 16], BF16, "condB")
    badaT = A.alloc([128, 96], F32, "badaT")
    gstage = A.alloc([1, 512], F32, "gstage")
    gbrow = A.alloc([1, 256], F32, "gbrow")
    pers_mark = A.mark()

    PS = [nc.alloc_psum_tensor("ps%d" % i, [128, 512], F32) for i in range(8)]
    tPS = [Tk() for _ in range(8)]
    PSB = [PS[6][:].bitcast(BF16), PS[7][:].bitcast(BF16)]
    tPSB = [tPS[6], tPS[7]]

    S.op(POOL, lambda e: e.memset(identf[:], 1.0), writes=[t_ident])
    S.op(POOL, lambda e: e.affine_select(out=identf[:], in_=identf[:], pattern=[[-1, 128]],
                                         compare_op=ALU.is_equal, fill=0.0, base=0, channel_multiplier=1),
         reads=[t_ident], writes=[t_ident])
    S.op(DVE, lambda e: e.tensor_copy(out=ident[:], in_=identf[:]), reads=[t_ident], writes=[t_ident])

    NWB = 3
    wbufs = [A.alloc([128, KC, 256], BF16, "wbuf") for _ in range(NWB)]
    t_wb = [Tk() for _ in range(NWB)]
    wctr = [0]

    def load_w(src2d, ncols, kc=KC, rows_pk=False):
        b = wctr[0] % len(wbufs)
        wctr[0] += 1
        if rows_pk:
            v = src2d.rearrange("(p k) n -> p k n", k=kc)
        else:
            v = src2d.rearrange("(k p) n -> p k n", p=128)
        buf = wbufs[b]
        S.op(POOL, lambda e: e.dma_start(out=buf[:, 0:kc, 0:ncols], in_=v), writes=[t_wb[b]], dma=True)
        return buf, t_wb[b]

    work_mark = A.mark()

    pbank = [0]

    def next_bank(lo=0, hi=8):
        b = lo + pbank[0] % (hi - lo)
        pbank[0] += 1
        return b

    hT = A.alloc([128, KC, TOK], BF16, "hT")
    t_hT = [Tk() for _ in range(NT)]
    ln_mark = A.mark()
    condS = A.alloc([128, 2, 16], F32, "condS")
    badaR = A.alloc([96, 128], F32, "badaR")
    t_cond, t_condB, t_badaR, t_badaT, t_gst = [Tk() for _ in range(5)]
    g_scr = dscr("g_scr", [2, 2, D])
    t_gscr = Tk()

    for c in range(2):
        S.op(SP, lambda e, c=c: e.dma_start(out=condS[:, c, :], in_=cond[c].rearrange("(p k) -> p k", k=16)),
             writes=[t_cond], dma=True)
    S.op(ACT, lambda e: e.activation(out=condB[:], in_=condS[:], func=AF.Silu), reads=[t_cond], writes=[t_condB])
    S.op(SP, lambda e: e.dma_start(out=badaR[:], in_=b_ada.rearrange("(j p) -> j p", p=128)),
         writes=[t_badaR], dma=True)
    S.op(PE, lambda e: e.transpose(PS[1][:, 0:96], badaR[:], identf[0:96, 0:96]),
         reads=[t_badaR, t_ident], writes=[tPS[1]])
    S.op(ACT, lambda e: e.copy(out=badaT[:], in_=PS[1][:, 0:96]), writes=[tPS[1], t_badaT])
    for sec in (1, 4):
        S.op(DVE, lambda e, sec=sec: e.tensor_scalar_add(out=badaT[:, sec * 16:(sec + 1) * 16],
                                                         in0=badaT[:, sec * 16:(sec + 1) * 16], scalar1=1.0),
             writes=[t_badaT])

    def ada_block(blk):
        sec = blk // 8
        wb, twb = load_w(w_ada[:, blk * 256:(blk + 1) * 256], 256, rows_pk=True)
        b = next_bank()
        if sec in (2, 5):
            gi = 0 if sec == 2 else 1
            c0 = (blk % 8) * 256
            S.op(SP, lambda e: e.dma_start(out=gbrow[0:1, :], in_=b_ada[sec * D + c0:sec * D + c0 + 256].partition_broadcast(1)),
                 writes=[t_gst], dma=True)
            for c in range(2):
                for k in range(16):
                    S.op(PE, lambda e, c=c, k=k: e.matmul(
                        PS[b][0:1, c * 256:(c + 1) * 256], lhsT=condB[:, c, k:k + 1], rhs=wb[:, k, 0:256],
                        start=(k == 0), stop=(k == 15)), reads=[t_condB, twb], writes=[tPS[b]])
            for c in range(2):
                S.op(DVE, lambda e, c=c: e.tensor_tensor(
                    out=gstage[0:1, c * 256:(c + 1) * 256], in0=PS[b][0:1, c * 256:(c + 1) * 256], in1=gbrow[0:1, :],
                    op=ALU.add), writes=[tPS[b], t_gst])
            for c in range(2):
                S.op(SP, lambda e, c=c: e.dma_start(
                    out=g_scr[gi, c:c + 1, c0:c0 + 256], in_=gstage[0:1, c * 256:(c + 1) * 256]),
                    reads=[t_gst], mwrites=[t_gscr], dma=True)
        else:
            jj0 = blk * 2
            for j in range(2):
                for k in range(16):
                    S.op(PE, lambda e, j=j, k=k: e.matmul(
                        PS[b][:, j * 2:j * 2 + 2], lhsT=wb[:, k, j * 128:(j + 1) * 128], rhs=condB[:, :, k],
                        start=(k == 0), stop=(k == 15)), reads=[t_condB, twb], writes=[tPS[b]])
            for c in range(2):
                S.op(DVE, lambda e, c=c: e.tensor_tensor(
                    out=modT[:, jj0:jj0 + 2, c], in0=PS[b][:, 0:4].rearrange("p (j c) -> p j c", c=2)[:, :, c],
                    in1=badaT[:, jj0:jj0 + 2], op=ALU.add), reads=[t_badaT], writes=[tPS[b], t_modT])

    for blk in range(16):
        ada_block(blk)
    ada_pending = list(range(16, 48))

    class LNB:
        def __init__(self, with_x=True, with_xn=True):
            if with_x:
                self.xt = [A.alloc([128, D], F32, "xt") for _ in range(2)]
                self.pt = [A.alloc([128, D], F32, "pt") for _ in range(2)]
                self.t_xt = [Tk(), Tk()]
                self.t_pt = [Tk(), Tk()]
            if with_xn:
                self.xn = [A.alloc([128, D], BF16, "xn") for _ in range(2)]
                self.t_xn = [Tk(), Tk()]
            self.stats = [A.alloc([128, 4, 6], F32, "stats") for _ in range(2)]
            self.mv = [A.alloc([128, 4], F32, "mv") for _ in range(2)]
            self.t_mv = [Tk(), Tk()]

    def ln_stats(L, src, b, t_src):
        st, mvb, tmv = L.stats[b], L.mv[b], L.t_mv[b]
        for c4 in range(4):
            S.op(DVE, lambda e, c4=c4: e.bn_stats(out=st[:, c4, :], in_=src[:, c4 * 512:(c4 + 1) * 512]),
                 reads=[t_src], writes=[tmv])
        S.op(DVE, lambda e: e.bn_aggr(out=mvb[:, 0:2], in_=st[:]), writes=[tmv])
        S.op(DVE, lambda e: e.tensor_scalar_add(out=mvb[:, 2:3], in0=mvb[:, 1:2], scalar1=EPS), writes=[tmv])
        S.op(ACT, lambda e: e.activation(out=mvb[:, 2:3], in_=mvb[:, 2:3], func=AF.Ln), writes=[tmv])
        S.op(ACT, lambda e: e.activation(out=mvb[:, 2:3], in_=mvb[:, 2:3], func=AF.Exp, scale=-0.5), writes=[tmv])

    def norm_mod_T(L, src, b, t_src, i, dstT, t_dst, sec_sh, sec_sc, c):
        norm_part1(L, src, b, t_src)
        norm_part2(L, b, i, dstT, t_dst, sec_sh, sec_sc, c)

    def norm_part1(L, src, b, t_src):
        ln_stats(L, src, b, t_src)
        xnb, txn, mvb, tmv = L.xn[b], L.t_xn[b], L.mv[b], L.t_mv[b]
        S.op(DVE, lambda e: e.tensor_scalar(out=xnb[:], in0=src[:], scalar1=mvb[:, 0:1], scalar2=mvb[:, 2:3],
                                            op0=ALU.subtract, op1=ALU.mult), reads=[t_src, tmv], writes=[txn])

    def norm_part2(L, b, i, dstT, t_dst, sec_sh, sec_sc, c, act_evac=False, only=None):
        xnb, txn = L.xn[b], L.t_xn[b]
        for hb in range(2):
            for q in range(8):
                k = hb * 8 + q
                if only == "evac":
                    break
                S.op(PE, lambda e, k=k, q=q, hb=hb: e.transpose(
                    PSB[hb][:, q * 128:(q + 1) * 128], xnb[:, k * 128:(k + 1) * 128], ident[:]),
                    reads=[txn, t_ident], writes=[tPSB[hb]])
            for q in range(8):
                if only == "pe":
                    break
                k = hb * 8 + q
                src_ps = PSB[hb][:, q * 128:(q + 1) * 128]
                dst = dstT[:, k, i * 128:(i + 1) * 128]
                sc_ap = modT[:, sec_sc * 16 + k, c:c + 1]
                sh_ap = modT[:, sec_sh * 16 + k, c:c + 1]
                if act_evac == 2 or (act_evac and q % 2 == 0):
                    S.op(ACT, lambda e, src_ps=src_ps, dst=dst, sc_ap=sc_ap, sh_ap=sh_ap: e.activation(
                        out=dst, in_=src_ps, func=AF.Identity, bias=sh_ap, scale=sc_ap),
                        reads=[t_modT], writes=[tPSB[hb], t_dst])
                else:
                    S.op(DVE, lambda e, src_ps=src_ps, dst=dst, sc_ap=sc_ap, sh_ap=sh_ap: e.tensor_scalar(
                        out=dst, in0=src_ps, scalar1=sc_ap, scalar2=sh_ap, op0=ALU.mult, op1=ALU.add),
                        reads=[t_modT], writes=[tPSB[hb], t_dst])

    def load_x(L, i, b):
        xtb, ptb, txt, tpt = L.xt[b], L.pt[b], L.t_xt[b], L.t_pt[b]
        S.op(SP, lambda e: e.dma_start(out=xtb[:], in_=x_rows(i)), writes=[txt], dma=True)
        if i < 16:
            S.op(SP, lambda e: e.dma_start(out=ptb[:], in_=posemb[i * 128:(i + 1) * 128, :]), writes=[tpt], dma=True)
            S.op(POOL, lambda e: e.tensor_tensor(out=xtb[:], in0=xtb[:], in1=ptb[:], op=ALU.add),
                 reads=[tpt], writes=[txt])

    L1 = LNB()

    load_x(L1, 0, 0)
    load_x(L1, 1, 1)
    norm_part1(L1, L1.xt[0], 0, L1.t_xt[0])
    for i in range(NT):
        if i + 1 < NT:
            norm_part1(L1, L1.xt[(i + 1) % 2], (i + 1) % 2, L1.t_xt[(i + 1) % 2])
        if i + 2 < NT:
            load_x(L1, i + 2, i % 2)
        norm_part2(L1, i % 2, i, hT, t_hT[i], 0, 1, 0 if i < 16 else 1)
    if "modT" in debug:
        dbg["modT"] = dout("dbg_modT", [128, 192])
        S.op(SP, lambda e: e.dma_start(out=dbg["modT"], in_=modT[:].rearrange("p j c -> p (j c)")),
             reads=[t_modT], dma=True)
    if "hT" in debug:
        dbg["hT"] = dout("dbg_hT", [128, KC, TOK], BF16)
        for k in range(KC):
            S.op(SP, lambda e, k=k: e.dma_start(out=dbg["hT"][:, k, :], in_=hT[:, k, :]), reads=t_hT, dma=True)
    if upto <= 1:
        S.emit()
        return nc

    S.barrier()
    A.reset(ln_mark)
    maskF = A.alloc([128, 128], F32, "maskF")
    maskB = A.alloc([128, 128], F32, "maskB")
    U = A.alloc([128, 8, 20], F32, "U")
    WI = A.alloc([128, 8, 20], F32, "WI")
    CL = A.alloc([128, 8, 20], F32, "CL")
    gate_mark = A.mark()
    ones = A.alloc([128, 128], F32, "ones")
    bgate_bc = A.alloc([128, 16], F32, "bgate")
    GT = A.alloc([128, 16, 20], F32, "GT")
    LF = A.alloc([128, 8, 20], F32, "LF")
    Bc = A.alloc([128, 8, 20], F32, "Bc")
    BL = A.alloc([128, 8, 20], F32, "BL")
    A_ = A.alloc([128, 8, 20], F32, "A_")
    AMX = A.alloc([128, 8, 20], F32, "AMX")
    MX = A.alloc([128, 8, 20], F32, "MX")
    WIL = A.alloc([128, 8, 20], F32, "WIL")
    AM = A.alloc([80, 2], F32, "AM")
    D1 = A.alloc([80, 2, 80], F32, "D1")
    MS = A.alloc([128, 3, 8], F32, "MS")
    t_c, t_g, t_ms = Tk(), Tk(), Tk()
    t_rec = [Tk(), Tk()]
    S.op(SP, lambda e: e.dma_start(out=maskF[:], in_=maskF_d), writes=[t_c], dma=True)
    S.op(SP, lambda e: e.dma_start(out=maskB[:], in_=maskB_d), writes=[t_c], dma=True)
    S.op(SP, lambda e: e.dma_start(out=bgate_bc[:], in_=b_gate.partition_broadcast(128)), writes=[t_c], dma=True)
    S.op(POOL, lambda e: e.memset(ones[:], 1.0), writes=[t_c])
    S.op(SP, lambda e: e.dma_start(out=MS[:, 0, :], in_=sm.partition_broadcast(128)), writes=[t_ms], dma=True)
    S.op(POOL, lambda e: e.memset(MS[:, 1:3, :], 0.0), writes=[t_ms])

    wg, twg = load_w(w_in[:, 4096:4112], 16)
    for i in range(NT):
        for k in range(KC):
            S.op(PE, lambda e, i=i, k=k: e.matmul(PS[0][:, i * 16:(i + 1) * 16], lhsT=hT[:, k, i * 128:(i + 1) * 128],
                                                  rhs=wg[:, k, 0:16], start=(k == 0), stop=(k == KC - 1)),
                 reads=[twg], writes=[tPS[0]])
    S.op(DVE, lambda e: e.tensor_tensor(out=GT[:], in0=PS[0][:, 0:320].rearrange("p (i g) -> p g i", g=16),
                                        in1=bgate_bc[:, :].unsqueeze(2).to_broadcast([128, 16, 20]), op=ALU.add),
         reads=[t_c], writes=[tPS[0], t_g])
    GT4 = GT[:].rearrange("p (d k h) i -> p d k h i", d=2, k=2)
    for d in range(2):
        S.op(ACT, lambda e, d=d: e.activation(out=LF[:, d * 4:(d + 1) * 4, :], in_=GT4[:, d, 1], func=AF.Exp,
                                              scale=-1.0), reads=[t_g], writes=[t_g])
    S.op(ACT, lambda e: e.activation(out=LF[:], in_=LF[:], func=AF.Ln, bias=1.0), reads=[t_g], writes=[t_g])
    S.op(DVE, lambda e: e.tensor_scalar_mul(out=LF[:], in0=LF[:], scalar1=-1.0), reads=[t_g], writes=[t_g])
    LFf = LF[:].rearrange("p a b -> p (a b)")
    S.op(PE, lambda e: e.matmul(PS[1][:, 0:80], lhsT=maskF[:], rhs=LFf[:, 0:80], start=True, stop=True),
         reads=[t_c, t_g], writes=[tPS[1]])
    S.op(PE, lambda e: e.matmul(PS[1][:, 80:160], lhsT=maskB[:], rhs=LFf[:, 80:160], start=True, stop=True),
         reads=[t_c, t_g], writes=[tPS[1]])
    S.op(PE, lambda e: e.matmul(PS[1][:, 160:320], lhsT=ones[:], rhs=LFf, start=True, stop=True),
         reads=[t_c, t_g], writes=[tPS[1]])
    S.op(DVE, lambda e: e.tensor_copy(out=Bc[:].rearrange("p a b -> p (a b)"), in_=PS[1][:, 0:160]),
         writes=[tPS[1], t_g])
    S.op(DVE, lambda e: e.tensor_copy(out=BL[:].rearrange("p a b -> p (a b)"), in_=PS[1][:, 160:320]),
         writes=[tPS[1], t_g])
    for d in range(2):
        S.op(DVE, lambda e, d=d: e.tensor_tensor(out=A_[:, d * 4:(d + 1) * 4, :], in0=GT4[:, d, 0],
                                                 in1=Bc[:, d * 4:(d + 1) * 4, :], op=ALU.subtract),
             reads=[t_g], writes=[t_g])
    Af = A_[:].rearrange("p a b -> p (a b)")
    for half in range(2):
        S.op(PE, lambda e, half=half: e.transpose(PS[2][0:80, half * 128:(half + 1) * 128],
                                                  Af[:, half * 80:(half + 1) * 80], identf[:]),
             reads=[t_g, t_ident], writes=[tPS[2]])
    for half in range(2):
        S.op(DVE, lambda e, half=half: e.reduce_max(out=AM[:, half:half + 1],
                                                    in_=PS[2][0:80, half * 128:(half + 1) * 128], axis=AX.X),
             writes=[tPS[2], t_g])
        S.op(DVE, lambda e, half=half: e.tensor_scalar_mul(out=D1[:, half, :], in0=identf[0:80, 0:80],
                                                           scalar1=AM[:, half:half + 1]),
             reads=[t_ident, t_g], writes=[t_g])
        S.op(PE, lambda e, half=half: e.matmul(PS[3][:, half * 80:(half + 1) * 80], lhsT=ones[0:80, :],
                                               rhs=D1[:, half, :], start=True, stop=True),
             reads=[t_c, t_g], writes=[tPS[3]])
    S.op(DVE, lambda e: e.tensor_copy(out=AMX[:].rearrange("p a b -> p (a b)"), in_=PS[3][:, 0:160]),
         writes=[tPS[3], t_g])
    for sq, (st0, ln0, _c) in enumerate(SEQS):
        i0, n = st0 // 128, ln0 // 128
        for j in range(n):
            for d in range(2):
                i = i0 + j if d == 0 else i0 + n - 1 - j
                eng = DVE
                sl = slice(d * 4, (d + 1) * 4)
                mp = MS[:, sq, sl]
                S.op(eng, lambda e, mp=mp, sl=sl, i=i: e.tensor_tensor(out=MX[:, sl, i], in0=mp, in1=AMX[:, sl, i],
                                                                       op=ALU.max),
                     reads=[t_g, t_ms], writes=[t_rec[d]])
                S.op(eng, lambda e, mp=mp, sl=sl, i=i: e.tensor_tensor(out=WIL[:, sl, i], in0=mp, in1=MX[:, sl, i],
                                                                       op=ALU.subtract),
                     reads=[t_g, t_ms], writes=[t_rec[d]])
                S.op(eng, lambda e, mp=mp, sl=sl, i=i: e.tensor_tensor(out=mp, in0=BL[:, sl, i], in1=MX[:, sl, i],
                                                                       op=ALU.add),
                     reads=[t_g, t_ms], writes=[t_rec[d]])
    for p in range(2):
        S.op(SP, lambda e, p=p: e.dma_start(out=o_m[p:p + 1, :], in_=MS[0:1, 1 + p, :]), reads=t_rec, dma=True)
    S.op(DVE, lambda e: e.tensor_tensor(out=U[:], in0=A_[:], in1=MX[:], op=ALU.subtract),
         reads=[t_g] + t_rec, writes=[t_g])
    S.op(ACT, lambda e: e.activation(out=U[:], in_=U[:], func=AF.Exp), reads=[t_g], writes=[t_g])
    S.op(DVE, lambda e: e.tensor_tensor(out=CL[:], in0=Bc[:], in1=MX[:], op=ALU.add), reads=[t_g] + t_rec,
         writes=[t_g])
    S.op(ACT, lambda e: e.activation(out=CL[:], in_=CL[:], func=AF.Exp, scale=-1.0), reads=[t_g], writes=[t_g])
    S.op(ACT, lambda e: e.activation(out=WI[:], in_=WIL[:], func=AF.Exp), reads=t_rec, writes=[t_g])
    if "gates" in debug:
        for nm, tl in (("GT", GT), ("U", U), ("WI", WI), ("CL", CL), ("MX", MX), ("Bc", Bc)):
            dbg[nm] = dout("dbg_" + nm, [128, tl.shape[1] * 20])
            S.op(SP, lambda e, nm=nm, tl=tl: e.dma_start(out=dbg[nm], in_=tl[:].rearrange("p a b -> p (a b)")),
                 reads=[t_g] + t_rec, dma=True)
    if upto <= 2:
        S.emit()
        return nc

    S.barrier()
    A.reset(gate_mark)
    hmT_d = dscr("hmT_d", [8, 128, TOK], BF16)
    t_hmT_d = Tk()
    wbufs.append(A.alloc([128, KC, 256], BF16, "wbuf4"))
    t_wb.append(Tk())
    qT = A.alloc([128, 2, TOK], BF16, "qT")
    kT = A.alloc([128, 2, TOK], BF16, "kT")
    kh = A.alloc([128, NT, 256], BF16, "kh")
    vh = A.alloc([128, NT, 257], BF16, "vh")
    HS = A.alloc([128, NT, 256], F32, "HS")
    hmTh = [A.alloc([128, 2, 128], BF16, "hmTh") for _ in range(2)]
    whns = [A.alloc([128, 256], F32, "whn") for _ in range(2)]
    t_whns = [Tk(), Tk()]
    pending_f = []
    Cst = [[A.alloc([128, 2, 257], F32, "Cst") for d in range(2)] for sq in range(2)]
    Cbf = [[A.alloc([128, 2, 257], BF16, "Cbf") for d in range(2)] for sq in range(2)]
    PT = [[A.alloc([128, 128], BF16, "PT") for d in range(2)] for sq in range(2)]
    Vp = [[A.alloc([128, 257], BF16, "Vp") for d in range(2)] for sq in range(2)]
    dnr = [[A.alloc([128, 2], F32, "dnr") for d in range(2)] for sq in range(2)]
    so = [A.alloc([128, 256], F32, "so") for _ in range(2)]
    hn = [A.alloc([128, 256], F32, "hn") for _ in range(2)]
    hg = [A.alloc([128, 256], BF16, "hg") for _ in range(2)]
    hst = A.alloc([128, NT, 6], F32, "hst")
    hmv = A.alloc([128, NT, 2], F32, "hmv")
    hrs = A.alloc([128, NT, 2], F32, "hrs")
    t_qT, t_kT, t_kh, t_vh, t_whn, t_hmTh, t_hstat = [Tk() for _ in range(7)]
    t_HS = [Tk() for _ in range(NT)]
    t_Cst = [[Tk() for d in range(2)] for sq in range(2)]
    t_Cbf = [[Tk() for d in range(2)] for sq in range(2)]
    t_PT = [[Tk() for d in range(2)] for sq in range(2)]
    t_Vp = [[Tk() for d in range(2)] for sq in range(2)]
    t_dnr = [[Tk() for d in range(2)] for sq in range(2)]
    t_so = [Tk(), Tk()]
    t_hn = [Tk(), Tk()]
    t_hg = [Tk(), Tk()]
    S.op(POOL, lambda e: e.memset(vh[:, :, 256:257], 1.0), writes=[t_vh])
    t_hmTh = [Tk(), Tk()]
    evac_ctr = [0]

    def evac(dst, src_ps, t_ps, t_dst, scale=None):
        evac_ctr[0] += 1
        if evac_ctr[0] % 2 == 0:
            if scale is None:
                S.op(ACT, lambda e: e.copy(out=dst, in_=src_ps), writes=[t_ps, t_dst])
            else:
                S.op(ACT, lambda e: e.mul(out=dst, in_=src_ps, mul=scale), writes=[t_ps, t_dst])
        else:
            if scale is None:
                S.op(DVE, lambda e: e.tensor_copy(out=dst, in_=src_ps), writes=[t_ps, t_dst])
            else:
                S.op(DVE, lambda e: e.tensor_scalar_mul(out=dst, in0=src_ps, scalar1=scale), writes=[t_ps, t_dst])

    def proj_featmajor(wb, twb, ncol0, dstT, t_dst, j, scale=None, banks=(0, 8), hook=None):
        for tb in range(5):
            b = next_bank(*banks)
            for k in range(KC):
                S.op(PE, lambda e, k=k, b=b, tb=tb: e.matmul(
                    PS[b][:, :], lhsT=wb[:, k, ncol0:ncol0 + 128], rhs=hT[:, k, tb * 512:(tb + 1) * 512],
                    start=(k == 0), stop=(k == KC - 1)), reads=[twb], writes=[tPS[b]])
            evac(dstT[:, j, tb * 512:(tb + 1) * 512], PS[b][:, :], tPS[b], t_dst, scale)
            if hook is not None:
                hook()

    def proj_tokmajor(wb, twb, dst3, t_dst, banks=(0, 8), hook=None):
        for i2 in range(NT // 2):
            b = next_bank(*banks)
            for ii in range(2):
                i = i2 * 2 + ii
                for k in range(KC):
                    S.op(PE, lambda e, k=k, b=b, i=i, ii=ii: e.matmul(
                        PS[b][:, ii * 256:(ii + 1) * 256], lhsT=hT[:, k, i * 128:(i + 1) * 128], rhs=wb[:, k, 0:256],
                        start=(k == 0), stop=(k == KC - 1)), reads=[twb], writes=[tPS[b]])
            evac(dst3[:, i2 * 2:i2 * 2 + 2, 0:256], PS[b][:, :].rearrange("p (a b) -> p a b", a=2), tPS[b], t_dst)
            if hook is not None:
                hook()

    for h in range(NH):
        wq, twq = load_w(w_in[:, h * 256:(h + 1) * 256], 256)
        wk, twk = load_w(w_in[:, 1024 + h * 256:1024 + (h + 1) * 256], 256)
        wv, twv = load_w(w_in[:, 2048 + h * 256:2048 + (h + 1) * 256], 256)
        hk_ctr = [0]

        def f_hook():
            hk_ctr[0] += 1
            if pending_f and hk_ctr[0] % 2 == 0:
                pending_f.pop(0)()

        for j in range(2):
            proj_featmajor(wq, twq, j * 128, qT, t_qT, j, scale=DH ** -0.5, hook=f_hook)
        for j in range(2):
            proj_featmajor(wk, twk, j * 128, kT, t_kT, j, hook=f_hook)
        for g4 in range(NT // 4):
            b = next_bank()
            bv = PS[b][:].bitcast(BF16)
            for ii in range(4):
                i = g4 * 4 + ii
                for jj in range(2):
                    S.op(PE, lambda e, bv=bv, ii=ii, jj=jj, i=i: e.transpose(
                        bv[:, ii * 256 + jj * 128:ii * 256 + (jj + 1) * 128], kT[:, jj, i * 128:(i + 1) * 128], ident[:]),
                        reads=[t_kT, t_ident], writes=[tPS[b]])
            evac(kh[:, g4 * 4:(g4 + 1) * 4, :], bv[:, :].rearrange("p (a b) -> p a b", a=4), tPS[b], t_kh)
            f_hook()
        proj_tokmajor(wv, twv, vh, t_vh, hook=f_hook)
        while pending_f:
            pending_f.pop(0)()
        wo, two = load_w(w_in[:, 3072 + h * 256:3072 + (h + 1) * 256], 256)
        whn = whns[h % 2]
        t_whn = t_whns[h % 2]
        S.op(SP, lambda e, h=h, whn=whn: e.dma_start(out=whn[:], in_=w_hnorm[h * 256:(h + 1) * 256].partition_broadcast(128)),
             writes=[t_whn], dma=True)

        for d in range(2):
            S.op(SP, lambda e, d=d, h=h: e.dma_start(out=Cst[0][d][:, :, 0:256],
                                                in_=sC[d, h].rearrange("(j p) v -> p j v", p=128)),
                 writes=[t_Cst[0][d]], dma=True)
            S.op(SP, lambda e, d=d, h=h: e.dma_start(out=Cst[0][d][:, :, 256],
                                                in_=sn[d, h].rearrange("(j p) -> p j", p=128),
                                                allow_slow_non_contiguous=True),
                 writes=[t_Cst[0][d]], dma=True)
            S.op(POOL, lambda e, d=d: e.memset(Cst[1][d][:], 0.0), writes=[t_Cst[1][d]])

        visited = set()
        for j in range(16):
          for sq0, (st0, ln0, _c) in enumerate(SEQS):
            i0, n = st0 // 128, ln0 // 128
            off = 2 if sq0 == 2 else 0
            if not (off <= j < off + n):
                continue
            js = j - off
            sq = min(sq0, 1)
            if sq0 == 2 and js == 0:
                for d in range(2):
                    S.op(POOL, lambda e, d=d: e.memset(Cst[1][d][:], 0.0), writes=[t_Cst[1][d]])
            chains = []
            if True:
                for d in range(2):
                    i = i0 + js if d == 0 else i0 + n - 1 - js
                    chains.append(dict(
                        sq=sq, d=d, i=i, tok=slice(i * 128, (i + 1) * 128), col=d * 4 + h,
                        bS=d * 3, bN=d * 3 + 1, bC=d * 3 + 2,
                        cs=Cst[sq][d], cb=Cbf[sq][d], pt=PT[sq][d], vp=Vp[sq][d], dn=dnr[sq][d],
                        tcs=t_Cst[sq][d], tcb=t_Cbf[sq][d], tpt=t_PT[sq][d], tvp=t_Vp[sq][d], tdn=t_dnr[sq][d],
                        wi=WI[:, d * 4 + h, i:i + 1]))
            for C_ in chains:
                S.op(ACT, lambda e, C_=C_: e.mul(out=C_["cb"][:], in_=C_["cs"][:], mul=C_["wi"]),
                     reads=[C_["tcs"], t_g], writes=[C_["tcb"]])
                S.op(ACT, lambda e, C_=C_: e.mul(out=C_["vp"][:], in_=vh[:, C_["i"], :],
                                                 mul=U[:, C_["col"], C_["i"]:C_["i"] + 1]),
                     reads=[t_vh, t_g], writes=[C_["tvp"]])
            for C_ in chains:
                for jj in range(2):
                    S.op(PE, lambda e, jj=jj, C_=C_: e.matmul(
                        PS[C_["bS"]][:, 0:128], lhsT=kT[:, jj, C_["tok"]], rhs=qT[:, jj, C_["tok"]], start=(jj == 0),
                        stop=(jj == 1)), reads=[t_kT, t_qT], writes=[tPS[C_["bS"]]])
            for C_ in chains:
                S.op(DVE, lambda e, C_=C_: e.tensor_tensor(
                    out=C_["pt"][:], in0=PS[C_["bS"]][:, 0:128], in1=(maskF if C_["d"] == 0 else maskB)[:],
                    op=ALU.mult), reads=[t_c], writes=[tPS[C_["bS"]], C_["tpt"]])
            for C_ in chains:
                for jj in range(2):
                    S.op(PE, lambda e, jj=jj, C_=C_: e.matmul(
                        PS[C_["bC"]][:, jj * 256:(jj + 1) * 256], lhsT=kh[:, C_["i"], jj * 128:(jj + 1) * 128],
                        rhs=C_["vp"][:, 0:256], start=True, stop=True), reads=[t_kh, C_["tvp"]],
                        writes=[tPS[C_["bC"]]])
                for jj in range(2):
                    S.op(PE, lambda e, jj=jj, C_=C_: e.matmul(
                        PS[C_["bS"]][:, 300 + jj:301 + jj], lhsT=kh[:, C_["i"], jj * 128:(jj + 1) * 128],
                        rhs=C_["vp"][:, 256:257], start=True, stop=True), reads=[t_kh, C_["tvp"]],
                        writes=[tPS[C_["bS"]]])
            for C_ in chains:
                S.op(DVE, lambda e, C_=C_: e.scalar_tensor_tensor(
                    out=C_["cs"][:, :, 0:256], in0=C_["cs"][:, :, 0:256], scalar=C_["wi"],
                    in1=PS[C_["bC"]][:, :].rearrange("p (a b) -> p a b", a=2), op0=ALU.mult, op1=ALU.add),
                    reads=[t_g, C_["tcb"]], writes=[tPS[C_["bC"]], C_["tcs"]])
                S.op(DVE, lambda e, C_=C_: e.scalar_tensor_tensor(
                    out=C_["cs"][:, :, 256], in0=C_["cs"][:, :, 256], scalar=C_["wi"], in1=PS[C_["bS"]][:, 300:302],
                    op0=ALU.mult, op1=ALU.add), reads=[t_g, C_["tcb"]], writes=[tPS[C_["bS"]], C_["tcs"]])
            if sq0 >= 1 and js == n - 1:
                pass
            for C_ in chains:
                S.op(PE, lambda e, C_=C_: e.matmul(
                    PS[C_["bN"]][:, 0:257], lhsT=C_["pt"][:], rhs=C_["vp"][:], start=True, stop=False),
                    reads=[C_["tpt"], C_["tvp"]], writes=[tPS[C_["bN"]]])
                for jj in range(2):
                    S.op(PE, lambda e, jj=jj, C_=C_: e.matmul(
                        PS[C_["bN"]][:, 0:257], lhsT=qT[:, jj, C_["tok"]], rhs=C_["cb"][:, jj, :], start=False,
                        stop=(jj == 1)), reads=[t_qT, C_["tcb"]], writes=[tPS[C_["bN"]]])
            for C_ in chains:
                S.op(DVE, lambda e, C_=C_: e.tensor_scalar_mul(
                    out=C_["dn"][:, 0:1], in0=PS[C_["bN"]][:, 256:257], scalar1=-1.0),
                    writes=[tPS[C_["bN"]], C_["tdn"]])
            for C_ in chains:
                S.op(DVE, lambda e, C_=C_: e.scalar_tensor_tensor(
                    out=C_["dn"][:, 0:1], in0=C_["dn"][:, 0:1], scalar=CL[:, C_["col"], C_["i"]:C_["i"] + 1],
                    in1=PS[C_["bN"]][:, 256:257], op0=ALU.max, op1=ALU.max), reads=[t_g],
                    writes=[tPS[C_["bN"]], C_["tdn"]])
            for C_ in chains:
                S.op(DVE, lambda e, C_=C_: e.reciprocal(out=C_["dn"][:, 1:2], in_=C_["dn"][:, 0:1]),
                     writes=[C_["tdn"]])
            for C_ in chains:
                i = C_["i"]
                if i not in visited:
                    visited.add(i)
                    S.op(ACT, lambda e, C_=C_: e.mul(out=HS[:, C_["i"], :], in_=PS[C_["bN"]][:, 0:256],
                                                     mul=C_["dn"][:, 1:2]),
                         reads=[C_["tdn"]], writes=[tPS[C_["bN"]], t_HS[i]])
                else:
                    S.op(DVE, lambda e, C_=C_: e.scalar_tensor_tensor(
                        out=HS[:, C_["i"], :], in0=PS[C_["bN"]][:, 0:256], scalar=C_["dn"][:, 1:2],
                        in1=HS[:, C_["i"], :], op0=ALU.mult, op1=ALU.add), reads=[C_["tdn"]],
                        writes=[tPS[C_["bN"]], t_HS[i]])
            if sq0 >= 1 and js == n - 1:
                for d in range(2):
                    S.op(SP, lambda e, sq0=sq0, d=d, h=h: e.dma_start(
                        out=o_C[sq0 - 1, d, h].rearrange("(j p) v -> p j v", p=128), in_=Cst[1][d][:, :, 0:256]),
                        reads=[t_Cst[1][d]], dma=True)
                    S.op(SP, lambda e, sq0=sq0, d=d, h=h: e.dma_start(
                        out=o_n[sq0 - 1, d, h].rearrange("(j p) -> p j", p=128), in_=Cst[1][d][:, :, 256],
                        allow_slow_non_contiguous=True), reads=[t_Cst[1][d]], dma=True)
        if "hraw" in debug and h == 0:
            dbg["hraw"] = dout("dbg_hraw", [128, NT, 256])
            S.op(SP, lambda e: e.dma_start(out=dbg["hraw"], in_=HS[:]), reads=t_HS, dma=True)

        for i in range(NT):
            S.op(DVE, lambda e, i=i: e.bn_stats(out=hst[:, i, :], in_=HS[:, i, :]), reads=[t_HS[i]], writes=[t_hstat])
            S.op(DVE, lambda e, i=i: e.bn_aggr(out=hmv[:, i, :], in_=hst[:, i, :]), writes=[t_hstat])
        S.op(DVE, lambda e: e.tensor_scalar_add(out=hrs[:, :, 0], in0=hmv[:, :, 1], scalar1=EPS), writes=[t_hstat])
        S.op(ACT, lambda e: e.activation(out=hrs[:, :, 0], in_=hrs[:, :, 0], func=AF.Ln), writes=[t_hstat])
        S.op(ACT, lambda e: e.activation(out=hrs[:, :, 0], in_=hrs[:, :, 0], func=AF.Exp, scale=-0.5),
             writes=[t_hstat])
        S.op(DVE, lambda e: e.scalar_tensor_tensor(out=hrs[:, :, 1], in0=hmv[:, :, 0], scalar=-1.0, in1=hrs[:, :, 0],
                                                   op0=ALU.mult, op1=ALU.mult), writes=[t_hstat])
        def f_tile(i, h=h, wo=wo, two=two, whn=whn, t_whn=t_whn):
                b2 = i % 2
                bo = 6 + b2
                for k in range(KC):
                    S.op(PE, lambda e, k=k, i=i, bo=bo, wo=wo: e.matmul(
                        PS[bo][:, 0:256], lhsT=hT[:, k, i * 128:(i + 1) * 128], rhs=wo[:, k, 0:256],
                        start=(k == 0), stop=(k == KC - 1)), reads=[two], writes=[tPS[bo]])
                S.op(ACT, lambda e, b2=b2, bo=bo: e.activation(out=so[b2][:], in_=PS[bo][:, 0:256], func=AF.Sigmoid),
                     writes=[tPS[bo], t_so[b2]])
                S.op(ACT, lambda e, b2=b2, i=i: e.activation(out=hn[b2][:], in_=HS[:, i, :], func=AF.Identity,
                                                             bias=hrs[:, i, 1:2], scale=hrs[:, i, 0:1]),
                     reads=[t_HS[i], t_hstat], writes=[t_hn[b2]])
                S.op(DVE, lambda e, b2=b2: e.tensor_tensor(out=hn[b2][:], in0=hn[b2][:], in1=whn[:],
                                                           op=ALU.mult), reads=[t_whn], writes=[t_hn[b2]])
                S.op(DVE, lambda e, b2=b2: e.tensor_tensor(out=hg[b2][:], in0=hn[b2][:], in1=so[b2][:], op=ALU.mult),
                     reads=[t_hn[b2], t_so[b2]], writes=[t_hg[b2]])
                bt = 4 + b2
                btv = PS[bt][:].bitcast(BF16)
                for jj in range(2):
                    S.op(PE, lambda e, jj=jj, b2=b2, btv=btv: e.transpose(btv[:, jj * 128:(jj + 1) * 128],
                                                                        hg[b2][:, jj * 128:(jj + 1) * 128], ident[:]),
                         reads=[t_hg[b2], t_ident], writes=[tPS[bt]])
                evac(hmTh[b2][:], btv[:, 0:256].rearrange("p (a b) -> p a b", a=2), tPS[bt], t_hmTh[b2])
                S.op(SP, lambda e, b2=b2, i=i, h=h: e.dma_start(
                    out=hmT_d[h * 2:h * 2 + 2, :, i * 128:(i + 1) * 128].rearrange("j p t -> p j t"), in_=hmTh[b2][:]),
                    reads=[t_hmTh[b2]], mwrites=[t_hmT_d], dma=True)

        for i in range(NT):
            pending_f.append(lambda i=i, f_tile=f_tile: f_tile(i))
        if h == NH - 1:
            while pending_f:
                pending_f.pop(0)()
        if "hm" in debug and h == 0:
            dbg["hm"] = dout("dbg_hm", [2, 128, TOK], BF16)
            S.op(SP, lambda e: e.dma_start(out=dbg["hm"], in_=hmT_d[0:2]), reads=[t_hmT_d], dma=True)
        if upto == 3 and h == 0 and "onehead" in debug:
            break
    if upto <= 3:
        S.emit()
        return nc

    S.barrier()
    A.reset(ln_mark)
    wbufs.pop()
    t_wb.pop()
    frT_d = dscr("frT_d", [8, 128, TOK], BF16)
    t_frT_d = Tk()
    XT = A.alloc([128, 2, 2, TOK], BF16, "XT")
    AB = A.alloc([128, 2, NT, 512], BF16, "AB")
    dftc = A.alloc([128, 2, 512], BF16, "dftc")
    ct_s = A.alloc([128, 2, 256], BF16, "ct_s")
    st_s = A.alloc([128, 2, 256], BF16, "st_s")
    ctb = [A.alloc([128, 16, 256], BF16, "ctb") for _ in range(2)]
    stb = [A.alloc([128, 16, 256], BF16, "stb") for _ in range(2)]
    frs = [A.alloc([128, 256], BF16, "frs") for _ in range(2)]
    t_XT, t_AB, t_dc = Tk(), Tk(), Tk()
    t_ctb = [Tk(), Tk()]
    t_frs = [Tk(), Tk()]
    S.op(SP, lambda e: e.dma_start(out=dftc[:], in_=dftc_d.rearrange("(j p) n -> p j n", p=128)), writes=[t_dc], dma=True)
    S.op(SP, lambda e: e.dma_start(out=ct_s[:], in_=ct256.rearrange("(j p) n -> p j n", p=128)), writes=[t_dc], dma=True)
    S.op(SP, lambda e: e.dma_start(out=st_s[:], in_=st256.rearrange("(j p) n -> p j n", p=128)), writes=[t_dc], dma=True)
    frc = [0]
    for gp in range(2):
        for gg in range(2):
            g = gp * 2 + gg
            wf, twf = load_w(w_in[:, 4112 + g * 256:4112 + (g + 1) * 256], 256)
            for j in range(2):
                proj_featmajor(wf, twf, j * 128, XT[:, gg], t_XT, j)
        for gg in range(2):
            for i in range(NT):
                b = next_bank()
                for j in range(2):
                    S.op(PE, lambda e, gg=gg, i=i, j=j, b=b: e.matmul(
                        PS[b][:, :], lhsT=XT[:, gg, j, i * 128:(i + 1) * 128], rhs=dftc[:, j, :], start=(j == 0),
                        stop=(j == 1)), reads=[t_XT, t_dc], writes=[tPS[b]])
                evac(AB[:, gg, i, :], PS[b][:, :], tPS[b], t_AB)
        for tb8 in range(8):
            cb2 = tb8 % 2
            S.op(SP, lambda e, tb8=tb8, cb2=cb2: e.dma_start(
                out=ctb[cb2][:], in_=ct2048[:, tb8 * 256:(tb8 + 1) * 256].rearrange("(i p) n -> p i n", p=128)),
                writes=[t_ctb[cb2]], dma=True)
            S.op(SP, lambda e, tb8=tb8, cb2=cb2: e.dma_start(
                out=stb[cb2][:], in_=st2048[:, tb8 * 256:(tb8 + 1) * 256].rearrange("(i p) n -> p i n", p=128)),
                writes=[t_ctb[cb2]], dma=True)
            for gg in range(2):
                for j in range(2):
                    b = next_bank()
                    for i in range(16):
                        S.op(PE, lambda e, gg=gg, j=j, i=i, b=b, cb2=cb2: e.matmul(
                            PS[b][:, 0:256], lhsT=AB[:, gg, i, j * 128:(j + 1) * 128], rhs=ctb[cb2][:, i, :],
                            start=(i == 0), stop=False), reads=[t_AB, t_ctb[cb2]], writes=[tPS[b]])
                        S.op(PE, lambda e, gg=gg, j=j, i=i, b=b, cb2=cb2: e.matmul(
                            PS[b][:, 0:256], lhsT=AB[:, gg, i, 256 + j * 128:256 + (j + 1) * 128], rhs=stb[cb2][:, i, :],
                            start=False, stop=(i == 15)), reads=[t_AB, t_ctb[cb2]], writes=[tPS[b]])
                    fb = frc[0] % 2
                    frc[0] += 1
                    evac(frs[fb][:], PS[b][:, 0:256], tPS[b], t_frs[fb])
                    S.op(POOL, lambda e, gp=gp, gg=gg, j=j, tb8=tb8, fb=fb: e.dma_start(
                        out=frT_d[(gp * 2 + gg) * 2 + j, :, tb8 * 256:(tb8 + 1) * 256], in_=frs[fb][:]),
                        reads=[t_frs[fb]], mwrites=[t_frT_d], dma=True)
        for p in range(2):
            for gg in range(2):
                for j in range(2):
                    b = next_bank()
                    for ii in range(2):
                        i = 16 + p * 2 + ii
                        S.op(PE, lambda e, gg=gg, j=j, i=i, ii=ii, b=b: e.matmul(
                            PS[b][:, 0:256], lhsT=AB[:, gg, i, j * 128:(j + 1) * 128], rhs=ct_s[:, ii, :],
                            start=(ii == 0), stop=False), reads=[t_AB, t_dc], writes=[tPS[b]])
                        S.op(PE, lambda e, gg=gg, j=j, i=i, ii=ii, b=b: e.matmul(
                            PS[b][:, 0:256], lhsT=AB[:, gg, i, 256 + j * 128:256 + (j + 1) * 128], rhs=st_s[:, ii, :],
                            start=False, stop=(ii == 1)), reads=[t_AB, t_dc], writes=[tPS[b]])
                    fb = frc[0] % 2
                    frc[0] += 1
                    evac(frs[fb][:], PS[b][:, 0:256], tPS[b], t_frs[fb])
                    S.op(POOL, lambda e, gp=gp, gg=gg, j=j, p=p, fb=fb: e.dma_start(
                        out=frT_d[(gp * 2 + gg) * 2 + j, :, 2048 + p * 256:2048 + (p + 1) * 256], in_=frs[fb][:]),
                        reads=[t_frs[fb]], mwrites=[t_frT_d], dma=True)
    if "fr" in debug:
        dbg["fr"] = dout("dbg_fr", [8, 128, TOK], BF16)
        S.op(SP, lambda e: e.dma_start(out=dbg["fr"], in_=frT_d), reads=[t_frT_d], dma=True)
    if upto <= 4:
        S.emit()
        return nc

    S.barrier()
    A.reset(ln_mark)
    sg_d = dscr("sg_d", [2, 16, 128, TOK], BF16)
    t_sg_d = Tk()
    hmT = A.alloc([128, 8, TOK], BF16, "hmT")
    frT = A.alloc([128, 8, TOK], BF16, "frT")
    t_hf = Tk()
    for k8 in range(8):
        S.op(SP, lambda e, k8=k8: e.dma_start(out=hmT[:, k8, :], in_=hmT_d[k8]), reads=[t_hmT_d], writes=[t_hf], dma=True)
        S.op(SP, lambda e, k8=k8: e.dma_start(out=frT[:, k8, :], in_=frT_d[k8]), reads=[t_frT_d], writes=[t_hf], dma=True)
    mark_hf = A.mark()
    sgs = [A.alloc([128, 512], BF16, "sgs") for _ in range(4)]
    t_sgs = [Tk() for _ in range(4)]
    sgc = [0]
    for cbk in range(16):
        for which in range(2):
            c0 = (5136 if which == 0 else 7184) + cbk * 128
            wgx, twgx = load_w(w_in[:, c0:c0 + 128], 128)
            for tb in range(5):
                b = next_bank()
                for k in range(KC):
                    S.op(PE, lambda e, k=k, b=b, tb=tb, wgx=wgx: e.matmul(
                        PS[b][:, :], lhsT=wgx[:, k, 0:128], rhs=hT[:, k, tb * 512:(tb + 1) * 512],
                        start=(k == 0), stop=(k == KC - 1)), reads=[twgx], writes=[tPS[b]])
                sb_ = sgc[0] % 4
                sgc[0] += 1
                S.op(ACT, lambda e, b=b, sb_=sb_: e.activation(out=sgs[sb_][:], in_=PS[b][:, :], func=AF.Sigmoid),
                     writes=[tPS[b], t_sgs[sb_]])
                S.op(SP, lambda e, which=which, cbk=cbk, tb=tb, sb_=sb_: e.dma_start(
                    out=sg_d[which, cbk, :, tb * 512:(tb + 1) * 512], in_=sgs[sb_][:]),
                    reads=[t_sgs[sb_]], mwrites=[t_sg_d], dma=True)
            if ada_pending:
                ada_block(ada_pending.pop(0))
    assert not ada_pending

    S.barrier()
    mixT_d = dscr("mixT_d", [16, 128, TOK], BF16)
    t_mixT_d = Tk()
    A.reset(pers_mark)
    wout = A.alloc([128, KC, D], BF16, "wout")
    g1bc = A.alloc([128, 2, D], F32, "g1bc")
    l1g = A.alloc([128, D], F32, "l1g")
    l1b = A.alloc([128, D], F32, "l1b")
    mark_6 = A.mark()
    wbr = [[A.alloc([128, 8, 128], BF16, "wbr") for _ in range(2)] for _ in range(2)]
    assert A.mark() <= ln_mark, (A.mark(), ln_mark)
    A.reset(mark_hf)
    NSG = 3
    sga = [A.alloc([128, 512], BF16, "sga") for _ in range(NSG)]
    sgb = [A.alloc([128, 512], BF16, "sgb") for _ in range(NSG)]
    tm1 = [A.alloc([128, 512], F32, "tm1") for _ in range(2)]
    tm2 = [A.alloc([128, 512], F32, "tm2") for _ in range(2)]
    mxs = [A.alloc([128, 512], BF16, "mxs") for _ in range(2)]
    t_sga = [Tk() for _ in range(NSG)]
    t_tm1 = [Tk(), Tk()]
    t_tm2 = [Tk(), Tk()]
    t_mxs = [Tk(), Tk()]
    steps5 = [(cbk, tb) for cbk in range(16) for tb in range(5)]

    def load_sg(n):
        cbk, tb = steps5[n]
        r3 = n % NSG
        S.op(SP, lambda e: e.dma_start(out=sga[r3][:], in_=sg_d[0, cbk, :, tb * 512:(tb + 1) * 512]),
             reads=[t_sg_d], writes=[t_sga[r3]], dma=True)
        S.op(SP, lambda e: e.dma_start(out=sgb[r3][:], in_=sg_d[1, cbk, :, tb * 512:(tb + 1) * 512]),
             reads=[t_sg_d], writes=[t_sga[r3]], dma=True)

    t_wbr = [[Tk(), Tk()], [Tk(), Tk()]]

    def load_br(cbk):
        par = cbk % 2
        for which, wsrc in enumerate((w_br_m, w_br_f)):
            buf = wbr[par][which]
            v = wsrc[:, cbk * 128:(cbk + 1) * 128].rearrange("(k p) n -> p k n", p=128)
            S.op(POOL, lambda e, buf=buf, v=v: e.dma_start(out=buf[:], in_=v), writes=[t_wbr[par][which]], dma=True)

    load_sg(0)
    load_br(0)
    t_wout, t_bc = Tk(), Tk()
    for c8 in range(8):
        S.op(POOL, lambda e, c8=c8: e.dma_start(
            out=wout[:, :, c8 * 256:(c8 + 1) * 256],
            in_=w_out[:, c8 * 256:(c8 + 1) * 256].rearrange("(k p) n -> p k n", p=128)), writes=[t_wout], dma=True)
    for c in range(2):
        S.op(SP, lambda e, c=c: e.dma_start(out=g1bc[:, c, :], in_=g_scr[0, c].partition_broadcast(128)),
             reads=[t_gscr], writes=[t_bc], dma=True)
    S.op(SP, lambda e: e.dma_start(out=l1g[:], in_=ln1_g.partition_broadcast(128)), writes=[t_bc], dma=True)
    S.op(SP, lambda e: e.dma_start(out=l1b[:], in_=ln1_b.partition_broadcast(128)), writes=[t_bc], dma=True)
    wm = wf2 = twm = twf2 = None
    for n, (cbk, tb) in enumerate(steps5):
        if n + 1 < len(steps5):
            load_sg(n + 1)
        if tb == 0:
            wm, twm = wbr[cbk % 2][0], t_wbr[cbk % 2][0]
            wf2, twf2 = wbr[cbk % 2][1], t_wbr[cbk % 2][1]
            if cbk + 1 < 16:
                load_br(cbk + 1)
        r2 = n % 2
        r3 = n % NSG
        bm, bf = next_bank(), next_bank()
        for k in range(8):
            S.op(PE, lambda e, k=k, bm=bm, tb=tb, wm=wm: e.matmul(
                PS[bm][:, :], lhsT=wm[:, k, 0:128], rhs=hmT[:, k, tb * 512:(tb + 1) * 512],
                start=(k == 0), stop=(k == 7)), reads=[twm, t_hf], writes=[tPS[bm]])
        for k in range(8):
            S.op(PE, lambda e, k=k, bf=bf, tb=tb, wf2=wf2: e.matmul(
                PS[bf][:, :], lhsT=wf2[:, k, 0:128], rhs=frT[:, k, tb * 512:(tb + 1) * 512],
                start=(k == 0), stop=(k == 7)), reads=[twf2, t_hf], writes=[tPS[bf]])
        S.op(DVE, lambda e, bm=bm, r2=r2, r3=r3: e.tensor_tensor(out=tm1[r2][:], in0=PS[bm][:, :], in1=sga[r3][:],
                                                                 op=ALU.mult), reads=[t_sga[r3]], writes=[tPS[bm], t_tm1[r2]])
        S.op(DVE, lambda e, bf=bf, r2=r2, r3=r3: e.tensor_tensor(out=tm2[r2][:], in0=PS[bf][:, :], in1=sgb[r3][:],
                                                                 op=ALU.mult), reads=[t_sga[r3]], writes=[tPS[bf], t_tm2[r2]])
        S.op(DVE, lambda e, r2=r2: e.tensor_tensor(out=mxs[r2][:], in0=tm1[r2][:], in1=tm2[r2][:], op=ALU.add),
             reads=[t_tm1[r2], t_tm2[r2]], writes=[t_mxs[r2]])
        S.op(ACT, lambda e, cbk=cbk, tb=tb, r2=r2: e.dma_start(
            out=mixT_d[cbk, :, tb * 512:(tb + 1) * 512], in_=mxs[r2][:]), reads=[t_mxs[r2]], mwrites=[t_mixT_d], dma=True)
    if "mixed" in debug:
        dbg["hmall"] = dout("dbg_hmall", [8, 128, TOK], BF16)
        S.op(SP, lambda e: e.dma_start(out=dbg["hmall"], in_=hmT_d), reads=[t_hmT_d], dma=True)
        dbg["sg"] = dout("dbg_sg", [2, 16, 128, TOK], BF16)
        for w_ in range(2):
            S.op(SP, lambda e, w_=w_: e.dma_start(out=dbg["sg"][w_], in_=sg_d[w_]), reads=[t_sg_d], dma=True)
        dbg["mixed"] = dout("dbg_mixed", [16, 128, TOK], BF16)
        S.op(SP, lambda e: e.dma_start(out=dbg["mixed"], in_=mixT_d), reads=[t_mixT_d], dma=True)
    if upto <= 5:
        S.emit()
        return nc

    S.barrier()
    A.reset(mark_6)
    x1_d = dscr("x1_d", [TOK, D])
    h2T_d = dscr("h2T_d", [KC, 128, TOK], BF16)
    t_x1_d, t_h2T_d = Tk(), Tk()
    L6 = LNB()
    rrs = [A.alloc([128, D], F32, "rr") for _ in range(2)]
    x1ts = [A.alloc([128, D], F32, "x1t") for _ in range(3)]
    mxt = [A.alloc([128, KC, 128], BF16, "mxt") for _ in range(3)]
    h2s = [A.alloc([128, KC, 128], BF16, "h2s") for _ in range(2)]
    t_rrs = [Tk(), Tk()]
    t_x1ts = [Tk(), Tk(), Tk()]
    t_mxt = [Tk(), Tk(), Tk()]
    t_h2s = [Tk(), Tk()]

    def load_mx(i, b):
        S.op(SP, lambda e: e.dma_start(out=mxt[b][:], in_=mixT_d[:, :, i * 128:(i + 1) * 128].rearrange("k p t -> p k t")),
             reads=[t_mixT_d], writes=[t_mxt[b]], dma=True)

    def y1_mm(i):
        b = i % 3
        for cb4 in range(4):
            for k in range(KC):
                S.op(PE, lambda e, k=k, cb4=cb4: e.matmul(
                    PS[cb4][:, :], lhsT=mxt[b][:, k, :], rhs=wout[:, k, cb4 * 512:(cb4 + 1) * 512],
                    start=(k == 0), stop=(k == KC - 1)), reads=[t_mxt[b], t_wout], writes=[tPS[cb4]])

    L6b = LNB(with_x=False, with_xn=False)
    nmt = [A.alloc([128, 4], F32, "nmt") for _ in range(2)]
    t_nmt = [Tk(), Tk()]

    def act_norm(L, nm, t_nm, src, t_src, dst, t_dst, b):
        mvb, tmv = L.mv[b], L.t_mv[b]
        S.op(DVE, lambda e: e.tensor_scalar_mul(out=nm[:, 0:1], in0=mvb[:, 0:1], scalar1=-1.0), reads=[tmv], writes=[t_nm])
        S.op(ACT, lambda e: e.mul(out=nm[:, 1:2], in_=nm[:, 0:1], mul=mvb[:, 2:3]), reads=[tmv], writes=[t_nm])
        S.op(ACT, lambda e: e.activation(out=dst[:], in_=src[:], func=AF.Identity, bias=nm[:, 1:2], scale=mvb[:, 2:3]),
             reads=[t_src, tmv, t_nm], writes=[t_dst])

    def a1_dve(i):
        b = i % 2
        c = 0 if i < 16 else 1
        rr, t_rr = rrs[b], t_rrs[b]
        for cb4 in range(4):
            sl = slice(cb4 * 512, (cb4 + 1) * 512)
            S.op(DVE, lambda e, cb4=cb4, sl=sl: e.tensor_tensor(out=rr[:, sl], in0=PS[cb4][:, :], in1=g1bc[:, c, sl],
                                                               op=ALU.mult), reads=[t_bc], writes=[tPS[cb4], t_rr])
        S.op(DVE, lambda e: e.scalar_tensor_tensor(out=rr[:], in0=L6.xt[b][:], scalar=ALPHA, in1=rr[:],
                                                   op0=ALU.mult, op1=ALU.add), reads=[L6.t_xt[b]], writes=[t_rr])
        ln_stats_dve(L6, rr, b, t_rr, nmt[0], t_nmt[0])

    def a1_act(i):
        b = i % 2
        ln_act_norm(L6, nmt[0], t_nmt[0], rrs[b], t_rrs[b], x1ts[i % 3], t_x1ts[i % 3], b)

    def a2_pool(i):
        x1t, t_x1t = x1ts[i % 3], t_x1ts[i % 3]
        S.op(DVE, lambda e: e.tensor_tensor(out=x1t[:], in0=x1t[:], in1=l1g[:], op=ALU.mult), reads=[t_bc],
             writes=[t_x1t])

    def a2_dve(i):
        x1t, t_x1t = x1ts[i % 3], t_x1ts[i % 3]
        S.op(DVE, lambda e: e.tensor_tensor(out=x1t[:], in0=x1t[:], in1=l1b[:], op=ALU.add), reads=[t_bc],
             writes=[t_x1t])
        S.op(POOL, lambda e: e.dma_start(out=x1_d[i * 128:(i + 1) * 128, :], in_=x1t[:]), reads=[t_x1t],
             mwrites=[t_x1_d], dma=True)

    def b_dve(i):
        ln_stats_dve(L6b, x1ts[i % 3], i % 2, t_x1ts[i % 3], nmt[1], t_nmt[1])

    def b_act(i):
        b = i % 2
        ln_act_norm(L6b, nmt[1], t_nmt[1], x1ts[i % 3], t_x1ts[i % 3], L6.xn[b], L6.t_xn[b], b)

    def ln_stats_dve(L, src, b, t_src, nm, t_nm):
        st, mvb, tmv = L.stats[b], L.mv[b], L.t_mv[b]
        for c4 in range(4):
            S.op(DVE, lambda e, c4=c4: e.bn_stats(out=st[:, c4, :], in_=src[:, c4 * 512:(c4 + 1) * 512]),
                 reads=[t_src], writes=[tmv])
        S.op(DVE, lambda e: e.bn_aggr(out=mvb[:, 0:2], in_=st[:]), writes=[tmv])
        S.op(DVE, lambda e: e.tensor_scalar_add(out=mvb[:, 2:3], in0=mvb[:, 1:2], scalar1=EPS), writes=[tmv])
        S.op(DVE, lambda e: e.tensor_scalar_mul(out=nm[:, 0:1], in0=mvb[:, 0:1], scalar1=-1.0), reads=[tmv], writes=[t_nm])

    def ln_act_norm(L, nm, t_nm, src, t_src, dst, t_dst, b):
        mvb, tmv = L.mv[b], L.t_mv[b]
        S.op(ACT, lambda e: e.activation(out=mvb[:, 2:3], in_=mvb[:, 2:3], func=AF.Ln), writes=[tmv])
        S.op(ACT, lambda e: e.activation(out=mvb[:, 2:3], in_=mvb[:, 2:3], func=AF.Exp, scale=-0.5), writes=[tmv])
        S.op(ACT, lambda e: e.mul(out=nm[:, 1:2], in_=nm[:, 0:1], mul=mvb[:, 2:3]), reads=[tmv], writes=[t_nm])
        S.op(ACT, lambda e: e.activation(out=dst[:], in_=src[:], func=AF.Identity, bias=nm[:, 1:2], scale=mvb[:, 2:3]),
             reads=[t_src, tmv, t_nm], writes=[t_dst])

    def c_pe(i):
        norm_part2(L6, i % 2, 0, h2s[i % 2], t_h2s[i % 2], 3, 4, 0 if i < 16 else 1, act_evac=2, only="pe")

    def c_act(i):
        norm_part2(L6, i % 2, 0, h2s[i % 2], t_h2s[i % 2], 3, 4, 0 if i < 16 else 1, act_evac=2, only="evac")

    def c_store(i):
        b = i % 2
        S.op(POOL, lambda e: e.dma_start(out=h2T_d[:, :, i * 128:(i + 1) * 128].rearrange("k p t -> p k t"),
                                         in_=h2s[b][:]), reads=[t_h2s[b]], mwrites=[t_h2T_d], dma=True)

    ok = lambda i: 0 <= i < NT
    load_x(L6, 0, 0)
    load_x(L6, 1, 1)
    for i0 in range(3):
        load_mx(i0, i0)
    y1_mm(0)
    for t in range(-2, NT + 1):
        if ok(t - 1):
            c_pe(t - 1)
        if ok(t + 2):
            a1_dve(t + 2)
        if ok(t - 1):
            c_act(t - 1)
        if ok(t + 1):
            a2_pool(t + 1)
        if ok(t + 2):
            a1_act(t + 2)
        if ok(t):
            b_dve(t)
        if ok(t + 1):
            a2_dve(t + 1)
        if ok(t):
            b_act(t)
        if ok(t + 3):
            if t + 3 >= 2:
                load_x(L6, t + 3, (t + 3) % 2)
            if t + 3 >= 3:
                load_mx(t + 3, (t + 3) % 3)
            y1_mm(t + 3)
        if ok(t - 1):
            c_store(t - 1)
    if "x1" in debug:
        dbg["x1"] = dout("dbg_x1", [TOK, D])
        S.op(SP, lambda e: e.dma_start(out=dbg["x1"], in_=x1_d), reads=[t_x1_d], dma=True)
        dbg["h2"] = dout("dbg_h2", [KC, 128, TOK], BF16)
        S.op(SP, lambda e: e.dma_start(out=dbg["h2"], in_=h2T_d), reads=[t_h2T_d], dma=True)
    if upto <= 6:
        S.emit()
        return nc

    S.barrier()
    A.reset(work_mark)
    z_d = dscr("z_d", [NJ, 128, TOK], BF16)
    t_z_d = Tk()
    h2T = A.alloc([128, KC, TOK], BF16, "h2T")
    cvR = A.alloc([88, 4, 128], F32, "cvR")
    cvT = A.alloc([128, 4, 88], F32, "cvT")
    ub = [A.alloc([128, 2, TOK], F32, "ub") for _ in range(2)]
    cbuf = A.alloc([128, 2, TOK], F32, "cbuf")
    sgt = A.alloc([128, TOK], F32, "sgt")
    zs = [A.alloc([128, TOK], BF16, "zs") for _ in range(2)]
    t_h2T, t_cv, t_cb, t_sgt = Tk(), Tk(), Tk(), Tk()
    t_ub = [Tk(), Tk()]
    t_zs = [Tk(), Tk()]
    t_h2Tb = [Tk() for _ in range(5)]
    for tb in range(5):
        S.op(SP, lambda e, tb=tb: e.dma_start(
            out=h2T[:, :, tb * 512:(tb + 1) * 512], in_=h2T_d[:, :, tb * 512:(tb + 1) * 512].rearrange("k p t -> p k t")),
            reads=[t_h2T_d], writes=[t_h2Tb[tb]], dma=True)
    for tap in range(3):
        S.op(SP, lambda e, tap=tap: e.dma_start(out=cvR[:, tap, :], in_=w_conv[tap].rearrange("(j p) -> j p", p=128)),
             writes=[t_cv], dma=True)
    S.op(SP, lambda e: e.dma_start(out=cvR[:, 3, :], in_=b_conv.rearrange("(j p) -> j p", p=128)), writes=[t_cv], dma=True)
    for tap in range(4):
        S.op(PE, lambda e, tap=tap: e.transpose(PS[7][:, tap * 88:(tap + 1) * 88], cvR[:, tap, :], identf[0:88, 0:88]),
             reads=[t_cv, t_ident], writes=[tPS[7]])
    S.op(DVE, lambda e: e.tensor_copy(out=cvT[:].rearrange("p a b -> p (a b)"), in_=PS[7][:, 0:352]),
         writes=[tPS[7], t_cv])
    for j in range(NJ):
        ubj = ub[j % 2]
        tub = t_ub[j % 2]
        for part in range(2):
            blk = part * NJ + j
            wu, twu = load_w(w_up[:, blk * 128:(blk + 1) * 128], 128)
            for tb in range(5):
                b = next_bank()
                for k in range(KC):
                    S.op(PE, lambda e, k=k, b=b, tb=tb, wu=wu: e.matmul(
                        PS[b][:, :], lhsT=wu[:, k, 0:128], rhs=h2T[:, k, tb * 512:(tb + 1) * 512],
                        start=(k == 0), stop=(k == KC - 1)), reads=[twu, t_h2Tb[tb]], writes=[tPS[b]])
                tsl = slice(tb * 512, (tb + 1) * 512)
                S.op(ACT, lambda e, b=b, part=part, tsl=tsl, blk=blk: e.activation(
                    out=cbuf[:, part, tsl], in_=PS[b][:, :], func=AF.Identity, bias=cvT[:, 3, blk:blk + 1],
                    scale=cvT[:, 1, blk:blk + 1]), reads=[t_cv], writes=[tPS[b], t_cb])
                S.op(DVE, lambda e, b=b, part=part, tsl=tsl, ubj=ubj: e.tensor_copy(out=ubj[:, part, tsl], in_=PS[b][:, :]),
                     writes=[tPS[b], tub])
            for (st0, ln0, _c) in SEQS:
                S.op(DVE, lambda e, part=part, st0=st0, ln0=ln0, blk=blk, ubj=ubj: e.scalar_tensor_tensor(
                    out=cbuf[:, part, st0 + 1:st0 + ln0], in0=ubj[:, part, st0:st0 + ln0 - 1],
                    scalar=cvT[:, 0, blk:blk + 1], in1=cbuf[:, part, st0 + 1:st0 + ln0], op0=ALU.mult, op1=ALU.add),
                    reads=[tub, t_cv], writes=[t_cb])
                S.op(DVE, lambda e, part=part, st0=st0, ln0=ln0, blk=blk, ubj=ubj: e.scalar_tensor_tensor(
                    out=cbuf[:, part, st0:st0 + ln0 - 1], in0=ubj[:, part, st0 + 1:st0 + ln0],
                    scalar=cvT[:, 2, blk:blk + 1], in1=cbuf[:, part, st0:st0 + ln0 - 1], op0=ALU.mult, op1=ALU.add),
                    reads=[tub, t_cv], writes=[t_cb])
        S.op(ACT, lambda e: e.activation(out=sgt[:], in_=cbuf[:, 1, :], func=AF.Silu), reads=[t_cb], writes=[t_sgt])
        zb = j % 2
        S.op(DVE, lambda e, zb=zb: e.tensor_tensor(out=zs[zb][:], in0=sgt[:], in1=cbuf[:, 0, :], op=ALU.mult),
             reads=[t_sgt, t_cb], writes=[t_zs[zb]])
        S.op(SP, lambda e, j=j, zb=zb: e.dma_start(out=z_d[j], in_=zs[zb][:]), reads=[t_zs[zb]], mwrites=[t_z_d], dma=True)
    if "z" in debug:
        dbg["z"] = dout("dbg_z", [NJ, 128, TOK], BF16)
        for j in range(NJ):
            S.op(SP, lambda e, j=j: e.dma_start(out=dbg["z"][j], in_=z_d[j]), reads=[t_z_d], dma=True)
    if upto <= 7:
        S.emit()
        return nc

    S.barrier()
    A.reset(pers_mark)
    r2_d = dscr("r2_d", [TOK, D])
    t_r2_d = Tk()
    wd = A.alloc([128, NJ, 1024], BF16, "wd")
    NZB = 3
    zb_ = [A.alloc([128, 11, 512], BF16, "zb") for _ in range(NZB)]
    g2bc = A.alloc([128, 2, D], F32, "g2bc")
    x1q = [A.alloc([128, 512], F32, "x1q") for _ in range(2)]
    y2s = [A.alloc([128, 512], F32, "y2s") for _ in range(2)]
    l2g = A.alloc([128, D], F32, "l2g")
    l2b = A.alloc([128, D], F32, "l2b")
    rt = [A.alloc([128, D], F32, "rt") for _ in range(4)]
    L9s = [LNB(with_x=False, with_xn=False) for _ in range(2)]
    t_wd = [Tk() for _ in range(4)]
    t_zb = [Tk() for _ in range(NZB)]
    t_g2, t_l2 = Tk(), Tk()
    t_x1q = [Tk(), Tk()]
    t_y2s = [Tk(), Tk()]
    t_rt = [Tk() for _ in range(4)]
    ec = 0
    fc = 0
    pieces = [(half, tb, jp) for half in range(2) for tb in range(5) for jp in range(4)]

    def load_zb(n):
        _h, tb, jp = pieces[n]
        zz = n % NZB
        S.op(SP, lambda e: e.dma_start(
            out=zb_[zz][:], in_=z_d[jp * 11:(jp + 1) * 11, :, tb * 512:(tb + 1) * 512].rearrange("j p t -> p j t")),
            reads=[t_z_d], writes=[t_zb[zz]], dma=True)

    def load_wd(half, jp2):
        S.op(POOL, lambda e: e.dma_start(
            out=wd[:, jp2 * 11:(jp2 + 1) * 11, :],
            in_=w_down[jp2 * 11 * 128:(jp2 + 1) * 11 * 128, half * 1024:(half + 1) * 1024].rearrange(
                "(j p) n -> p j n", p=128)), writes=[t_wd[jp2]], dma=True)

    load_zb(0)
    load_zb(1)
    for c in range(2):
        S.op(SP, lambda e, c=c: e.dma_start(out=g2bc[:, c, :], in_=g_scr[1, c].partition_broadcast(128)),
             reads=[t_gscr], writes=[t_g2], dma=True)
    S.op(SP, lambda e: e.dma_start(out=l2g[:], in_=ln2_g.partition_broadcast(128)), writes=[t_l2], dma=True)
    S.op(SP, lambda e: e.dma_start(out=l2b[:], in_=ln2_b.partition_broadcast(128)), writes=[t_l2], dma=True)
    for n, (half, tb, jp) in enumerate(pieces):
        if n == 0:
            for jp2 in range(4):
                load_wd(0, jp2)
        if n + 2 < len(pieces):
            load_zb(n + 2)
        zz = n % NZB
        for ii in range(4):
            for qq in range(2):
                bk = ii * 2 + qq
                for jl in range(11):
                    jg = jp * 11 + jl
                    S.op(PE, lambda e, zz=zz, ii=ii, qq=qq, jl=jl, jg=jg, bk=bk: e.matmul(
                        PS[bk][:, :], lhsT=zb_[zz][:, jl, ii * 128:(ii + 1) * 128],
                        rhs=wd[:, jg, qq * 512:(qq + 1) * 512], start=(jg == 0), stop=(jg == NJ - 1)),
                        reads=[t_zb[zz], t_wd[jp]], writes=[tPS[bk]])
        if half == 0 and tb == 4:
            load_wd(1, jp)
        if jp == 3:
            for ii in range(4):
                i = tb * 4 + ii
                c = 0 if i < 16 else 1
                for qq in range(2):
                    bk = ii * 2 + qq
                    e2 = ec % 2
                    ec += 1
                    csl = slice(half * 1024 + qq * 512, half * 1024 + (qq + 1) * 512)
                    S.op(SP, lambda e, i=i, csl=csl, e2=e2: e.dma_start(out=x1q[e2][:], in_=x1_d[i * 128:(i + 1) * 128, csl]),
                         reads=[t_x1_d], writes=[t_x1q[e2]], dma=True)
                    S.op(DVE, lambda e, bk=bk, c=c, csl=csl, e2=e2: e.tensor_tensor(
                        out=y2s[e2][:], in0=PS[bk][:, :], in1=g2bc[:, c, csl], op=ALU.mult), reads=[t_g2],
                        writes=[tPS[bk], t_y2s[e2]])
                    S.op(DVE, lambda e, e2=e2: e.scalar_tensor_tensor(out=y2s[e2][:], in0=x1q[e2][:], scalar=ALPHA,
                                                                      in1=y2s[e2][:], op0=ALU.mult, op1=ALU.add),
                         reads=[t_x1q[e2]], writes=[t_y2s[e2]])
                    S.op(SP, lambda e, i=i, csl=csl, e2=e2: e.dma_start(out=r2_d[i * 128:(i + 1) * 128, csl], in_=y2s[e2][:]),
                         reads=[t_y2s[e2]], mwrites=[t_r2_d], dma=True)
            if half == 1:
                tl = []
                for ii in range(4):
                    tl.append((tb * 4 + ii, fc % 4, L9s[(fc // 2) % 2], fc % 2))
                    fc += 1
                for (i, b, L9, lb) in tl:
                    S.op(ACT, lambda e, i=i, b=b: e.dma_start(out=rt[b][:], in_=r2_d[i * 128:(i + 1) * 128, :]),
                         reads=[t_r2_d], writes=[t_rt[b]], dma=True)
                for (i, b, L9, lb) in tl:
                    ln_stats(L9, rt[b], lb, t_rt[b])
                for (i, b, L9, lb) in tl:
                    S.op(DVE, lambda e, b=b, L9=L9, lb=lb: e.tensor_scalar(
                        out=rt[b][:], in0=rt[b][:], scalar1=L9.mv[lb][:, 0:1], scalar2=L9.mv[lb][:, 2:3],
                        op0=ALU.subtract, op1=ALU.mult), reads=[L9.t_mv[lb]], writes=[t_rt[b]])
                for (i, b, L9, lb) in tl:
                    S.op(POOL, lambda e, b=b: e.tensor_tensor(out=rt[b][:], in0=rt[b][:], in1=l2g[:], op=ALU.mult),
                         reads=[t_l2], writes=[t_rt[b]])
                    S.op(POOL, lambda e, b=b: e.tensor_tensor(out=rt[b][:], in0=rt[b][:], in1=l2b[:], op=ALU.add),
                         reads=[t_l2], writes=[t_rt[b]])
                    S.op(POOL, lambda e, i=i, b=b: e.dma_start(out=y_rows(i), in_=rt[b][:]), reads=[t_rt[b]], dma=True)
    S.emit()
    return nc


def _consts():
    f64 = np.float64
    t = np.arange(2048)
    r = (t // 64).astype(np.float32)[:, None]
    col = (t % 64).astype(np.float32)[:, None]
    quarter = D // 4
    freq = (1.0 / (10000.0 ** (np.arange(quarter, dtype=np.float32) / np.float32(quarter)))).astype(np.float32)
    er, ec = r * freq, col * freq
    posemb = np.concatenate([np.sin(er), np.cos(er), np.sin(ec), np.cos(ec)], -1).astype(np.float32)
    s = np.arange(128)
    maskF = (s[:, None] <= s[None, :]).astype(np.float32)
    maskB = (s[:, None] >= s[None, :]).astype(np.float32)
    c = np.arange(256, dtype=f64)
    ang = 2 * np.pi * np.outer(c, c) / 256.0
    dftc = np.concatenate([np.cos(ang), np.sin(ang)], 1).astype(ml_dtypes.bfloat16)

    def seq_tables(T):
        tt = np.arange(T, dtype=f64)
        a = 2 * np.pi * (np.outer(tt, tt) % T) / T
        nrm = 1.0 / np.sqrt(T * 256.0)
        return (np.cos(a) * nrm).astype(ml_dtypes.bfloat16), (-np.sin(a) * nrm).astype(ml_dtypes.bfloat16)

    ct2048, st2048 = seq_tables(2048)
    ct256, st256 = seq_tables(256)
    return dict(posemb=posemb, maskF=maskF, maskB=maskB, dftc=dftc, ct2048=ct2048, st2048=st2048,
                ct256=ct256, st256=st256)


def make_in_maps(inputs, n=8):
    cs = _consts()
    f = lambda a: np.ascontiguousarray(np.asarray(a, dtype=np.float32))
    shared = {k: f(inputs[k][0]) for k in ["w_ada", "b_ada", "w_in", "b_gate", "w_hnorm", "w_br_m", "w_br_f", "w_out",
                                           "ln1_g", "ln1_b", "w_up", "w_conv", "b_conv", "w_down", "ln2_g", "ln2_b"]}
    shared.update(cs)
    maps = []
    for i in range(n):
        m = dict(shared)
        m["xs"] = f(inputs["x_sample"][i])
        m["xp"] = f(inputs["x_prompt"][2 * i:2 * i + 2]).reshape(512, D)
        m["cond"] = np.stack([f(inputs["c"][i]), f(inputs["c_ctx"])], 0)
        m["sC"] = f(inputs["state_C"][i, 0])
        m["sn"] = f(inputs["state_n"][i, 0])
        m["sm"] = f(inputs["state_m"][i, 0]).reshape(8)
        maps.append(m)
    return maps


def kernel(**inputs):
    nc = build_program()
    maps = make_in_maps(inputs)
    res = run_bass_kernel_spmd(nc, maps, core_ids=list(range(8)))
    R = res.results
    y_p = np.concatenate([r["y_p"].reshape(2, 256, D) for r in R], 0)
    y_s = np.stack([r["y_s"] for r in R], 0)
    o_C = np.concatenate([r["o_C"] for r in R], 0)[:, None]
    o_n = np.concatenate([r["o_n"] for r in R], 0)[:, None]
    o_m = np.concatenate([r["o_m"].reshape(2, 2, NH) for r in R], 0)[:, None]
    return (y_p.astype(np.float32), y_s.astype(np.float32), o_C.astype(np.float32), o_n.astype(np.float32),
            o_m.astype(np.float32))
```

```python
import numpy as np
import ml_dtypes
import concourse.bass as bass
import concourse.mybir as mybir
from concourse.bass_utils import run_bass_kernel_spmd

F32 = mybir.dt.float32
BF16 = mybir.dt.bfloat16
AF = mybir.ActivationFunctionType
ALU = mybir.AluOpType
AX = mybir.AxisListType

PE, ACT, DVE, POOL, SP = "pe", "act", "dve", "pool", "sp"
N_DMA_SEMS = 8

D = 2048
KC = 16
NT = 20
TOK = 2560
DH = 256
NH = 4
DFF = 5632
NJ = 44
D_IN = 9232
ALPHA = 2.0 ** 0.25
EPS = 1e-5
SEQS = [(0, 2048, 0), (2048, 256, 1), (2304, 256, 1)]


class Tk:
    __slots__ = ("w", "r", "ws")

    def __init__(self):
        self.w = None
        self.r = []
        self.ws = []


class Op:
    __slots__ = ("eng", "fn", "deps", "is_dma", "needs_inc", "idx", "dslot", "dgen")

    def __init__(self, eng, fn, is_dma):
        self.eng = eng
        self.fn = fn
        self.deps = []
        self.is_dma = is_dma
        self.needs_inc = False
        self.idx = None
        self.dslot = None
        self.dgen = None


class Sched:
    def __init__(self, nc):
        self.nc = nc
        self.ops = {PE: [], ACT: [], DVE: [], POOL: [], SP: []}
        self.ndma = {ACT: 0, POOL: 0, SP: 0}
        self.dma_hist = {ACT: [], POOL: [], SP: []}
        self.last = {PE: None, ACT: None, DVE: None, POOL: None}
        self.bar = {}

    def op(self, eng, fn, reads=(), writes=(), dma=False, mwrites=()):
        o = Op(eng, fn, dma)
        deps = []
        for t in reads:
            if t.w is not None:
                deps.append(t.w)
            deps.extend(t.ws)
        for t in writes:
            if t.w is not None:
                deps.append(t.w)
            deps.extend(t.r)
            deps.extend(t.ws)
        for t in mwrites:
            if t.w is not None:
                deps.append(t.w)
            deps.extend(t.r)
        if eng in self.bar:
            deps.extend(self.bar.pop(eng))
        if dma:
            k = self.ndma[eng]
            o.dslot = k % N_DMA_SEMS
            o.dgen = k // N_DMA_SEMS
            self.ndma[eng] = k + 1
            hist = self.dma_hist[eng]
            if k >= N_DMA_SEMS:
                deps.append(hist[k - N_DMA_SEMS])
            hist.append(o)
        else:
            self.last[eng] = o
        seen = set()
        for d in deps:
            if d is o or id(d) in seen:
                continue
            seen.add(id(d))
            if (not d.is_dma) and (not dma) and d.eng == PE and eng == PE:
                continue
            o.deps.append(d)
            if not d.is_dma:
                d.needs_inc = True
        for t in reads:
            t.r.append(o)
        for t in writes:
            t.w = o
            t.r = []
            t.ws = []
        for t in mwrites:
            t.ws.append(o)
        self.ops[eng].append(o)
        return o

    def barrier(self):
        pend = [o for o in self.last.values() if o is not None]
        for e in self.dma_hist:
            pend.extend(self.dma_hist[e][-N_DMA_SEMS:])
        for e in self.ops:
            self.bar[e] = list(pend)

    def emit(self):
        nc = self.nc
        from contextlib import ExitStack
        with ExitStack() as es:
            esem = {e: es.enter_context(nc.semaphore("s_" + e)) for e in (PE, ACT, DVE, POOL)}
            dsem = {e: [es.enter_context(nc.semaphore("d_%s%d" % (e, i))) for i in range(N_DMA_SEMS)]
                    for e in (ACT, POOL, SP)}
            for e in (PE, ACT, DVE, POOL):
                c = 0
                for o in self.ops[e]:
                    if (not o.is_dma) and o.needs_inc:
                        c += 1
                        o.idx = c
            block = es.enter_context(nc.Block())

            def run(e, engh):
                waited = {}
                for o in self.ops[e]:
                    for d in o.deps:
                        if d.is_dma:
                            sem = dsem[d.eng][d.dslot]
                            val = 16 * (d.dgen + 1)
                        else:
                            sem = esem[d.eng]
                            val = d.idx
                        key = id(sem)
                        if waited.get(key, 0) >= val:
                            continue
                        waited[key] = val
                        engh.wait_ge(sem, val)
                    ins = o.fn(engh)
                    if o.is_dma:
                        ins.then_inc(dsem[e][o.dslot], 16)
                    elif o.needs_inc:
                        ins.then_inc(esem[e], 1)
                if e in self.dma_hist:
                    for o in self.dma_hist[e][-N_DMA_SEMS:]:
                        engh.wait_ge(dsem[e][o.dslot], 16 * (o.dgen + 1))

            @block.tensor
            def _(eng):
                run(PE, eng)

            @block.scalar
            def _(eng):
                run(ACT, eng)

            @block.vector
            def _(eng):
                run(DVE, eng)

            @block.gpsimd
            def _(eng):
                run(POOL, eng)

            @block.sync
            def _(eng):
                run(SP, eng)


class Arena:
    def __init__(self, nc, base, limit):
        self.nc = nc
        self.base = base
        self.limit = limit
        self.cur = base
        self.n = 0

    def mark(self):
        return self.cur

    def reset(self, m):
        self.cur = m

    def alloc(self, shape, dtype, name=None):
        esz = 4 if dtype == F32 else 2
        free = 1
        for s in shape[1:]:
            free *= s
        nbytes = (free * esz + 63) // 64 * 64
        off = self.cur
        assert off + nbytes <= self.limit, ("SBUF arena overflow", name, off, nbytes, self.limit)
        self.cur = off + nbytes
        self.n += 1
        return self.nc.alloc_sbuf_tensor_at("%s_%d" % (name or "t", self.n), list(shape), dtype, offset=off)


def build_program(upto=99, debug=()):
    nc = bass.Bass("TRN2", target_bir_lowering=False)
    S = Sched(nc)

    def din(name, shape, dt=F32):
        return nc.dram_tensor(name, list(shape), dt, kind="ExternalInput").ap()

    def dout(name, shape, dt=F32):
        return nc.dram_tensor(name, list(shape), dt, kind="ExternalOutput").ap()

    def dscr(name, shape, dt=F32):
        return nc.dram_tensor(name, list(shape), dt).ap()

    xs = din("xs", [2048, D])
    xp = din("xp", [512, D])
    cond = din("cond", [2, D])
    sC = din("sC", [2, NH, DH, DH])
    sn = din("sn", [2, NH, DH])
    sm = din("sm", [8])
    w_ada = din("w_ada", [D, 6 * D])
    b_ada = din("b_ada", [6 * D])
    w_in = din("w_in", [D, D_IN])
    b_gate = din("b_gate", [16])
    w_hnorm = din("w_hnorm", [1024])
    w_br_m = din("w_br_m", [1024, D])
    w_br_f = din("w_br_f", [1024, D])
    w_out = din("w_out", [D, D])
    ln1_g = din("ln1_g", [D])
    ln1_b = din("ln1_b", [D])
    w_up = din("w_up", [D, 2 * DFF])
    w_conv = din("w_conv", [3, 2 * DFF])
    b_conv = din("b_conv", [2 * DFF])
    w_down = din("w_down", [DFF, D])
    ln2_g = din("ln2_g", [D])
    ln2_b = din("ln2_b", [D])
    posemb = din("posemb", [2048, D])
    maskF_d = din("maskF", [128, 128])
    maskB_d = din("maskB", [128, 128])
    dftc_d = din("dftc", [256, 512], BF16)
    ct2048 = din("ct2048", [2048, 2048], BF16)
    st2048 = din("st2048", [2048, 2048], BF16)
    ct256 = din("ct256", [256, 256], BF16)
    st256 = din("st256", [256, 256], BF16)

    y_s = dout("y_s", [2048, D])
    y_p = dout("y_p", [512, D])
    o_C = dout("o_C", [2, 2, NH, DH, DH])
    o_n = dout("o_n", [2, 2, NH, DH])
    o_m = dout("o_m", [2, 8])
    dbg = {}

    def x_rows(i):
        return xs[i * 128:(i + 1) * 128, :] if i < 16 else xp[(i - 16) * 128:(i - 15) * 128, :]

    def y_rows(i):
        return y_s[i * 128:(i + 1) * 128, :] if i < 16 else y_p[(i - 16) * 128:(i - 15) * 128, :]

    A = Arena(nc, 16512, 229376)
    ident = A.alloc([128, 128], BF16, "ident")
    identf = A.alloc([128, 128], F32, "identf")
    modT = A.alloc([128, 96, 2], F32, "modT")
    t_modT = Tk()
    t_ident = Tk()
    condB = A.alloc([128, 2, 16], BF16, "condB")
    badaT = A.alloc([128, 96], F32, "badaT")
    gstage = A.alloc([1, 512], F32, "gstage")
    gbrow = A.alloc([1, 256], F32, "gbrow")
    pers_mark = A.mark()

    PS = [nc.alloc_psum_tensor("ps%d" % i, [128, 512], F32) for i in range(8)]
    tPS = [Tk() for _ in range(8)]
    PSB = [PS[6][:].bitcast(BF16), PS[7][:].bitcast(BF16)]
    tPSB = [tPS[6], tPS[7]]

    S.op(POOL, lambda e: e.memset(identf[:], 1.0), writes=[t_ident])
    S.op(POOL, lambda e: e.affine_select(out=identf[:], in_=identf[:], pattern=[[-1, 128]],
                                         compare_op=ALU.is_equal, fill=0.0, base=0, channel_multiplier=1),
         reads=[t_ident], writes=[t_ident])
    S.op(DVE, lambda e: e.tensor_copy(out=ident[:], in_=identf[:]), reads=[t_ident], writes=[t_ident])

    NWB = 3
    wbufs = [A.alloc([128, KC, 256], BF16, "wbuf") for _ in range(NWB)]
    t_wb = [Tk() for _ in range(NWB)]
    wctr = [0]

    def load_w(src2d, ncols, kc=KC, rows_pk=False):
        b = wctr[0] % len(wbufs)
        wctr[0] += 1
        if rows_pk:
            v = src2d.rearrange("(p k) n -> p k n", k=kc)
        else:
            v = src2d.rearrange("(k p) n -> p k n", p=128)
        buf = wbufs[b]
        S.op(POOL, lambda e: e.dma_start(out=buf[:, 0:kc, 0:ncols], in_=v), writes=[t_wb[b]], dma=True)
        return buf, t_wb[b]

    work_mark = A.mark()

    pbank = [0]

    def next_bank(lo=0, hi=8):
        b = lo + pbank[0] % (hi - lo)
        pbank[0] += 1
        return b

    hT = A.alloc([128, KC, TOK], BF16, "hT")
    t_hT = [Tk() for _ in range(NT)]
    ln_mark = A.mark()
    condS = A.alloc([128, 2, 16], F32, "condS")
    badaR = A.alloc([96, 128], F32, "badaR")
    t_cond, t_condB, t_badaR, t_badaT, t_gst = [Tk() for _ in range(5)]
    g_scr = dscr("g_scr", [2, 2, D])
    t_gscr = Tk()

    for c in range(2):
        S.op(SP, lambda e, c=c: e.dma_start(out=condS[:, c, :], in_=cond[c].rearrange("(p k) -> p k", k=16)),
             writes=[t_cond], dma=True)
    S.op(ACT, lambda e: e.activation(out=condB[:], in_=condS[:], func=AF.Silu), reads=[t_cond], writes=[t_condB])
    S.op(SP, lambda e: e.dma_start(out=badaR[:], in_=b_ada.rearrange("(j p) -> j p", p=128)),
         writes=[t_badaR], dma=True)
    S.op(PE, lambda e: e.transpose(PS[1][:, 0:96], badaR[:], identf[0:96, 0:96]),
         reads=[t_badaR, t_ident], writes=[tPS[1]])
    S.op(ACT, lambda e: e.copy(out=badaT[:], in_=PS[1][:, 0:96]), writes=[tPS[1], t_badaT])
    for sec in (1, 4):
        S.op(DVE, lambda e, sec=sec: e.tensor_scalar_add(out=badaT[:, sec * 16:(sec + 1) * 16],
                                                         in0=badaT[:, sec * 16:(sec + 1) * 16], scalar1=1.0),
             writes=[t_badaT])

    def ada_block(blk):
        sec = blk // 8
        wb, twb = load_w(w_ada[:, blk * 256:(blk + 1) * 256], 256, rows_pk=True)
        b = next_bank()
        if sec in (2, 5):
            gi = 0 if sec == 2 else 1
            c0 = (blk % 8) * 256
            S.op(SP, lambda e: e.dma_start(out=gbrow[0:1, :], in_=b_ada[sec * D + c0:sec * D + c0 + 256].partition_broadcast(1)),
                 writes=[t_gst], dma=True)
            for c in range(2):
                for k in range(16):
                    S.op(PE, lambda e, c=c, k=k: e.matmul(
                        PS[b][0:1, c * 256:(c + 1) * 256], lhsT=condB[:, c, k:k + 1], rhs=wb[:, k, 0:256],
                        start=(k == 0), stop=(k == 15)), reads=[t_condB, twb], writes=[tPS[b]])
            for c in range(2):
                S.op(DVE, lambda e, c=c: e.tensor_tensor(
                    out=gstage[0:1, c * 256:(c + 1) * 256], in0=PS[b][0:1, c * 256:(c + 1) * 256], in1=gbrow[0:1, :],
                    op=ALU.add), writes=[tPS[b], t_gst])
            for c in range(2):
                S.op(SP, lambda e, c=c: e.dma_start(
                    out=g_scr[gi, c:c + 1, c0:c0 + 256], in_=gstage[0:1, c * 256:(c + 1) * 256]),
                    reads=[t_gst], mwrites=[t_gscr], dma=True)
        else:
            jj0 = blk * 2
            for j in range(2):
                for k in range(16):
                    S.op(PE, lambda e, j=j, k=k: e.matmul(
                        PS[b][:, j * 2:j * 2 + 2], lhsT=wb[:, k, j * 128:(j + 1) * 128], rhs=condB[:, :, k],
                        start=(k == 0), stop=(k == 15)), reads=[t_condB, twb], writes=[tPS[b]])
            for c in range(2):
                S.op(DVE, lambda e, c=c: e.tensor_tensor(
                    out=modT[:, jj0:jj0 + 2, c], in0=PS[b][:, 0:4].rearrange("p (j c) -> p j c", c=2)[:, :, c],
                    in1=badaT[:, jj0:jj0 + 2], op=ALU.add), reads=[t_badaT], writes=[tPS[b], t_modT])

    for blk in range(16):
        ada_block(blk)
    ada_pending = list(range(16, 48))

    class LNB:
        def __init__(self, with_x=True, with_xn=True):
            if with_x:
                self.xt = [A.alloc([128, D], F32, "xt") for _ in range(2)]
                self.pt = [A.alloc([128, D], F32, "pt") for _ in range(2)]
                self.t_xt = [Tk(), Tk()]
                self.t_pt = [Tk(), Tk()]
            if with_xn:
                self.xn = [A.alloc([128, D], BF16, "xn") for _ in range(2)]
                self.t_xn = [Tk(), Tk()]
            self.stats = [A.alloc([128, 4, 6], F32, "stats") for _ in range(2)]
            self.mv = [A.alloc([128, 4], F32, "mv") for _ in range(2)]
            self.t_mv = [Tk(), Tk()]

    def ln_stats(L, src, b, t_src):
        st, mvb, tmv = L.stats[b], L.mv[b], L.t_mv[b]
        for c4 in range(4):
            S.op(DVE, lambda e, c4=c4: e.bn_stats(out=st[:, c4, :], in_=src[:, c4 * 512:(c4 + 1) * 512]),
                 reads=[t_src], writes=[tmv])
        S.op(DVE, lambda e: e.bn_aggr(out=mvb[:, 0:2], in_=st[:]), writes=[tmv])
        S.op(DVE, lambda e: e.tensor_scalar_add(out=mvb[:, 2:3], in0=mvb[:, 1:2], scalar1=EPS), writes=[tmv])
        S.op(ACT, lambda e: e.activation(out=mvb[:, 2:3], in_=mvb[:, 2:3], func=AF.Ln), writes=[tmv])
        S.op(ACT, lambda e: e.activation(out=mvb[:, 2:3], in_=mvb[:, 2:3], func=AF.Exp, scale=-0.5), writes=[tmv])

    def norm_mod_T(L, src, b, t_src, i, dstT, t_dst, sec_sh, sec_sc, c):
        norm_part1(L, src, b, t_src)
        norm_part2(L, b, i, dstT, t_dst, sec_sh, sec_sc, c)

    def norm_part1(L, src, b, t_src):
        ln_stats(L, src, b, t_src)
        xnb, txn, mvb, tmv = L.xn[b], L.t_xn[b], L.mv[b], L.t_mv[b]
        S.op(DVE, lambda e: e.tensor_scalar(out=xnb[:], in0=src[:], scalar1=mvb[:, 0:1], scalar2=mvb[:, 2:3],
                                            op0=ALU.subtract, op1=ALU.mult), reads=[t_src, tmv], writes=[txn])

    def norm_part2(L, b, i, dstT, t_dst, sec_sh, sec_sc, c, act_evac=False, only=None):
        xnb, txn = L.xn[b], L.t_xn[b]
        for hb in range(2):
            for q in range(8):
                k = hb * 8 + q
                if only == "evac":
                    break
                S.op(PE, lambda e, k=k, q=q, hb=hb: e.transpose(
                    PSB[hb][:, q * 128:(q + 1) * 128], xnb[:, k * 128:(k + 1) * 128], ident[:]),
                    reads=[txn, t_ident], writes=[tPSB[hb]])
            for q in range(8):
                if only == "pe":
                    break
                k = hb * 8 + q
                src_ps = PSB[hb][:, q * 128:(q + 1) * 128]
                dst = dstT[:, k, i * 128:(i + 1) * 128]
                sc_ap = modT[:, sec_sc * 16 + k, c:c + 1]
                sh_ap = modT[:, sec_sh * 16 + k, c:c + 1]
                if act_evac == 2 or (act_evac and q % 2 == 0):
                    S.op(ACT, lambda e, src_ps=src_ps, dst=dst, sc_ap=sc_ap, sh_ap=sh_ap: e.activation(
                        out=dst, in_=src_ps, func=AF.Identity, bias=sh_ap, scale=sc_ap),
                        reads=[t_modT], writes=[tPSB[hb], t_dst])
                else:
                    S.op(DVE, lambda e, src_ps=src_ps, dst=dst, sc_ap=sc_ap, sh_ap=sh_ap: e.tensor_scalar(
                        out=dst, in0=src_ps, scalar1=sc_ap, scalar2=sh_ap, op0=ALU.mult, op1=ALU.add),
                        reads=[t_modT], writes=[tPSB[hb], t_dst])

    def load_x(L, i, b):
        xtb, ptb, txt, tpt = L.xt[b], L.pt[b], L.t_xt[b], L.t_pt[b]
        S.op(SP, lambda e: e.dma_start(out=xtb[:], in_=x_rows(i)), writes=[txt], dma=True)
        if i < 16:
            S.op(SP, lambda e: e.dma_start(out=ptb[:], in_=posemb[i * 128:(i + 1) * 128, :]), writes=[tpt], dma=True)
            S.op(POOL, lambda e: e.tensor_tensor(out=xtb[:], in0=xtb[:], in1=ptb[:], op=ALU.add),
                 reads=[tpt], writes=[txt])

    L1 = LNB()

    load_x(L1, 0, 0)
    load_x(L1, 1, 1)
    norm_part1(L1, L1.xt[0], 0, L1.t_xt[0])
    for i in range(NT):
        if i + 1 < NT:
            norm_part1(L1, L1.xt[(i + 1) % 2], (i + 1) % 2, L1.t_xt[(i + 1) % 2])
        if i + 2 < NT:
            load_x(L1, i + 2, i % 2)
        norm_part2(L1, i % 2, i, hT, t_hT[i], 0, 1, 0 if i < 16 else 1)
    if "modT" in debug:
        dbg["modT"] = dout("dbg_modT", [128, 192])
        S.op(SP, lambda e: e.dma_start(out=dbg["modT"], in_=modT[:].rearrange("p j c -> p (j c)")),
             reads=[t_modT], dma=True)
    if "hT" in debug:
        dbg["hT"] = dout("dbg_hT", [128, KC, TOK], BF16)
        for k in range(KC):
            S.op(SP, lambda e, k=k: e.dma_start(out=dbg["hT"][:, k, :], in_=hT[:, k, :]), reads=t_hT, dma=True)
    if upto <= 1:
        S.emit()
        return nc

    S.barrier()
    A.reset(ln_mark)
    maskF = A.alloc([128, 128], F32, "maskF")
    maskB = A.alloc([128, 128], F32, "maskB")
    U = A.alloc([128, 8, 20], F32, "U")
    WI = A.alloc([128, 8, 20], F32, "WI")
    CL = A.alloc([128, 8, 20], F32, "CL")
    gate_mark = A.mark()
    ones = A.alloc([128, 128], F32, "ones")
    bgate_bc = A.alloc([128, 16], F32, "bgate")
    GT = A.alloc([128, 16, 20], F32, "GT")
    LF = A.alloc([128, 8, 20], F32, "LF")
    Bc = A.alloc([128, 8, 20], F32, "Bc")
    BL = A.alloc([128, 8, 20], F32, "BL")
    A_ = A.alloc([128, 8, 20], F32, "A_")
    AMX = A.alloc([128, 8, 20], F32, "AMX")
    MX = A.alloc([128, 8, 20], F32, "MX")
    WIL = A.alloc([128, 8, 20], F32, "WIL")
    AM = A.alloc([80, 2], F32, "AM")
    D1 = A.alloc([80, 2, 80], F32, "D1")
    MS = A.alloc([128, 3, 8], F32, "MS")
    t_c, t_g, t_ms = Tk(), Tk(), Tk()
    t_rec = [Tk(), Tk()]
    S.op(SP, lambda e: e.dma_start(out=maskF[:], in_=maskF_d), writes=[t_c], dma=True)
    S.op(SP, lambda e: e.dma_start(out=maskB[:], in_=maskB_d), writes=[t_c], dma=True)
    S.op(SP, lambda e: e.dma_start(out=bgate_bc[:], in_=b_gate.partition_broadcast(128)), writes=[t_c], dma=True)
    S.op(POOL, lambda e: e.memset(ones[:], 1.0), writes=[t_c])
    S.op(SP, lambda e: e.dma_start(out=MS[:, 0, :], in_=sm.partition_broadcast(128)), writes=[t_ms], dma=True)
    S.op(POOL, lambda e: e.memset(MS[:, 1:3, :], 0.0), writes=[t_ms])

    wg, twg = load_w(w_in[:, 4096:4112], 16)
    for i in range(NT):
        for k in range(KC):
            S.op(PE, lambda e, i=i, k=k: e.matmul(PS[0][:, i * 16:(i + 1) * 16], lhsT=hT[:, k, i * 128:(i + 1) * 128],
                                                  rhs=wg[:, k, 0:16], start=(k == 0), stop=(k == KC - 1)),
                 reads=[twg], writes=[tPS[0]])
    S.op(DVE, lambda e: e.tensor_tensor(out=GT[:], in0=PS[0][:, 0:320].rearrange("p (i g) -> p g i", g=16),
                                        in1=bgate_bc[:, :].unsqueeze(2).to_broadcast([128, 16, 20]), op=ALU.add),
         reads=[t_c], writes=[tPS[0], t_g])
    GT4 = GT[:].rearrange("p (d k h) i -> p d k h i", d=2, k=2)
    for d in range(2):
        S.op(ACT, lambda e, d=d: e.activation(out=LF[:, d * 4:(d + 1) * 4, :], in_=GT4[:, d, 1], func=AF.Exp,
                                              scale=-1.0), reads=[t_g], writes=[t_g])
    S.op(ACT, lambda e: e.activation(out=LF[:], in_=LF[:], func=AF.Ln, bias=1.0), reads=[t_g], writes=[t_g])
    S.op(DVE, lambda e: e.tensor_scalar_mul(out=LF[:], in0=LF[:], scalar1=-1.0), reads=[t_g], writes=[t_g])
    LFf = LF[:].rearrange("p a b -> p (a b)")
    S.op(PE, lambda e: e.matmul(PS[1][:, 0:80], lhsT=maskF[:], rhs=LFf[:, 0:80], start=True, stop=True),
         reads=[t_c, t_g], writes=[tPS[1]])
    S.op(PE, lambda e: e.matmul(PS[1][:, 80:160], lhsT=maskB[:], rhs=LFf[:, 80:160], start=True, stop=True),
         reads=[t_c, t_g], writes=[tPS[1]])
    S.op(PE, lambda e: e.matmul(PS[1][:, 160:320], lhsT=ones[:], rhs=LFf, start=True, stop=True),
         reads=[t_c, t_g], writes=[tPS[1]])
    S.op(DVE, lambda e: e.tensor_copy(out=Bc[:].rearrange("p a b -> p (a b)"), in_=PS[1][:, 0:160]),
         writes=[tPS[1], t_g])
    S.op(DVE, lambda e: e.tensor_copy(out=BL[:].rearrange("p a b -> p (a b)"), in_=PS[1][:, 160:320]),
         writes=[tPS[1], t_g])
    for d in range(2):
        S.op(DVE, lambda e, d=d: e.tensor_tensor(out=A_[:, d * 4:(d + 1) * 4, :], in0=GT4[:, d, 0],
                                                 in1=Bc[:, d * 4:(d + 1) * 4, :], op=ALU.subtract),
             reads=[t_g], writes=[t_g])
    Af = A_[:].rearrange("p a b -> p (a b)")
    for half in range(2):
        S.op(PE, lambda e, half=half: e.transpose(PS[2][0:80, half * 128:(half + 1) * 128],
                                                  Af[:, half * 80:(half + 1) * 80], identf[:]),
             reads=[t_g, t_ident], writes=[tPS[2]])
    for half in range(2):
        S.op(DVE, lambda e, half=half: e.reduce_max(out=AM[:, half:half + 1],
                                                    in_=PS[2][0:80, half * 128:(half + 1) * 128], axis=AX.X),
             writes=[tPS[2], t_g])
        S.op(DVE, lambda e, half=half: e.tensor_scalar_mul(out=D1[:, half, :], in0=identf[0:80, 0:80],
                                                           scalar1=AM[:, half:half + 1]),
             reads=[t_ident, t_g], writes=[t_g])
        S.op(PE, lambda e, half=half: e.matmul(PS[3][:, half * 80:(half + 1) * 80], lhsT=ones[0:80, :],
                                               rhs=D1[:, half, :], start=True, stop=True),
             reads=[t_c, t_g], writes=[tPS[3]])
    S.op(DVE, lambda e: e.tensor_copy(out=AMX[:].rearrange("p a b -> p (a b)"), in_=PS[3][:, 0:160]),
         writes=[tPS[3], t_g])
    for sq, (st0, ln0, _c) in enumerate(SEQS):
        i0, n = st0 // 128, ln0 // 128
        for j in range(n):
            for d in range(2):
                i = i0 + j if d == 0 else i0 + n - 1 - j
                eng = DVE
                sl = slice(d * 4, (d + 1) * 4)
                mp = MS[:, sq, sl]
                S.op(eng, lambda e, mp=mp, sl=sl, i=i: e.tensor_tensor(out=MX[:, sl, i], in0=mp, in1=AMX[:, sl, i],
                                                                       op=ALU.max),
                     reads=[t_g, t_ms], writes=[t_rec[d]])
                S.op(eng, lambda e, mp=mp, sl=sl, i=i: e.tensor_tensor(out=WIL[:, sl, i], in0=mp, in1=MX[:, sl, i],
                                                                       op=ALU.subtract),
                     reads=[t_g, t_ms], writes=[t_rec[d]])
                S.op(eng, lambda e, mp=mp, sl=sl, i=i: e.tensor_tensor(out=mp, in0=BL[:, sl, i], in1=MX[:, sl, i],
                                                                       op=ALU.add),
                     reads=[t_g, t_ms], writes=[t_rec[d]])
    for p in range(2):
        S.op(SP, lambda e, p=p: e.dma_start(out=o_m[p:p + 1, :], in_=MS[0:1, 1 + p, :]), reads=t_rec, dma=True)
    S.op(DVE, lambda e: e.tensor_tensor(out=U[:], in0=A_[:], in1=MX[:], op=ALU.subtract),
         reads=[t_g] + t_rec, writes=[t_g])
    S.op(ACT, lambda e: e.activation(out=U[:], in_=U[:], func=AF.Exp), reads=[t_g], writes=[t_g])
    S.op(DVE, lambda e: e.tensor_tensor(out=CL[:], in0=Bc[:], in1=MX[:], op=ALU.add), reads=[t_g] + t_rec,
         writes=[t_g])
    S.op(ACT, lambda e: e.activation(out=CL[:], in_=CL[:], func=AF.Exp, scale=-1.0), reads=[t_g], writes=[t_g])
    S.op(ACT, lambda e: e.activation(out=WI[:], in_=WIL[:], func=AF.Exp), reads=t_rec, writes=[t_g])
    if "gates" in debug:
        for nm, tl in (("GT", GT), ("U", U), ("WI", WI), ("CL", CL), ("MX", MX), ("Bc", Bc)):
            dbg[nm] = dout("dbg_" + nm, [128, tl.shape[1] * 20])
            S.op(SP, lambda e, nm=nm, tl=tl: e.dma_start(out=dbg[nm], in_=tl[:].rearrange("p a b -> p (a b)")),
                 reads=[t_g] + t_rec, dma=True)
    if upto <= 2:
        S.emit()
        return nc

    S.barrier()
    A.reset(gate_mark)
    hmT_d = dscr("hmT_d", [8, 128, TOK], BF16)
    t_hmT_d = Tk()
    wbufs.append(A.alloc([128, KC, 256], BF16, "wbuf4"))
    t_wb.append(Tk())
    qT = A.alloc([128, 2, TOK], BF16, "qT")
    kT = A.alloc([128, 2, TOK], BF16, "kT")
    kh = A.alloc([128, NT, 256], BF16, "kh")
    vh = A.alloc([128, NT, 257], BF16, "vh")
    HS = A.alloc([128, NT, 256], F32, "HS")
    hmTh = [A.alloc([128, 2, 128], BF16, "hmTh") for _ in range(2)]
    whns = [A.alloc([128, 256], F32, "whn") for _ in range(2)]
    t_whns = [Tk(), Tk()]
    pending_f = []
    Cst = [[A.alloc([128, 2, 257], F32, "Cst") for d in range(2)] for sq in range(2)]
    Cbf = [[A.alloc([128, 2, 257], BF16, "Cbf") for d in range(2)] for sq in range(2)]
    PT = [[A.alloc([128, 128], BF16, "PT") for d in range(2)] for sq in range(2)]
    Vp = [[A.alloc([128, 257], BF16, "Vp") for d in range(2)] for sq in range(2)]
    dnr = [[A.alloc([128, 2], F32, "dnr") for d in range(2)] for sq in range(2)]
    so = [A.alloc([128, 256], F32, "so") for _ in range(2)]
    hn = [A.alloc([128, 256], F32, "hn") for _ in range(2)]
    hg = [A.alloc([128, 256], BF16, "hg") for _ in range(2)]
    hst = A.alloc([128, NT, 6], F32, "hst")
    hmv = A.alloc([128, NT, 2], F32, "hmv")
    hrs = A.alloc([128, NT, 2], F32, "hrs")
    t_qT, t_kT, t_kh, t_vh, t_whn, t_hmTh, t_hstat = [Tk() for _ in range(7)]
    t_HS = [Tk() for _ in range(NT)]
    t_Cst = [[Tk() for d in range(2)] for sq in range(2)]
    t_Cbf = [[Tk() for d in range(2)] for sq in range(2)]
    t_PT = [[Tk() for d in range(2)] for sq in range(2)]
    t_Vp = [[Tk() for d in range(2)] for sq in range(2)]
    t_dnr = [[Tk() for d in range(2)] for sq in range(2)]
    t_so = [Tk(), Tk()]
    t_hn = [Tk(), Tk()]
    t_hg = [Tk(), Tk()]
    S.op(POOL, lambda e: e.memset(vh[:, :, 256:257], 1.0), writes=[t_vh])
    t_hmTh = [Tk(), Tk()]
    evac_ctr = [0]

    def evac(dst, src_ps, t_ps, t_dst, scale=None):
        evac_ctr[0] += 1
        if evac_ctr[0] % 2 == 0:
            if scale is None:
                S.op(ACT, lambda e: e.copy(out=dst, in_=src_ps), writes=[t_ps, t_dst])
            else:
                S.op(ACT, lambda e: e.mul(out=dst, in_=src_ps, mul=scale), writes=[t_ps, t_dst])
        else:
            if scale is None:
                S.op(DVE, lambda e: e.tensor_copy(out=dst, in_=src_ps), writes=[t_ps, t_dst])
            else:
                S.op(DVE, lambda e: e.tensor_scalar_mul(out=dst, in0=src_ps, scalar1=scale), writes=[t_ps, t_dst])

    def proj_featmajor(wb, twb, ncol0, dstT, t_dst, j, scale=None, banks=(0, 8), hook=None):
        for tb in range(5):
            b = next_bank(*banks)
            for k in range(KC):
                S.op(PE, lambda e, k=k, b=b, tb=tb: e.matmul(
                    PS[b][:, :], lhsT=wb[:, k, ncol0:ncol0 + 128], rhs=hT[:, k, tb * 512:(tb + 1) * 512],
                    start=(k == 0), stop=(k == KC - 1)), reads=[twb], writes=[tPS[b]])
            evac(dstT[:, j, tb * 512:(tb + 1) * 512], PS[b][:, :], tPS[b], t_dst, scale)
            if hook is not None:
                hook()

    def proj_tokmajor(wb, twb, dst3, t_dst, banks=(0, 8), hook=None):
        for i2 in range(NT // 2):
            b = next_bank(*banks)
            for ii in range(2):
                i = i2 * 2 + ii
                for k in range(KC):
                    S.op(PE, lambda e, k=k, b=b, i=i, ii=ii: e.matmul(
                        PS[b][:, ii * 256:(ii + 1) * 256], lhsT=hT[:, k, i * 128:(i + 1) * 128], rhs=wb[:, k, 0:256],
                        start=(k == 0), stop=(k == KC - 1)), reads=[twb], writes=[tPS[b]])
            evac(dst3[:, i2 * 2:i2 * 2 + 2, 0:256], PS[b][:, :].rearrange("p (a b) -> p a b", a=2), tPS[b], t_dst)
            if hook is not None:
                hook()

    for h in range(NH):
        wq, twq = load_w(w_in[:, h * 256:(h + 1) * 256], 256)
        wk, twk = load_w(w_in[:, 1024 + h * 256:1024 + (h + 1) * 256], 256)
        wv, twv = load_w(w_in[:, 2048 + h * 256:2048 + (h + 1) * 256], 256)
        hk_ctr = [0]

        def f_hook():
            hk_ctr[0] += 1
            if pending_f and hk_ctr[0] % 2 == 0:
                pending_f.pop(0)()

        for j in range(2):
            proj_featmajor(wq, twq, j * 128, qT, t_qT, j, scale=DH ** -0.5, hook=f_hook)
        for j in range(2):
            proj_featmajor(wk, twk, j * 128, kT, t_kT, j, hook=f_hook)
        for g4 in range(NT // 4):
            b = next_bank()
            bv = PS[b][:].bitcast(BF16)
            for ii in range(4):
                i = g4 * 4 + ii
                for jj in range(2):
                    S.op(PE, lambda e, bv=bv, ii=ii, jj=jj, i=i: e.transpose(
                        bv[:, ii * 256 + jj * 128:ii * 256 + (jj + 1) * 128], kT[:, jj, i * 128:(i + 1) * 128], ident[:]),
                        reads=[t_kT, t_ident], writes=[tPS[b]])
            evac(kh[:, g4 * 4:(g4 + 1) * 4, :], bv[:, :].rearrange("p (a b) -> p a b", a=4), tPS[b], t_kh)
            f_hook()
        proj_tokmajor(wv, twv, vh, t_vh, hook=f_hook)
        while pending_f:
            pending_f.pop(0)()
        wo, two = load_w(w_in[:, 3072 + h * 256:3072 + (h + 1) * 256], 256)
        whn = whns[h % 2]
        t_whn = t_whns[h % 2]
        S.op(SP, lambda e, h=h, whn=whn: e.dma_start(out=whn[:], in_=w_hnorm[h * 256:(h + 1) * 256].partition_broadcast(128)),
             writes=[t_whn], dma=True)

        for d in range(2):
            S.op(SP, lambda e, d=d, h=h: e.dma_start(out=Cst[0][d][:, :, 0:256],
                                                in_=sC[d, h].rearrange("(j p) v -> p j v", p=128)),
                 writes=[t_Cst[0][d]], dma=True)
            S.op(SP, lambda e, d=d, h=h: e.dma_start(out=Cst[0][d][:, :, 256],
                                                in_=sn[d, h].rearrange("(j p) -> p j", p=128),
                                                allow_slow_non_contiguous=True),
                 writes=[t_Cst[0][d]], dma=True)
            S.op(POOL, lambda e, d=d: e.memset(Cst[1][d][:], 0.0), writes=[t_Cst[1][d]])

        visited = set()
        for j in range(16):
          for sq0, (st0, ln0, _c) in enumerate(SEQS):
            i0, n = st0 // 128, ln0 // 128
            off = 2 if sq0 == 2 else 0
            if not (off <= j < off + n):
                continue
            js = j - off
            sq = min(sq0, 1)
            if sq0 == 2 and js == 0:
                for d in range(2):
                    S.op(POOL, lambda e, d=d: e.memset(Cst[1][d][:], 0.0), writes=[t_Cst[1][d]])
            chains = []
            if True:
                for d in range(2):
                    i = i0 + js if d == 0 else i0 + n - 1 - js
                    chains.append(dict(
                        sq=sq, d=d, i=i, tok=slice(i * 128, (i + 1) * 128), col=d * 4 + h,
                        bS=d * 3, bN=d * 3 + 1, bC=d * 3 + 2,
                        cs=Cst[sq][d], cb=Cbf[sq][d], pt=PT[sq][d], vp=Vp[sq][d], dn=dnr[sq][d],
                        tcs=t_Cst[sq][d], tcb=t_Cbf[sq][d], tpt=t_PT[sq][d], tvp=t_Vp[sq][d], tdn=t_dnr[sq][d],
                        wi=WI[:, d * 4 + h, i:i + 1]))
            for C_ in chains:
                S.op(ACT, lambda e, C_=C_: e.mul(out=C_["cb"][:], in_=C_["cs"][:], mul=C_["wi"]),
                     reads=[C_["tcs"], t_g], writes=[C_["tcb"]])
                S.op(ACT, lambda e, C_=C_: e.mul(out=C_["vp"][:], in_=vh[:, C_["i"], :],
                                                 mul=U[:, C_["col"], C_["i"]:C_["i"] + 1]),
                     reads=[t_vh, t_g], writes=[C_["tvp"]])
            for C_ in chains:
                for jj in range(2):
                    S.op(PE, lambda e, jj=jj, C_=C_: e.matmul(
                        PS[C_["bS"]][:, 0:128], lhsT=kT[:, jj, C_["tok"]], rhs=qT[:, jj, C_["tok"]], start=(jj == 0),
                        stop=(jj == 1)), reads=[t_kT, t_qT], writes=[tPS[C_["bS"]]])
            for C_ in chains:
                S.op(DVE, lambda e, C_=C_: e.tensor_tensor(
                    out=C_["pt"][:], in0=PS[C_["bS"]][:, 0:128], in1=(maskF if C_["d"] == 0 else maskB)[:],
                    op=ALU.mult), reads=[t_c], writes=[tPS[C_["bS"]], C_["tpt"]])
            for C_ in chains:
                for jj in range(2):
                    S.op(PE, lambda e, jj=jj, C_=C_: e.matmul(
                        PS[C_["bC"]][:, jj * 256:(jj + 1) * 256], lhsT=kh[:, C_["i"], jj * 128:(jj + 1) * 128],
                        rhs=C_["vp"][:, 0:256], start=True, stop=True), reads=[t_kh, C_["tvp"]],
                        writes=[tPS[C_["bC"]]])
                for jj in range(2):
                    S.op(PE, lambda e, jj=jj, C_=C_: e.matmul(
                        PS[C_["bS"]][:, 300 + jj:301 + jj], lhsT=kh[:, C_["i"], jj * 128:(jj + 1) * 128],
                        rhs=C_["vp"][:, 256:257], start=True, stop=True), reads=[t_kh, C_["tvp"]],
                        writes=[tPS[C_["bS"]]])
            for C_ in chains:
                S.op(DVE, lambda e, C_=C_: e.scalar_tensor_tensor(
                    out=C_["cs"][:, :, 0:256], in0=C_["cs"][:, :, 0:256], scalar=C_["wi"],
                    in1=PS[C_["bC"]][:, :].rearrange("p (a b) -> p a b", a=2), op0=ALU.mult, op1=ALU.add),
                    reads=[t_g, C_["tcb"]], writes=[tPS[C_["bC"]], C_["tcs"]])
                S.op(DVE, lambda e, C_=C_: e.scalar_tensor_tensor(
                    out=C_["cs"][:, :, 256], in0=C_["cs"][:, :, 256], scalar=C_["wi"], in1=PS[C_["bS"]][:, 300:302],
                    op0=ALU.mult, op1=ALU.add), reads=[t_g, C_["tcb"]], writes=[tPS[C_["bS"]], C_["tcs"]])
            if sq0 >= 1 and js == n - 1:
                pass
            for C_ in chains:
                S.op(PE, lambda e, C_=C_: e.matmul(
                    PS[C_["bN"]][:, 0:257], lhsT=C_["pt"][:], rhs=C_["vp"][:], start=True, stop=False),
                    reads=[C_["tpt"], C_["tvp"]], writes=[tPS[C_["bN"]]])
                for jj in range(2):
                    S.op(PE, lambda e, jj=jj, C_=C_: e.matmul(
                        PS[C_["bN"]][:, 0:257], lhsT=qT[:, jj, C_["tok"]], rhs=C_["cb"][:, jj, :], start=False,
                        stop=(jj == 1)), reads=[t_qT, C_["tcb"]], writes=[tPS[C_["bN"]]])
            for C_ in chains:
                S.op(DVE, lambda e, C_=C_: e.tensor_scalar_mul(
                    out=C_["dn"][:, 0:1], in0=PS[C_["bN"]][:, 256:257], scalar1=-1.0),
                    writes=[tPS[C_["bN"]], C_["tdn"]])
            for C_ in chains:
                S.op(DVE, lambda e, C_=C_: e.scalar_tensor_tensor(
                    out=C_["dn"][:, 0:1], in0=C_["dn"][:, 0:1], scalar=CL[:, C_["col"], C_["i"]:C_["i"] + 1],
                    in1=PS[C_["bN"]][:, 256:257], op0=ALU.max, op1=ALU.max), reads=[t_g],
                    writes=[tPS[C_["bN"]], C_["tdn"]])
            for C_ in chains:
                S.op(DVE, lambda e, C_=C_: e.reciprocal(out=C_["dn"][:, 1:2], in_=C_["dn"][:, 0:1]),
                     writes=[C_["tdn"]])
            for C_ in chains:
                i = C_["i"]
                if i not in visited:
                    visited.add(i)
                    S.op(ACT, lambda e, C_=C_: e.mul(out=HS[:, C_["i"], :], in_=PS[C_["bN"]][:, 0:256],
                                                     mul=C_["dn"][:, 1:2]),
                         reads=[C_["tdn"]], writes=[tPS[C_["bN"]], t_HS[i]])
                else:
                    S.op(DVE, lambda e, C_=C_: e.scalar_tensor_tensor(
                        out=HS[:, C_["i"], :], in0=PS[C_["bN"]][:, 0:256], scalar=C_["dn"][:, 1:2],
                        in1=HS[:, C_["i"], :], op0=ALU.mult, op1=ALU.add), reads=[C_["tdn"]],
                        writes=[tPS[C_["bN"]], t_HS[i]])
            if sq0 >= 1 and js == n - 1:
                for d in range(2):
                    S.op(SP, lambda e, sq0=sq0, d=d, h=h: e.dma_start(
                        out=o_C[sq0 - 1, d, h].rearrange("(j p) v -> p j v", p=128), in_=Cst[1][d][:, :, 0:256]),
                        reads=[t_Cst[1][d]], dma=True)
                    S.op(SP, lambda e, sq0=sq0, d=d, h=h: e.dma_start(
                        out=o_n[sq0 - 1, d, h].rearrange("(j p) -> p j", p=128), in_=Cst[1][d][:, :, 256],
                        allow_slow_non_contiguous=True), reads=[t_Cst[1][d]], dma=True)
        if "hraw" in debug and h == 0:
            dbg["hraw"] = dout("dbg_hraw", [128, NT, 256])
            S.op(SP, lambda e: e.dma_start(out=dbg["hraw"], in_=HS[:]), reads=t_HS, dma=True)

        for i in range(NT):
            S.op(DVE, lambda e, i=i: e.bn_stats(out=hst[:, i, :], in_=HS[:, i, :]), reads=[t_HS[i]], writes=[t_hstat])
            S.op(DVE, lambda e, i=i: e.bn_aggr(out=hmv[:, i, :], in_=hst[:, i, :]), writes=[t_hstat])
        S.op(DVE, lambda e: e.tensor_scalar_add(out=hrs[:, :, 0], in0=hmv[:, :, 1], scalar1=EPS), writes=[t_hstat])
        S.op(ACT, lambda e: e.activation(out=hrs[:, :, 0], in_=hrs[:, :, 0], func=AF.Ln), writes=[t_hstat])
        S.op(ACT, lambda e: e.activation(out=hrs[:, :, 0], in_=hrs[:, :, 0], func=AF.Exp, scale=-0.5),
             writes=[t_hstat])
        S.op(DVE, lambda e: e.scalar_tensor_tensor(out=hrs[:, :, 1], in0=hmv[:, :, 0], scalar=-1.0, in1=hrs[:, :, 0],
                                                   op0=ALU.mult, op1=ALU.mult), writes=[t_hstat])
        def f_tile(i, h=h, wo=wo, two=two, whn=whn, t_whn=t_whn):
                b2 = i % 2
                bo = 6 + b2
                for k in range(KC):
                    S.op(PE, lambda e, k=k, i=i, bo=bo, wo=wo: e.matmul(
                        PS[bo][:, 0:256], lhsT=hT[:, k, i * 128:(i + 1) * 128], rhs=wo[:, k, 0:256],
                        start=(k == 0), stop=(k == KC - 1)), reads=[two], writes=[tPS[bo]])
                S.op(ACT, lambda e, b2=b2, bo=bo: e.activation(out=so[b2][:], in_=PS[bo][:, 0:256], func=AF.Sigmoid),
                     writes=[tPS[bo], t_so[b2]])
                S.op(ACT, lambda e, b2=b2, i=i: e.activation(out=hn[b2][:], in_=HS[:, i, :], func=AF.Identity,
                                                             bias=hrs[:, i, 1:2], scale=hrs[:, i, 0:1]),
                     reads=[t_HS[i], t_hstat], writes=[t_hn[b2]])
                S.op(DVE, lambda e, b2=b2: e.tensor_tensor(out=hn[b2][:], in0=hn[b2][:], in1=whn[:],
                                                           op=ALU.mult), reads=[t_whn], writes=[t_hn[b2]])
                S.op(DVE, lambda e, b2=b2: e.tensor_tensor(out=hg[b2][:], in0=hn[b2][:], in1=so[b2][:], op=ALU.mult),
                     reads=[t_hn[b2], t_so[b2]], writes=[t_hg[b2]])
                bt = 4 + b2
                btv = PS[bt][:].bitcast(BF16)
                for jj in range(2):
                    S.op(PE, lambda e, jj=jj, b2=b2, btv=btv: e.transpose(btv[:, jj * 128:(jj + 1) * 128],
                                                                        hg[b2][:, jj * 128:(jj + 1) * 128], ident[:]),
                         reads=[t_hg[b2], t_ident], writes=[tPS[bt]])
                evac(hmTh[b2][:], btv[:, 0:256].rearrange("p (a b) -> p a b", a=2), tPS[bt], t_hmTh[b2])
                S.op(SP, lambda e, b2=b2, i=i, h=h: e.dma_start(
                    out=hmT_d[h * 2:h * 2 + 2, :, i * 128:(i + 1) * 128].rearrange("j p t -> p j t"), in_=hmTh[b2][:]),
                    reads=[t_hmTh[b2]], mwrites=[t_hmT_d], dma=True)

        for i in range(NT):
            pending_f.append(lambda i=i, f_tile=f_tile: f_tile(i))
        if h == NH - 1:
            while pending_f:
                pending_f.pop(0)()
        if "hm" in debug and h == 0:
            dbg["hm"] = dout("dbg_hm", [2, 128, TOK], BF16)
            S.op(SP, lambda e: e.dma_start(out=dbg["hm"], in_=hmT_d[0:2]), reads=[t_hmT_d], dma=True)
        if upto == 3 and h == 0 and "onehead" in debug:
            break
    if upto <= 3:
        S.emit()
        return nc

    S.barrier()
    A.reset(ln_mark)
    wbufs.pop()
    t_wb.pop()
    frT_d = dscr("frT_d", [8, 128, TOK], BF16)
    t_frT_d = Tk()
    XT = A.alloc([128, 2, 2, TOK], BF16, "XT")
    AB = A.alloc([128, 2, NT, 512], BF16, "AB")
    dftc = A.alloc([128, 2, 512], BF16, "dftc")
    ct_s = A.alloc([128, 2, 256], BF16, "ct_s")
    st_s = A.alloc([128, 2, 256], BF16, "st_s")
    ctb = [A.alloc([128, 16, 256], BF16, "ctb") for _ in range(2)]
    stb = [A.alloc([128, 16, 256], BF16, "stb") for _ in range(2)]
    frs = [A.alloc([128, 256], BF16, "frs") for _ in range(2)]
    t_XT, t_AB, t_dc = Tk(), Tk(), Tk()
    t_ctb = [Tk(), Tk()]
    t_frs = [Tk(), Tk()]
    S.op(SP, lambda e: e.dma_start(out=dftc[:], in_=dftc_d.rearrange("(j p) n -> p j n", p=128)), writes=[t_dc], dma=True)
    S.op(SP, lambda e: e.dma_start(out=ct_s[:], in_=ct256.rearrange("(j p) n -> p j n", p=128)), writes=[t_dc], dma=True)
    S.op(SP, lambda e: e.dma_start(out=st_s[:], in_=st256.rearrange("(j p) n -> p j n", p=128)), writes=[t_dc], dma=True)
    frc = [0]
    for gp in range(2):
        for gg in range(2):
            g = gp * 2 + gg
            wf, twf = load_w(w_in[:, 4112 + g * 256:4112 + (g + 1) * 256], 256)
            for j in range(2):
                proj_featmajor(wf, twf, j * 128, XT[:, gg], t_XT, j)
        for gg in range(2):
            for i in range(NT):
                b = next_bank()
                for j in range(2):
                    S.op(PE, lambda e, gg=gg, i=i, j=j, b=b: e.matmul(
                        PS[b][:, :], lhsT=XT[:, gg, j, i * 128:(i + 1) * 128], rhs=dftc[:, j, :], start=(j == 0),
                        stop=(j == 1)), reads=[t_XT, t_dc], writes=[tPS[b]])
                evac(AB[:, gg, i, :], PS[b][:, :], tPS[b], t_AB)
        for tb8 in range(8):
            cb2 = tb8 % 2
            S.op(SP, lambda e, tb8=tb8, cb2=cb2: e.dma_start(
                out=ctb[cb2][:], in_=ct2048[:, tb8 * 256:(tb8 + 1) * 256].rearrange("(i p) n -> p i n", p=128)),
                writes=[t_ctb[cb2]], dma=True)
            S.op(SP, lambda e, tb8=tb8, cb2=cb2: e.dma_start(
                out=stb[cb2][:], in_=st2048[:, tb8 * 256:(tb8 + 1) * 256].rearrange("(i p) n -> p i n", p=128)),
                writes=[t_ctb[cb2]], dma=True)
            for gg in range(2):
                for j in range(2):
                    b = next_bank()
                    for i in range(16):
                        S.op(PE, lambda e, gg=gg, j=j, i=i, b=b, cb2=cb2: e.matmul(
                            PS[b][:, 0:256], lhsT=AB[:, gg, i, j * 128:(j + 1) * 128], rhs=ctb[cb2][:, i, :],
                            start=(i == 0), stop=False), reads=[t_AB, t_ctb[cb2]], writes=[tPS[b]])
                        S.op(PE, lambda e, gg=gg, j=j, i=i, b=b, cb2=cb2: e.matmul(
                            PS[b][:, 0:256], lhsT=AB[:, gg, i, 256 + j * 128:256 + (j + 1) * 128], rhs=stb[cb2][:, i, :],
                            start=False, stop=(i == 15)), reads=[t_AB, t_ctb[cb2]], writes=[tPS[b]])
                    fb = frc[0] % 2
                    frc[0] += 1
                    evac(frs[fb][:], PS[b][:, 0:256], tPS[b], t_frs[fb])
                    S.op(POOL, lambda e, gp=gp, gg=gg, j=j, tb8=tb8, fb=fb: e.dma_start(
                        out=frT_d[(gp * 2 + gg) * 2 + j, :, tb8 * 256:(tb8 + 1) * 256], in_=frs[fb][:]),
                        reads=[t_frs[fb]], mwrites=[t_frT_d], dma=True)
        for p in range(2):
            for gg in range(2):
                for j in range(2):
                    b = next_bank()
                    for ii in range(2):
                        i = 16 + p * 2 + ii
                        S.op(PE, lambda e, gg=gg, j=j, i=i, ii=ii, b=b: e.matmul(
                            PS[b][:, 0:256], lhsT=AB[:, gg, i, j * 128:(j + 1) * 128], rhs=ct_s[:, ii, :],
                            start=(ii == 0), stop=False), reads=[t_AB, t_dc], writes=[tPS[b]])
                        S.op(PE, lambda e, gg=gg, j=j, i=i, ii=ii, b=b: e.matmul(
                            PS[b][:, 0:256], lhsT=AB[:, gg, i, 256 + j * 128:256 + (j + 1) * 128], rhs=st_s[:, ii, :],
                            start=False, stop=(ii == 1)), reads=[t_AB, t_dc], writes=[tPS[b]])
                    fb = frc[0] % 2
                    frc[0] += 1
                    evac(frs[fb][:], PS[b][:, 0:256], tPS[b], t_frs[fb])
                    S.op(POOL, lambda e, gp=gp, gg=gg, j=j, p=p, fb=fb: e.dma_start(
                        out=frT_d[(gp * 2 + gg) * 2 + j, :, 2048 + p * 256:2048 + (p + 1) * 256], in_=frs[fb][:]),
                        reads=[t_frs[fb]], mwrites=[t_frT_d], dma=True)
    if "fr" in debug:
        dbg["fr"] = dout("dbg_fr", [8, 128, TOK], BF16)
        S.op(SP, lambda e: e.dma_start(out=dbg["fr"], in_=frT_d), reads=[t_frT_d], dma=True)
    if upto <= 4:
        S.emit()
        return nc

    S.barrier()
    A.reset(ln_mark)
    sg_d = dscr("sg_d", [2, 16, 128, TOK], BF16)
    t_sg_d = Tk()
    hmT = A.alloc([128, 8, TOK], BF16, "hmT")
    frT = A.alloc([128, 8, TOK], BF16, "frT")
    t_hf = Tk()
    for k8 in range(8):
        S.op(SP, lambda e, k8=k8: e.dma_start(out=hmT[:, k8, :], in_=hmT_d[k8]), reads=[t_hmT_d], writes=[t_hf], dma=True)
        S.op(SP, lambda e, k8=k8: e.dma_start(out=frT[:, k8, :], in_=frT_d[k8]), reads=[t_frT_d], writes=[t_hf], dma=True)
    mark_hf = A.mark()
    sgs = [A.alloc([128, 512], BF16, "sgs") for _ in range(4)]
    t_sgs = [Tk() for _ in range(4)]
    sgc = [0]
    for cbk in range(16):
        for which in range(2):
            c0 = (5136 if which == 0 else 7184) + cbk * 128
            wgx, twgx = load_w(w_in[:, c0:c0 + 128], 128)
            for tb in range(5):
                b = next_bank()
                for k in range(KC):
                    S.op(PE, lambda e, k=k, b=b, tb=tb, wgx=wgx: e.matmul(
                        PS[b][:, :], lhsT=wgx[:, k, 0:128], rhs=hT[:, k, tb * 512:(tb + 1) * 512],
                        start=(k == 0), stop=(k == KC - 1)), reads=[twgx], writes=[tPS[b]])
                sb_ = sgc[0] % 4
                sgc[0] += 1
                S.op(ACT, lambda e, b=b, sb_=sb_: e.activation(out=sgs[sb_][:], in_=PS[b][:, :], func=AF.Sigmoid),
                     writes=[tPS[b], t_sgs[sb_]])
                S.op(SP, lambda e, which=which, cbk=cbk, tb=tb, sb_=sb_: e.dma_start(
                    out=sg_d[which, cbk, :, tb * 512:(tb + 1) * 512], in_=sgs[sb_][:]),
                    reads=[t_sgs[sb_]], mwrites=[t_sg_d], dma=True)
            if ada_pending:
                ada_block(ada_pending.pop(0))
    assert not ada_pending

    S.barrier()
    mixT_d = dscr("mixT_d", [16, 128, TOK], BF16)
    t_mixT_d = Tk()
    A.reset(pers_mark)
    wout = A.alloc([128, KC, D], BF16, "wout")
    g1bc = A.alloc([128, 2, D], F32, "g1bc")
    l1g = A.alloc([128, D], F32, "l1g")
    l1b = A.alloc([128, D], F32, "l1b")
    mark_6 = A.mark()
    wbr = [[A.alloc([128, 8, 128], BF16, "wbr") for _ in range(2)] for _ in range(2)]
    assert A.mark() <= ln_mark, (A.mark(), ln_mark)
    A.reset(mark_hf)
    NSG = 3
    sga = [A.alloc([128, 512], BF16, "sga") for _ in range(NSG)]
    sgb = [A.alloc([128, 512], BF16, "sgb") for _ in range(NSG)]
    tm1 = [A.alloc([128, 512], F32, "tm1") for _ in range(2)]
    tm2 = [A.alloc([128, 512], F32, "tm2") for _ in range(2)]
    mxs = [A.alloc([128, 512], BF16, "mxs") for _ in range(2)]
    t_sga = [Tk() for _ in range(NSG)]
    t_tm1 = [Tk(), Tk()]
    t_tm2 = [Tk(), Tk()]
    t_mxs = [Tk(), Tk()]
    steps5 = [(cbk, tb) for cbk in range(16) for tb in range(5)]

    def load_sg(n):
        cbk, tb = steps5[n]
        r3 = n % NSG
        S.op(SP, lambda e: e.dma_start(out=sga[r3][:], in_=sg_d[0, cbk, :, tb * 512:(tb + 1) * 512]),
             reads=[t_sg_d], writes=[t_sga[r3]], dma=True)
        S.op(SP, lambda e: e.dma_start(out=sgb[r3][:], in_=sg_d[1, cbk, :, tb * 512:(tb + 1) * 512]),
             reads=[t_sg_d], writes=[t_sga[r3]], dma=True)

    t_wbr = [[Tk(), Tk()], [Tk(), Tk()]]

    def load_br(cbk):
        par = cbk % 2
        for which, wsrc in enumerate((w_br_m, w_br_f)):
            buf = wbr[par][which]
            v = wsrc[:, cbk * 128:(cbk + 1) * 128].rearrange("(k p) n -> p k n", p=128)
            S.op(POOL, lambda e, buf=buf, v=v: e.dma_start(out=buf[:], in_=v), writes=[t_wbr[par][which]], dma=True)

    load_sg(0)
    load_br(0)
    load_br(1)
    t_wout, t_bc = Tk(), Tk()
    for c8 in range(8):
        S.op(POOL, lambda e, c8=c8: e.dma_start(
            out=wout[:, :, c8 * 256:(c8 + 1) * 256],
            in_=w_out[:, c8 * 256:(c8 + 1) * 256].rearrange("(k p) n -> p k n", p=128)), writes=[t_wout], dma=True)
    for c in range(2):
        S.op(SP, lambda e, c=c: e.dma_start(out=g1bc[:, c, :], in_=g_scr[0, c].partition_broadcast(128)),
             reads=[t_gscr], writes=[t_bc], dma=True)
    S.op(SP, lambda e: e.dma_start(out=l1g[:], in_=ln1_g.partition_broadcast(128)), writes=[t_bc], dma=True)
    S.op(SP, lambda e: e.dma_start(out=l1b[:], in_=ln1_b.partition_broadcast(128)), writes=[t_bc], dma=True)
    wm = wf2 = twm = twf2 = None
    for n, (cbk, tb) in enumerate(steps5):
        if n + 1 < len(steps5):
            load_sg(n + 1)
        if tb == 0:
            wm, twm = wbr[cbk % 2][0], t_wbr[cbk % 2][0]
            wf2, twf2 = wbr[cbk % 2][1], t_wbr[cbk % 2][1]
            if 1 <= cbk and cbk + 1 < 16:
                load_br(cbk + 1)
        r2 = n % 2
        r3 = n % NSG
        bm, bf = next_bank(), next_bank()
        for k in range(8):
            S.op(PE, lambda e, k=k, bm=bm, tb=tb, wm=wm: e.matmul(
                PS[bm][:, :], lhsT=wm[:, k, 0:128], rhs=hmT[:, k, tb * 512:(tb + 1) * 512],
                start=(k == 0), stop=(k == 7)), reads=[twm, t_hf], writes=[tPS[bm]])
        for k in range(8):
            S.op(PE, lambda e, k=k, bf=bf, tb=tb, wf2=wf2: e.matmul(
                PS[bf][:, :], lhsT=wf2[:, k, 0:128], rhs=frT[:, k, tb * 512:(tb + 1) * 512],
                start=(k == 0), stop=(k == 7)), reads=[twf2, t_hf], writes=[tPS[bf]])
        S.op(DVE, lambda e, bm=bm, r2=r2, r3=r3: e.tensor_tensor(out=tm1[r2][:], in0=PS[bm][:, :], in1=sga[r3][:],
                                                                 op=ALU.mult), reads=[t_sga[r3]], writes=[tPS[bm], t_tm1[r2]])
        S.op(DVE, lambda e, bf=bf, r2=r2, r3=r3: e.tensor_tensor(out=tm2[r2][:], in0=PS[bf][:, :], in1=sgb[r3][:],
                                                                 op=ALU.mult), reads=[t_sga[r3]], writes=[tPS[bf], t_tm2[r2]])
        S.op(DVE, lambda e, r2=r2: e.tensor_tensor(out=mxs[r2][:], in0=tm1[r2][:], in1=tm2[r2][:], op=ALU.add),
             reads=[t_tm1[r2], t_tm2[r2]], writes=[t_mxs[r2]])
        S.op(ACT, lambda e, cbk=cbk, tb=tb, r2=r2: e.dma_start(
            out=mixT_d[cbk, :, tb * 512:(tb + 1) * 512], in_=mxs[r2][:]), reads=[t_mxs[r2]], mwrites=[t_mixT_d], dma=True)
    if "mixed" in debug:
        dbg["hmall"] = dout("dbg_hmall", [8, 128, TOK], BF16)
        S.op(SP, lambda e: e.dma_start(out=dbg["hmall"], in_=hmT_d), reads=[t_hmT_d], dma=True)
        dbg["sg"] = dout("dbg_sg", [2, 16, 128, TOK], BF16)
        for w_ in range(2):
            S.op(SP, lambda e, w_=w_: e.dma_start(out=dbg["sg"][w_], in_=sg_d[w_]), reads=[t_sg_d], dma=True)
        dbg["mixed"] = dout("dbg_mixed", [16, 128, TOK], BF16)
        S.op(SP, lambda e: e.dma_start(out=dbg["mixed"], in_=mixT_d), reads=[t_mixT_d], dma=True)
    if upto <= 5:
        S.emit()
        return nc

    S.barrier()
    A.reset(mark_6)
    x1_d = dscr("x1_d", [TOK, D])
    h2T_d = dscr("h2T_d", [KC, 128, TOK], BF16)
    t_x1_d, t_h2T_d = Tk(), Tk()
    L6 = LNB()
    rrs = [A.alloc([128, D], F32, "rr") for _ in range(2)]
    x1ts = [A.alloc([128, D], F32, "x1t") for _ in range(3)]
    mxt = [A.alloc([128, KC, 128], BF16, "mxt") for _ in range(3)]
    h2s = [A.alloc([128, KC, 128], BF16, "h2s") for _ in range(2)]
    t_rrs = [Tk(), Tk()]
    t_x1ts = [Tk(), Tk(), Tk()]
    t_mxt = [Tk(), Tk(), Tk()]
    t_h2s = [Tk(), Tk()]

    def load_mx(i, b):
        S.op(SP, lambda e: e.dma_start(out=mxt[b][:], in_=mixT_d[:, :, i * 128:(i + 1) * 128].rearrange("k p t -> p k t")),
             reads=[t_mixT_d], writes=[t_mxt[b]], dma=True)

    def y1_mm(i):
        b = i % 3
        for cb4 in range(4):
            for k in range(KC):
                S.op(PE, lambda e, k=k, cb4=cb4: e.matmul(
                    PS[cb4][:, :], lhsT=mxt[b][:, k, :], rhs=wout[:, k, cb4 * 512:(cb4 + 1) * 512],
                    start=(k == 0), stop=(k == KC - 1)), reads=[t_mxt[b], t_wout], writes=[tPS[cb4]])

    L6b = LNB(with_x=False, with_xn=False)
    nmt = [A.alloc([128, 4], F32, "nmt") for _ in range(2)]
    t_nmt = [Tk(), Tk()]

    def act_norm(L, nm, t_nm, src, t_src, dst, t_dst, b):
        mvb, tmv = L.mv[b], L.t_mv[b]
        S.op(DVE, lambda e: e.tensor_scalar_mul(out=nm[:, 0:1], in0=mvb[:, 0:1], scalar1=-1.0), reads=[tmv], writes=[t_nm])
        S.op(ACT, lambda e: e.mul(out=nm[:, 1:2], in_=nm[:, 0:1], mul=mvb[:, 2:3]), reads=[tmv], writes=[t_nm])
        S.op(ACT, lambda e: e.activation(out=dst[:], in_=src[:], func=AF.Identity, bias=nm[:, 1:2], scale=mvb[:, 2:3]),
             reads=[t_src, tmv, t_nm], writes=[t_dst])

    def a1_dve(i):
        b = i % 2
        c = 0 if i < 16 else 1
        rr, t_rr = rrs[b], t_rrs[b]
        for cb4 in range(4):
            sl = slice(cb4 * 512, (cb4 + 1) * 512)
            S.op(DVE, lambda e, cb4=cb4, sl=sl: e.tensor_tensor(out=rr[:, sl], in0=PS[cb4][:, :], in1=g1bc[:, c, sl],
                                                               op=ALU.mult), reads=[t_bc], writes=[tPS[cb4], t_rr])
        S.op(DVE, lambda e: e.scalar_tensor_tensor(out=rr[:], in0=L6.xt[b][:], scalar=ALPHA, in1=rr[:],
                                                   op0=ALU.mult, op1=ALU.add), reads=[L6.t_xt[b]], writes=[t_rr])
        ln_stats_dve(L6, rr, b, t_rr, nmt[0], t_nmt[0])

    def a1_act(i):
        b = i % 2
        ln_act_norm(L6, nmt[0], t_nmt[0], rrs[b], t_rrs[b], x1ts[i % 3], t_x1ts[i % 3], b)

    def a2_pool(i):
        x1t, t_x1t = x1ts[i % 3], t_x1ts[i % 3]
        S.op(DVE, lambda e: e.tensor_tensor(out=x1t[:], in0=x1t[:], in1=l1g[:], op=ALU.mult), reads=[t_bc],
             writes=[t_x1t])

    def a2_dve(i):
        x1t, t_x1t = x1ts[i % 3], t_x1ts[i % 3]
        S.op(DVE, lambda e: e.tensor_tensor(out=x1t[:], in0=x1t[:], in1=l1b[:], op=ALU.add), reads=[t_bc],
             writes=[t_x1t])
        S.op(POOL, lambda e: e.dma_start(out=x1_d[i * 128:(i + 1) * 128, :], in_=x1t[:]), reads=[t_x1t],
             mwrites=[t_x1_d], dma=True)

    def b_dve(i):
        ln_stats_dve(L6b, x1ts[i % 3], i % 2, t_x1ts[i % 3], nmt[1], t_nmt[1])

    def b_act(i):
        b = i % 2
        ln_act_norm(L6b, nmt[1], t_nmt[1], x1ts[i % 3], t_x1ts[i % 3], L6.xn[b], L6.t_xn[b], b)

    def ln_stats_dve(L, src, b, t_src, nm, t_nm):
        st, mvb, tmv = L.stats[b], L.mv[b], L.t_mv[b]
        for c4 in range(4):
            S.op(DVE, lambda e, c4=c4: e.bn_stats(out=st[:, c4, :], in_=src[:, c4 * 512:(c4 + 1) * 512]),
                 reads=[t_src], writes=[tmv])
        S.op(DVE, lambda e: e.bn_aggr(out=mvb[:, 0:2], in_=st[:]), writes=[tmv])
        S.op(DVE, lambda e: e.tensor_scalar_add(out=mvb[:, 2:3], in0=mvb[:, 1:2], scalar1=EPS), writes=[tmv])
        S.op(DVE, lambda e: e.tensor_scalar_mul(out=nm[:, 0:1], in0=mvb[:, 0:1], scalar1=-1.0), reads=[tmv], writes=[t_nm])

    def ln_act_norm(L, nm, t_nm, src, t_src, dst, t_dst, b):
        mvb, tmv = L.mv[b], L.t_mv[b]
        S.op(ACT, lambda e: e.activation(out=mvb[:, 2:3], in_=mvb[:, 2:3], func=AF.Ln), writes=[tmv])
        S.op(ACT, lambda e: e.activation(out=mvb[:, 2:3], in_=mvb[:, 2:3], func=AF.Exp, scale=-0.5), writes=[tmv])
        S.op(ACT, lambda e: e.mul(out=nm[:, 1:2], in_=nm[:, 0:1], mul=mvb[:, 2:3]), reads=[tmv], writes=[t_nm])
        S.op(ACT, lambda e: e.activation(out=dst[:], in_=src[:], func=AF.Identity, bias=nm[:, 1:2], scale=mvb[:, 2:3]),
             reads=[t_src, tmv, t_nm], writes=[t_dst])

    def c_pe(i):
        norm_part2(L6, i % 2, 0, h2s[i % 2], t_h2s[i % 2], 3, 4, 0 if i < 16 else 1, act_evac=2, only="pe")

    def c_act(i):
        norm_part2(L6, i % 2, 0, h2s[i % 2], t_h2s[i % 2], 3, 4, 0 if i < 16 else 1, act_evac=2, only="evac")

    def c_store(i):
        b = i % 2
        S.op(POOL, lambda e: e.dma_start(out=h2T_d[:, :, i * 128:(i + 1) * 128].rearrange("k p t -> p k t"),
                                         in_=h2s[b][:]), reads=[t_h2s[b]], mwrites=[t_h2T_d], dma=True)

    ok = lambda i: 0 <= i < NT
    load_x(L6, 0, 0)
    load_x(L6, 1, 1)
    for i0 in range(3):
        load_mx(i0, i0)
    y1_mm(0)
    for t in range(-2, NT + 1):
        if ok(t - 1):
            c_pe(t - 1)
        if ok(t + 2):
            a1_dve(t + 2)
        if ok(t - 1):
            c_act(t - 1)
        if ok(t + 1):
            a2_pool(t + 1)
        if ok(t + 2):
            a1_act(t + 2)
        if ok(t):
            b_dve(t)
        if ok(t + 1):
            a2_dve(t + 1)
        if ok(t):
            b_act(t)
        if ok(t + 3):
            if t + 3 >= 2:
                load_x(L6, t + 3, (t + 3) % 2)
            if t + 3 >= 3:
                load_mx(t + 3, (t + 3) % 3)
            y1_mm(t + 3)
        if ok(t - 1):
            c_store(t - 1)
    if "x1" in debug:
        dbg["x1"] = dout("dbg_x1", [TOK, D])
        S.op(SP, lambda e: e.dma_start(out=dbg["x1"], in_=x1_d), reads=[t_x1_d], dma=True)
        dbg["h2"] = dout("dbg_h2", [KC, 128, TOK], BF16)
        S.op(SP, lambda e: e.dma_start(out=dbg["h2"], in_=h2T_d), reads=[t_h2T_d], dma=True)
    if upto <= 6:
        S.emit()
        return nc

    S.barrier()
    A.reset(work_mark)
    z_d = dscr("z_d", [NJ, 128, TOK], BF16)
    t_z_d = Tk()
    h2T = A.alloc([128, KC, TOK], BF16, "h2T")
    cvR = A.alloc([88, 4, 128], F32, "cvR")
    cvT = A.alloc([128, 4, 88], F32, "cvT")
    ub = [A.alloc([128, 2, TOK], F32, "ub") for _ in range(2)]
    cbuf = A.alloc([128, 2, TOK], F32, "cbuf")
    sgt = A.alloc([128, TOK], F32, "sgt")
    zs = [A.alloc([128, TOK], BF16, "zs") for _ in range(2)]
    t_h2T, t_cv, t_cb, t_sgt = Tk(), Tk(), Tk(), Tk()
    t_ub = [Tk(), Tk()]
    t_zs = [Tk(), Tk()]
    t_h2Tb = [Tk() for _ in range(5)]
    for tb in range(5):
        S.op(SP, lambda e, tb=tb: e.dma_start(
            out=h2T[:, :, tb * 512:(tb + 1) * 512], in_=h2T_d[:, :, tb * 512:(tb + 1) * 512].rearrange("k p t -> p k t")),
            reads=[t_h2T_d], writes=[t_h2Tb[tb]], dma=True)
    for tap in range(3):
        S.op(SP, lambda e, tap=tap: e.dma_start(out=cvR[:, tap, :], in_=w_conv[tap].rearrange("(j p) -> j p", p=128)),
             writes=[t_cv], dma=True)
    S.op(SP, lambda e: e.dma_start(out=cvR[:, 3, :], in_=b_conv.rearrange("(j p) -> j p", p=128)), writes=[t_cv], dma=True)
    for tap in range(4):
        S.op(PE, lambda e, tap=tap: e.transpose(PS[7][:, tap * 88:(tap + 1) * 88], cvR[:, tap, :], identf[0:88, 0:88]),
             reads=[t_cv, t_ident], writes=[tPS[7]])
    S.op(DVE, lambda e: e.tensor_copy(out=cvT[:].rearrange("p a b -> p (a b)"), in_=PS[7][:, 0:352]),
         writes=[tPS[7], t_cv])
    for j in range(NJ):
        ubj = ub[j % 2]
        tub = t_ub[j % 2]
        for part in range(2):
            blk = part * NJ + j
            wu, twu = load_w(w_up[:, blk * 128:(blk + 1) * 128], 128)
            for tb in range(5):
                b = next_bank()
                for k in range(KC):
                    S.op(PE, lambda e, k=k, b=b, tb=tb, wu=wu: e.matmul(
                        PS[b][:, :], lhsT=wu[:, k, 0:128], rhs=h2T[:, k, tb * 512:(tb + 1) * 512],
                        start=(k == 0), stop=(k == KC - 1)), reads=[twu, t_h2Tb[tb]], writes=[tPS[b]])
                tsl = slice(tb * 512, (tb + 1) * 512)
                S.op(ACT, lambda e, b=b, part=part, tsl=tsl, blk=blk: e.activation(
                    out=cbuf[:, part, tsl], in_=PS[b][:, :], func=AF.Identity, bias=cvT[:, 3, blk:blk + 1],
                    scale=cvT[:, 1, blk:blk + 1]), reads=[t_cv], writes=[tPS[b], t_cb])
                S.op(DVE, lambda e, b=b, part=part, tsl=tsl, ubj=ubj: e.tensor_copy(out=ubj[:, part, tsl], in_=PS[b][:, :]),
                     writes=[tPS[b], tub])
            for (st0, ln0, _c) in SEQS:
                S.op(DVE, lambda e, part=part, st0=st0, ln0=ln0, blk=blk, ubj=ubj: e.scalar_tensor_tensor(
                    out=cbuf[:, part, st0 + 1:st0 + ln0], in0=ubj[:, part, st0:st0 + ln0 - 1],
                    scalar=cvT[:, 0, blk:blk + 1], in1=cbuf[:, part, st0 + 1:st0 + ln0], op0=ALU.mult, op1=ALU.add),
                    reads=[tub, t_cv], writes=[t_cb])
                S.op(DVE, lambda e, part=part, st0=st0, ln0=ln0, blk=blk, ubj=ubj: e.scalar_tensor_tensor(
                    out=cbuf[:, part, st0:st0 + ln0 - 1], in0=ubj[:, part, st0 + 1:st0 + ln0],
                    scalar=cvT[:, 2, blk:blk + 1], in1=cbuf[:, part, st0:st0 + ln0 - 1], op0=ALU.mult, op1=ALU.add),
                    reads=[tub, t_cv], writes=[t_cb])
        S.op(ACT, lambda e: e.activation(out=sgt[:], in_=cbuf[:, 1, :], func=AF.Silu), reads=[t_cb], writes=[t_sgt])
        zb = j % 2
        S.op(DVE, lambda e, zb=zb: e.tensor_tensor(out=zs[zb][:], in0=sgt[:], in1=cbuf[:, 0, :], op=ALU.mult),
             reads=[t_sgt, t_cb], writes=[t_zs[zb]])
        S.op(SP, lambda e, j=j, zb=zb: e.dma_start(out=z_d[j], in_=zs[zb][:]), reads=[t_zs[zb]], mwrites=[t_z_d], dma=True)
    if "z" in debug:
        dbg["z"] = dout("dbg_z", [NJ, 128, TOK], BF16)
        for j in range(NJ):
            S.op(SP, lambda e, j=j: e.dma_start(out=dbg["z"][j], in_=z_d[j]), reads=[t_z_d], dma=True)
    if upto <= 7:
        S.emit()
        return nc

    S.barrier()
    A.reset(pers_mark)
    r2_d = dscr("r2_d", [TOK, D])
    t_r2_d = Tk()
    wd = A.alloc([128, NJ, 1024], BF16, "wd")
    NZB = 3
    zb_ = [A.alloc([128, 11, 512], BF16, "zb") for _ in range(NZB)]
    g2bc = A.alloc([128, 2, D], F32, "g2bc")
    x1q = [A.alloc([128, 512], F32, "x1q") for _ in range(2)]
    y2s = [A.alloc([128, 512], F32, "y2s") for _ in range(2)]
    l2g = A.alloc([128, D], F32, "l2g")
    l2b = A.alloc([128, D], F32, "l2b")
    rt = [A.alloc([128, D], F32, "rt") for _ in range(4)]
    L9s = [LNB(with_x=False, with_xn=False) for _ in range(2)]
    t_wd = [Tk() for _ in range(4)]
    t_zb = [Tk() for _ in range(NZB)]
    t_g2, t_l2 = Tk(), Tk()
    t_x1q = [Tk(), Tk()]
    t_y2s = [Tk(), Tk()]
    t_rt = [Tk() for _ in range(4)]
    ec = 0
    fc = 0
    pieces = [(half, tb, jp) for half in range(2) for tb in range(5) for jp in range(4)]

    def load_zb(n):
        _h, tb, jp = pieces[n]
        zz = n % NZB
        S.op(SP, lambda e: e.dma_start(
            out=zb_[zz][:], in_=z_d[jp * 11:(jp + 1) * 11, :, tb * 512:(tb + 1) * 512].rearrange("j p t -> p j t")),
            reads=[t_z_d], writes=[t_zb[zz]], dma=True)

    def load_wd(half, jp2):
        S.op(POOL, lambda e: e.dma_start(
            out=wd[:, jp2 * 11:(jp2 + 1) * 11, :],
            in_=w_down[jp2 * 11 * 128:(jp2 + 1) * 11 * 128, half * 1024:(half + 1) * 1024].rearrange(
                "(j p) n -> p j n", p=128)), writes=[t_wd[jp2]], dma=True)

    load_zb(0)
    load_zb(1)
    for c in range(2):
        S.op(SP, lambda e, c=c: e.dma_start(out=g2bc[:, c, :], in_=g_scr[1, c].partition_broadcast(128)),
             reads=[t_gscr], writes=[t_g2], dma=True)
    S.op(SP, lambda e: e.dma_start(out=l2g[:], in_=ln2_g.partition_broadcast(128)), writes=[t_l2], dma=True)
    S.op(SP, lambda e: e.dma_start(out=l2b[:], in_=ln2_b.partition_broadcast(128)), writes=[t_l2], dma=True)
    for n, (half, tb, jp) in enumerate(pieces):
        if n == 0:
            for jp2 in range(4):
                load_wd(0, jp2)
        if n + 2 < len(pieces):
            load_zb(n + 2)
        zz = n % NZB
        for ii in range(4):
            for qq in range(2):
                bk = ii * 2 + qq
                for jl in range(11):
                    jg = jp * 11 + jl
                    S.op(PE, lambda e, zz=zz, ii=ii, qq=qq, jl=jl, jg=jg, bk=bk: e.matmul(
                        PS[bk][:, :], lhsT=zb_[zz][:, jl, ii * 128:(ii + 1) * 128],
                        rhs=wd[:, jg, qq * 512:(qq + 1) * 512], start=(jg == 0), stop=(jg == NJ - 1)),
                        reads=[t_zb[zz], t_wd[jp]], writes=[tPS[bk]])
        if half == 0 and tb == 4:
            load_wd(1, jp)
        if jp == 3:
            for ii in range(4):
                i = tb * 4 + ii
                c = 0 if i < 16 else 1
                for qq in range(2):
                    bk = ii * 2 + qq
                    e2 = ec % 2
                    ec += 1
                    csl = slice(half * 1024 + qq * 512, half * 1024 + (qq + 1) * 512)
                    S.op(SP, lambda e, i=i, csl=csl, e2=e2: e.dma_start(out=x1q[e2][:], in_=x1_d[i * 128:(i + 1) * 128, csl]),
                         reads=[t_x1_d], writes=[t_x1q[e2]], dma=True)
                    S.op(DVE, lambda e, bk=bk, c=c, csl=csl, e2=e2: e.tensor_tensor(
                        out=y2s[e2][:], in0=PS[bk][:, :], in1=g2bc[:, c, csl], op=ALU.mult), reads=[t_g2],
                        writes=[tPS[bk], t_y2s[e2]])
                    S.op(DVE, lambda e, e2=e2: e.scalar_tensor_tensor(out=y2s[e2][:], in0=x1q[e2][:], scalar=ALPHA,
                                                                      in1=y2s[e2][:], op0=ALU.mult, op1=ALU.add),
                         reads=[t_x1q[e2]], writes=[t_y2s[e2]])
                    S.op(SP, lambda e, i=i, csl=csl, e2=e2: e.dma_start(out=r2_d[i * 128:(i + 1) * 128, csl], in_=y2s[e2][:]),
                         reads=[t_y2s[e2]], mwrites=[t_r2_d], dma=True)
            if half == 1:
                tl = []
                for ii in range(4):
                    tl.append((tb * 4 + ii, fc % 4, L9s[(fc // 2) % 2], fc % 2))
                    fc += 1
                for (i, b, L9, lb) in tl:
                    S.op(ACT, lambda e, i=i, b=b: e.dma_start(out=rt[b][:], in_=r2_d[i * 128:(i + 1) * 128, :]),
                         reads=[t_r2_d], writes=[t_rt[b]], dma=True)
                for (i, b, L9, lb) in tl:
                    ln_stats(L9, rt[b], lb, t_rt[b])
                for (i, b, L9, lb) in tl:
                    S.op(DVE, lambda e, b=b, L9=L9, lb=lb: e.tensor_scalar(
                        out=rt[b][:], in0=rt[b][:], scalar1=L9.mv[lb][:, 0:1], scalar2=L9.mv[lb][:, 2:3],
                        op0=ALU.subtract, op1=ALU.mult), reads=[L9.t_mv[lb]], writes=[t_rt[b]])
                for (i, b, L9, lb) in tl:
                    S.op(POOL, lambda e, b=b: e.tensor_tensor(out=rt[b][:], in0=rt[b][:], in1=l2g[:], op=ALU.mult),
                         reads=[t_l2], writes=[t_rt[b]])
                    S.op(POOL, lambda e, b=b: e.tensor_tensor(out=rt[b][:], in0=rt[b][:], in1=l2b[:], op=ALU.add),
                         reads=[t_l2], writes=[t_rt[b]])
                    S.op(POOL, lambda e, i=i, b=b: e.dma_start(out=y_rows(i), in_=rt[b][:]), reads=[t_rt[b]], dma=True)
    S.emit()
    return nc


def _consts():
    f64 = np.float64
    t = np.arange(2048)
    r = (t // 64).astype(np.float32)[:, None]
    col = (t % 64).astype(np.float32)[:, None]
    quarter = D // 4
    freq = (1.0 / (10000.0 ** (np.arange(quarter, dtype=np.float32) / np.float32(quarter)))).astype(np.float32)
    er, ec = r * freq, col * freq
    posemb = np.concatenate([np.sin(er), np.cos(er), np.sin(ec), np.cos(ec)], -1).astype(np.float32)
    s = np.arange(128)
    maskF = (s[:, None] <= s[None, :]).astype(np.float32)
    maskB = (s[:, None] >= s[None, :]).astype(np.float32)
    c = np.arange(256, dtype=f64)
    ang = 2 * np.pi * np.outer(c, c) / 256.0
    dftc = np.concatenate([np.cos(ang), np.sin(ang)], 1).astype(ml_dtypes.bfloat16)

    def seq_tables(T):
        tt = np.arange(T, dtype=f64)
        a = 2 * np.pi * (np.outer(tt, tt) % T) / T
        nrm = 1.0 / np.sqrt(T * 256.0)
        return (np.cos(a) * nrm).astype(ml_dtypes.bfloat16), (-np.sin(a) * nrm).astype(ml_dtypes.bfloat16)

    ct2048, st2048 = seq_tables(2048)
    ct256, st256 = seq_tables(256)
    return dict(posemb=posemb, maskF=maskF, maskB=maskB, dftc=dftc, ct2048=ct2048, st2048=st2048,
                ct256=ct256, st256=st256)


def make_in_maps(inputs, n=8):
    cs = _consts()
    f = lambda a: np.ascontiguousarray(np.asarray(a, dtype=np.float32))
    shared = {k: f(inputs[k][0]) for k in ["w_ada", "b_ada", "w_in", "b_gate", "w_hnorm", "w_br_m", "w_br_f", "w_out",
                                           "ln1_g", "ln1_b", "w_up", "w_conv", "b_conv", "w_down", "ln2_g", "ln2_b"]}
    shared.update(cs)
    maps = []
    for i in range(n):
        m = dict(shared)
        m["xs"] = f(inputs["x_sample"][i])
        m["xp"] = f(inputs["x_prompt"][2 * i:2 * i + 2]).reshape(512, D)
        m["cond"] = np.stack([f(inputs["c"][i]), f(inputs["c_ctx"])], 0)
        m["sC"] = f(inputs["state_C"][i, 0])
        m["sn"] = f(inputs["state_n"][i, 0])
        m["sm"] = f(inputs["state_m"][i, 0]).reshape(8)
        maps.append(m)
    return maps


def kernel(**inputs):
    nc = build_program()
    maps = make_in_maps(inputs)
    res = run_bass_kernel_spmd(nc, maps, core_ids=list(range(8)))
    R = res.results
    y_p = np.concatenate([r["y_p"].reshape(2, 256, D) for r in R], 0)
    y_s = np.stack([r["y_s"] for r in R], 0)
    o_C = np.concatenate([r["o_C"] for r in R], 0)[:, None]
    o_n = np.concatenate([r["o_n"] for r in R], 0)[:, None]
    o_m = np.concatenate([r["o_m"].reshape(2, 2, NH) for r in R], 0)[:, None]
    return (y_p.astype(np.float32), y_s.astype(np.float32), o_C.astype(np.float32), o_n.astype(np.float32),
            o_m.astype(np.float32))
```
